# Optimizing a Trainium2 kernel written in Bass

```python
import jax, jax.numpy as jnp
from jax import lax
import numpy as np

D_MODEL = 1024
BATCH = 32
SEQ = 256
DEPTH = 1
DEC_BATCH = 2
DEC_SEQ = 2048
PAST_LEN = 512

GRID_W = 64
F_GROUPS = 4
F_GROUP_W = D_MODEL // 4
F_WIDTH = F_GROUPS * F_GROUP_W
D_INNER = 2 * D_MODEL
HEAD_DIM = 64
N_HEADS = D_INNER // HEAD_DIM
N_BC_GROUPS = 4
HEADS_PER_GROUP = N_HEADS // N_BC_GROUPS
D_STATE = 128
CONV_W = 5
CHUNK = 128
D_FF = 4 * D_MODEL
N_BRANCH = 2
N_MOD = 6
D_XBC = D_INNER + 2 * N_BC_GROUPS * D_STATE
D_IN_PROJ = F_WIDTH + D_INNER + D_XBC + 2 * N_HEADS + N_BRANCH * D_MODEL
SPLITS = (F_WIDTH, F_WIDTH + D_INNER, F_WIDTH + D_INNER + D_XBC,
          F_WIDTH + D_INNER + D_XBC + 2 * N_HEADS)
EPS = 1e-6

kernel_name = "hybrid_fnet_bissd_dit_step"


def rmsnorm(x, g):
    xf = x.astype(jnp.float32)
    y = xf * lax.rsqrt(jnp.mean(xf * xf, axis=-1, keepdims=True) + EPS)
    return (y * g.astype(jnp.float32)).astype(x.dtype)


def dwconv_centred(u, w, b):
    y = lax.conv_general_dilated(
        u, w[:, None, :].astype(u.dtype), window_strides=(1,),
        padding=[(CONV_W // 2, CONV_W // 2)],
        dimension_numbers=('NWC', 'WIO', 'NWC'), feature_group_count=u.shape[-1])
    return y + b.astype(u.dtype)


def fourier_mix(u, grid):
    b, l, _ = u.shape
    uf = u.astype(jnp.float32).reshape(b, l, F_GROUPS, F_GROUP_W)
    if grid:
        rows = l // GRID_W
        uf = uf.reshape(b, rows, GRID_W, F_GROUPS, F_GROUP_W)
        y = jnp.fft.fftn(uf, axes=(1, 2, 4), norm="ortho").real
    else:
        y = jnp.fft.fftn(uf, axes=(1, 3), norm="ortho").real
    return y.reshape(b, l, F_WIDTH).astype(u.dtype)


def ssd_scan(xh, dt, A, Bm, Cm, h0):
    b, l = xh.shape[:2]
    nc = l // CHUNK
    G, R = N_BC_GROUPS, HEADS_PER_GROUP
    x = xh.reshape(b, nc, CHUNK, G, R, HEAD_DIM)
    dtc = dt.reshape(b, nc, CHUNK, G, R)
    acum = jnp.cumsum(dtc * A.reshape(G, R), axis=2)
    xdt = x * dtc[..., None]
    Bc = Bm.reshape(b, nc, CHUNK, G, D_STATE)
    Cc = Cm.reshape(b, nc, CHUNK, G, D_STATE)
    seg = acum[:, :, :, None] - acum[:, :, None, :]
    lower = jnp.tril(jnp.ones((CHUNK, CHUNK), dtype=bool))[:, :, None, None]
    Lmat = jnp.exp(jnp.where(lower, seg, -jnp.inf))
    cb = jnp.einsum('bcign,bcjgn->bcijg', Cc, Bc)
    y_diag = jnp.einsum('bcijg,bcijgr,bcjgrp->bcigrp', cb, Lmat, xdt)
    decay = jnp.exp(acum[:, :, -1:] - acum)
    states = jnp.einsum('bcjgn,bcjgr,bcjgrp->bcgrpn', Bc, decay, xdt)
    chunk_decay = jnp.exp(acum[:, :, -1])

    def step(h, inp):
        s, dcy = inp
        return h * dcy[..., None, None] + s, h

    h0g = h0.reshape(b, G, R, HEAD_DIM, D_STATE)
    h_fin, h_enter = lax.scan(step, h0g, (jnp.moveaxis(states, 1, 0), jnp.moveaxis(chunk_decay, 1, 0)))
    h_enter = jnp.moveaxis(h_enter, 0, 1)
    y_off = jnp.einsum('bcign,bcgrpn,bcigr->bcigrp', Cc, h_enter, jnp.exp(acum))
    y = (y_diag + y_off).reshape(b, l, N_HEADS, HEAD_DIM)
    return y, h_fin.reshape(b, N_HEADS, HEAD_DIM, D_STATE)


def ssd_mixer(z, xbc, dt_raw, h_f0, h_b0, conv_w, conv_b, dt_bias, A_log, D_skip, norm_g):
    f32 = jnp.float32
    b, l, _ = xbc.shape
    xbc = jax.nn.silu(dwconv_centred(xbc, conv_w, conv_b))
    xs, Bm, Cm = jnp.split(xbc, [D_INNER, D_INNER + N_BC_GROUPS * D_STATE], axis=-1)
    xh = xs.astype(f32).reshape(b, l, N_HEADS, HEAD_DIM)
    Bm = Bm.astype(f32).reshape(b, l, N_BC_GROUPS, D_STATE)
    Cm = Cm.astype(f32).reshape(b, l, N_BC_GROUPS, D_STATE)
    dt = jax.nn.softplus(dt_raw.astype(f32).reshape(b, l, 2, N_HEADS) + dt_bias.astype(f32))
    A = -jnp.exp(A_log.astype(f32))
    y_f, h_f = ssd_scan(xh, dt[:, :, 0], A[0], Bm, Cm, h_f0.astype(f32))
    flip = lambda t: jnp.flip(t, axis=1)
    y_b, h_b = ssd_scan(flip(xh), flip(dt[:, :, 1]), A[1], flip(Bm), flip(Cm), h_b0.astype(f32))
    y = y_f + flip(y_b) + xh * D_skip.astype(f32)[:, None]
    y = y.reshape(b, l, D_INNER) * jax.nn.silu(z.astype(f32))
    return rmsnorm(y, norm_g).astype(z.dtype), h_f, h_b


def layer(x, mod, h_f0, h_b0, grid, norm1_g, w_in, w_fourier, conv_w, conv_b, dt_bias, A_log,
          D_skip, ssd_norm_g, w_ssd_out, w_out, norm2_g, w_ff1, w_ff2):
    shift1, scale1, gate1, shift2, scale2, gate2 = [mod[:, k][:, None] for k in range(N_MOD)]
    h = rmsnorm(x, norm1_g) * (1 + scale1) + shift1
    proj = h @ w_in
    u_f, z, xbc, dt_raw, gates = jnp.split(proj, list(SPLITS), axis=-1)
    y_f = fourier_mix(u_f, grid) @ w_fourier
    y_s, h_f, h_b = ssd_mixer(z, xbc, dt_raw, h_f0, h_b0, conv_w, conv_b, dt_bias, A_log,
                              D_skip, ssd_norm_g)
    y_s = y_s @ w_ssd_out
    g_f, g_s = jnp.split(jax.nn.sigmoid(gates), N_BRANCH, axis=-1)
    x = x + gate1 * ((g_f * y_f + g_s * y_s) @ w_out)
    h2 = rmsnorm(x, norm2_g) * (1 + scale2) + shift2
    x = x + gate2 * (jnp.square(jax.nn.relu(h2 @ w_ff1)) @ w_ff2)
    return x, h_f, h_b


def setup_inputs(seed: int = 0) -> dict:
    key = jax.random.key(seed)
    ks = jax.random.split(key, 32)
    f32 = jnp.float32
    nrm = lambda k, shape, s: jax.random.normal(k, shape, f32) * s
    dt0 = jnp.exp(jax.random.uniform(ks[10], (DEPTH, 2, N_HEADS), f32, np.log(1e-3), np.log(1e-1)))
    return {
        "x_prompt": nrm(ks[0], (BATCH, SEQ, D_MODEL), 1.0),
        "x_sample": nrm(ks[1], (DEC_BATCH, DEC_SEQ, D_MODEL), 1.0),
        "state_ssm_fwd": nrm(ks[2], (DEC_BATCH, DEPTH, N_HEADS, HEAD_DIM, D_STATE), 0.1),
        "state_ssm_bwd": nrm(ks[3], (DEC_BATCH, DEPTH, N_HEADS, HEAD_DIM, D_STATE), 0.1),
        "c": nrm(ks[4], (DEC_BATCH, D_MODEL), 1.0),
        "c_ctx": nrm(ks[5], (D_MODEL,), 1.0),
        "w_mod": nrm(ks[6], (DEPTH, D_MODEL, N_MOD * D_MODEL), D_MODEL ** -0.5),
        "b_mod": nrm(ks[7], (DEPTH, N_MOD * D_MODEL), 0.02),
        "norm1_g": 1.0 + nrm(ks[8], (DEPTH, D_MODEL), 0.02),
        "w_in": nrm(ks[9], (DEPTH, D_MODEL, D_IN_PROJ), D_MODEL ** -0.5),
        "w_fourier": nrm(ks[11], (DEPTH, F_WIDTH, D_MODEL), F_WIDTH ** -0.5),
        "conv_w": nrm(ks[12], (DEPTH, CONV_W, D_XBC), CONV_W ** -0.5),
        "conv_b": nrm(ks[13], (DEPTH, D_XBC), 0.02),
        "dt_bias": dt0 + jnp.log(-jnp.expm1(-dt0)),
        "A_log": jnp.log(jax.random.uniform(ks[14], (DEPTH, 2, N_HEADS), f32, 1.0, 16.0)),
        "D_skip": 1.0 + nrm(ks[15], (DEPTH, N_HEADS), 0.1),
        "ssd_norm_g": 1.0 + nrm(ks[16], (DEPTH, D_INNER), 0.02),
        "w_ssd_out": nrm(ks[17], (DEPTH, D_INNER, D_MODEL), D_INNER ** -0.5),
        "w_out": nrm(ks[18], (DEPTH, D_MODEL, D_MODEL), D_MODEL ** -0.5),
        "norm2_g": 1.0 + nrm(ks[19], (DEPTH, D_MODEL), 0.02),
        "w_ff1": nrm(ks[20], (DEPTH, D_MODEL, D_FF), D_MODEL ** -0.5),
        "w_ff2": nrm(ks[21], (DEPTH, D_FF, D_MODEL), D_FF ** -0.5),
        "final_norm_g": 1.0 + nrm(ks[22], (D_MODEL,), 0.02),
    }


def reference(x_prompt, x_sample, state_ssm_fwd, state_ssm_bwd, c, c_ctx, w_mod, b_mod, norm1_g,
              w_in, w_fourier, conv_w, conv_b, dt_bias, A_log, D_skip, ssd_norm_g, w_ssd_out,
              w_out, norm2_g, w_ff1, w_ff2, final_norm_g):
    bp = x_prompt.shape[0]
    xp, xs = x_prompt, x_sample
    zeros = jnp.zeros((bp, N_HEADS, HEAD_DIM, D_STATE), jnp.float32)
    new_f, new_b = [], []
    for i in range(DEPTH):
        mod_ctx = (jax.nn.silu(c_ctx)[None] @ w_mod[i] + b_mod[i]).reshape(1, N_MOD, D_MODEL)
        mod_lat = (jax.nn.silu(c) @ w_mod[i] + b_mod[i]).reshape(c.shape[0], N_MOD, D_MODEL)
        lw = (norm1_g[i], w_in[i], w_fourier[i], conv_w[i], conv_b[i], dt_bias[i], A_log[i],
              D_skip[i], ssd_norm_g[i], w_ssd_out[i], w_out[i], norm2_g[i], w_ff1[i], w_ff2[i])
        xp, hf, hb = layer(xp, mod_ctx, zeros, zeros, False, *lw)
        new_f.append(hf.astype(x_prompt.dtype))
        new_b.append(hb.astype(x_prompt.dtype))
        xs, _, _ = layer(xs, mod_lat, state_ssm_fwd[:, i], state_ssm_bwd[:, i], True, *lw)
    y_prompt = rmsnorm(xp, final_norm_g)
    y_sample = rmsnorm(xs, final_norm_g)
    new_state_fwd = jnp.stack(new_f, axis=1)
    new_state_bwd = jnp.stack(new_b, axis=1)
    return (y_prompt, y_sample, new_state_fwd, new_state_bwd)
```

```python
import contextlib
import numpy as np
import concourse.bass as bass
import concourse.mybir as mybir
from concourse.bass_utils import run_bass_kernel_spmd

F32 = mybir.dt.float32
BF16 = mybir.dt.bfloat16
F32R = mybir.dt.float32r
AF = mybir.ActivationFunctionType
ALU = mybir.AluOpType

ENGS = ["pe", "act", "dve", "pool", "sp"]
EPS = 1e-6
BUILD_SAMPLE = True


class Prog:
    def __init__(self, nc, n_dma_sems=14):
        self.nc = nc
        self.ops = []
        self.n_dma_sems = n_dma_sems

    def op(self, eng, fn, r=(), w=(), dma=False):
        rr = set()
        ww = set()
        for k in r:
            rr.add(k)
            rr.add((k[0], "*"))
        for k in w:
            ww.add(k)
            rr.add((k[0], "*"))
        self.ops.append(dict(eng=eng, fn=fn, r=tuple(rr), w=tuple(ww), dma=dma))

    def barrier(self, eng, fn, names):
        self.ops.append(dict(eng=eng, fn=fn, r=(), w=tuple((n, "*") for n in names), dma=False))

    def emit(self):
        nc = self.nc
        ops = self.ops
        last_w = {}
        readers = {}
        for i, o in enumerate(ops):
            deps = set()
            for k in o["r"]:
                if k in last_w:
                    deps.add(last_w[k])
            for k in o["w"]:
                if k in last_w:
                    deps.add(last_w[k])
                for rr in readers.get(k, ()):
                    deps.add(rr)
            deps.discard(i)
            o["deps"] = deps
            for k in o["r"]:
                readers.setdefault(k, []).append(i)
            for k in o["w"]:
                last_w[k] = i
                readers[k] = []
        stack = contextlib.ExitStack()
        eng_sem = {e: stack.enter_context(nc.semaphore("s_" + e)) for e in ENGS}
        dma_rings = {}
        for e in ("sp", "act", "pool"):
            dma_rings[e] = [stack.enter_context(nc.semaphore("d_%s%d" % (e, j)))
                            for j in range(self.n_dma_sems)]
        ring_pos = {e: 0 for e in dma_rings}
        ring_cnt = {e: [0] * self.n_dma_sems for e in dma_rings}
        ring_last = {e: [None] * self.n_dma_sems for e in dma_rings}
        for i, o in enumerate(ops):
            if o["dma"]:
                e = o["eng"]
                j = ring_pos[e]
                ring_pos[e] = (j + 1) % self.n_dma_sems
                ring_cnt[e][j] += 16
                o["dsem"] = dma_rings[e][j]
                o["dval"] = ring_cnt[e][j]
                o["dprev"] = ring_last[e][j]
                ring_last[e][j] = i
        for i, o in enumerate(ops):
            cdeps = {}
            ddeps = set()
            for d in o["deps"]:
                od = ops[d]
                if od["dma"]:
                    ddeps.add(d)
                else:
                    e = od["eng"]
                    if e == "pe" and o["eng"] == "pe":
                        continue
                    if e not in cdeps or cdeps[e] < d:
                        cdeps[e] = d
            if o["dma"] and o["dprev"] is not None:
                ddeps.add(o["dprev"])
            o["cdeps"] = cdeps
            o["ddeps"] = ddeps
        signal = set()
        for o in ops:
            for e, d in o["cdeps"].items():
                signal.add(d)
        cnt = {e: 0 for e in ENGS}
        for i, o in enumerate(ops):
            if not o["dma"] and i in signal:
                cnt[o["eng"]] += 1
                o["sig"] = cnt[o["eng"]]
        per_eng = {e: [i for i, o in enumerate(ops) if o["eng"] == e] for e in ENGS}
        self.stats = {e: len(per_eng[e]) for e in ENGS}

        def run_engine(ename, eobj):
            known = {e: 0 for e in ENGS}
            dknown = set()
            for i in per_eng[ename]:
                o = ops[i]
                for e, d in o["cdeps"].items():
                    v = ops[d]["sig"]
                    if known[e] < v:
                        eobj.wait_ge(eng_sem[e], v)
                        known[e] = v
                for d in sorted(o["ddeps"]):
                    if d not in dknown:
                        eobj.wait_ge(ops[d]["dsem"], ops[d]["dval"])
                        dknown.add(d)
                ins = o["fn"](eobj)
                if o["dma"]:
                    ins.then_inc(o["dsem"], 16)
                elif "sig" in o:
                    ins.then_inc(eng_sem[ename], 1)
            if ename in dma_rings:
                for j, s in enumerate(dma_rings[ename]):
                    if ring_cnt[ename][j] > 0:
                        eobj.wait_ge(s, ring_cnt[ename][j])

        with nc.Block() as block:
            @block.tensor
            def _(e):
                run_engine("pe", e)

            @block.scalar
            def _(e):
                run_engine("act", e)

            @block.vector
            def _(e):
                run_engine("dve", e)

            @block.gpsimd
            def _(e):
                run_engine("pool", e)

            @block.sync
            def _(e):
                run_engine("sp", e)
        stack.close()


def bc_last(ap2d, n):
    p, a = ap2d.shape
    return ap2d.unsqueeze(2).to_broadcast([p, a, n])


def bc_mid(ap2d, n):
    p, b = ap2d.shape
    return ap2d.unsqueeze(1).to_broadcast([p, n, b])


class Builder:
    def __init__(self, nc):
        self.nc = nc
        self.P = Prog(nc)
        self.stack = contextlib.ExitStack()
        self.mm_rr = 0
        self.tp_rr = 0
        self.w_rr = 0
        self.dmaq = 0

    def sb(self, name, shape, dt):
        return self.stack.enter_context(self.nc.sbuf_tensor("sb_" + name, list(shape), dt))

    def pst(self, name, shape, dt):
        return self.stack.enter_context(self.nc.psum_tensor("ps_" + name, list(shape), dt))

    def dram_in(self, name, shape):
        return self.nc.dram_tensor(name, list(shape), F32, kind="ExternalInput").ap()

    def dram_out(self, name, shape):
        return self.nc.dram_tensor(name, list(shape), F32, kind="ExternalOutput").ap()

    def mm(self, out, lhsT, rhs, start, stop, r, w, skip=False):
        self.P.op("pe", lambda e: e.matmul(out, lhsT=lhsT, rhs=rhs, start=start, stop=stop,
                                           skip_group_check=skip), r=r, w=w)

    def tr(self, out, in_, ident, r, w):
        self.P.op("pe", lambda e: e.transpose(out=out, in_=in_, identity=ident), r=r, w=w)

    def act(self, out, in_, func, r, w, bias=None, scale=None, accum=None, eng="act"):
        kw = {}
        if bias is not None:
            kw["bias"] = bias
        if scale is not None:
            kw["scale"] = scale
        if accum is not None:
            kw["accum_out"] = accum
        self.P.op("act", lambda e: e.activation(out=out, in_=in_, func=func, **kw), r=r, w=w)

    def tt(self, eng, out, in0, in1, op, r, w):
        self.P.op(eng, lambda e: e.tensor_tensor(out=out, in0=in0, in1=in1, op=op), r=r, w=w)

    def ts(self, eng, out, in0, s1, s2, op0, op1, r, w):
        if op1 is None:
            self.P.op(eng, lambda e: e.tensor_scalar(out=out, in0=in0, scalar1=s1, scalar2=None, op0=op0),
                      r=r, w=w)
        else:
            self.P.op(eng, lambda e: e.tensor_scalar(out=out, in0=in0, scalar1=s1, scalar2=s2,
                                                     op0=op0, op1=op1), r=r, w=w)

    def stt(self, out, in0, scalar, in1, op0, op1, r, w):
        self.P.op("dve", lambda e: e.scalar_tensor_tensor(out=out, in0=in0, scalar=scalar, in1=in1,
                                                          op0=op0, op1=op1), r=r, w=w)

    def cp(self, eng, out, in_, r, w):
        if eng == "act":
            self.P.op("act", lambda e: e.copy(out=out, in_=in_), r=r, w=w)
        else:
            self.P.op(eng, lambda e: e.tensor_copy(out=out, in_=in_), r=r, w=w)

    def memset(self, eng, ap, val, w):
        self.P.op(eng, lambda e: e.memset(ap, val), w=w)

    def recip(self, out, in_, r, w):
        self.P.op("dve", lambda e: e.reciprocal(out=out, in_=in_), r=r, w=w)

    def dma(self, out, in_, r, w, q="sp"):
        self.P.op(q, lambda e: e.dma_start(out=out, in_=in_), r=r, w=w, dma=True)


D = 1024
NG = 4
DIN = 2048
DFF = 4096
NW = 8256
C_UF = 0
C_GRP = 1024
C_DT = 1024 + 4 * 1280
C_GATE = C_DT + 64


def build_program(n_pblocks=2, sample=True):
    nc = bass.Bass("TRN2", target_bir_lowering=False)
    B = Builder(nc)

    xp = B.dram_in("xp", [4, 256, D])
    cvec = B.dram_in("cvec", [128, 8, 2])
    w_mod = B.dram_in("w_mod", [D, 6 * D])
    b_mod2 = B.dram_in("b_mod2", [2, 6 * D])
    g1bc_d = B.dram_in("g1bc", [128, D])
    g2bc_d = B.dram_in("g2bc", [128, D])
    gFbc_d = B.dram_in("gFbc", [128, D])
    w_in = B.dram_in("w_in_r", [D, NW])
    w_fourier = B.dram_in("w_fourier", [D, D])
    w_ssd_out = B.dram_in("w_ssd_out", [DIN, D])
    w_out = B.dram_in("w_out", [D, D])
    w_ff1 = B.dram_in("w_ff1", [D, DFF])
    w_ff2 = B.dram_in("w_ff2", [DFF, D])
    convw_d = B.dram_in("convw", [128, 24, 5])
    convb_d = B.dram_in("convb", [128, 24])
    dtb_d = B.dram_in("dtb", [128, 64])
    alog_d = B.dram_in("alog", [128, 64])
    dskip_d = B.dram_in("dskip", [128, 32])
    ssdg_d = B.dram_in("ssdg", [128, 16])
    ident_d = B.dram_in("ident", [128, 128])
    tris_d = B.dram_in("tris", [128, 5, 128])
    dft_d = B.dram_in("dft256", [3, 256, 256])
    modscr = nc.dram_tensor("modscr", [2, 6 * D], F32, kind="Internal").ap()
    xs_all = B.dram_in("xs_all", [16, 128, D])
    xhalo_d = B.dram_in("xhalo", [128, D])
    hmask_d = B.dram_in("hmask", [128, 1])
    omask_d = B.dram_in("omask", [128, 6])
    dftp_d = B.dram_in("dftp", [2, 2048, 512])
    h0_d = B.dram_in("h0", [2, 128, DIN])
    Escr = nc.dram_tensor("Escr", [3, 2, 4, 128, 512], F32, kind="Internal").ap()

    yp = B.dram_out("yp", [4, 256, D])
    hf_o = B.dram_out("hf", [4, 128, DIN])
    hb_o = B.dram_out("hb", [4, 128, DIN])
    ys = B.dram_out("ys", [512, D])

    ident = B.sb("ident", [128, 128], BF16)
    tris = B.sb("tris", [128, 5, 128], F32)
    trisr = B.sb("trisr", [128, 3, 128], F32R)
    lndt = B.sb("lndt", [128, 4, 64], F32R)
    Rbr = [B.sb("Rbr%d" % i, [128, 1024], F32R) for i in range(2)]
    masks = B.sb("masks", [128, 2, 128], BF16)
    dft = B.sb("dft", [128, 3, 2, 256], BF16)
    convw = B.sb("convw", [128, 24, 5], F32)
    convb = B.sb("convb", [128, 24], F32)
    dtb = B.sb("dtb", [128, 64], F32)
    Abc = B.sb("Abc", [128, 64], F32)
    Dbc = B.sb("Dbc", [128, 32], F32)
    ssdg = B.sb("ssdg", [128, 16], F32)
    cv = B.sb("cv", [128, 16], F32)
    scb = B.sb("scb", [128, 16], BF16)
    modb = B.sb("modb", [128, 4, D], BF16)
    modg = B.sb("modg", [128, 2, D], F32)
    dummy = B.sb("dummy", [128, 2], F32)
    hTh = B.sb("hTh", [128, 8, 128], BF16)
    hmask = B.sb("hmask", [128, 1], F32)
    omask = B.sb("omask", [128, 6], F32)
    Dst = B.sb("Dst", [128, 3, 64], F32)
    suft = B.sb("suft", [128, 64], F32)

    NSLOT = 4
    wsl = [B.sb("wsl%d" % i, [128, 8, 512], BF16) for i in range(NSLOT)]

    xres = B.sb("xres", [128, 4, D], F32)
    htm = [B.sb("htm%d" % i, [128, D], BF16) for i in range(2)]
    junk = B.sb("junk", [128, D], BF16)
    st_ss = B.sb("st_ss", [128, 12], F32)
    st_rs = B.sb("st_rs", [128, 12], F32)
    ssq = B.sb("ssq", [128, 4, 4], F32)
    ry = B.sb("ry", [128, 4], F32)
    hT = B.sb("hT", [128, 8, 512], BF16)
    YT = B.sb("YT", [128, 8, 512], BF16)
    yzT = B.sb("yzT", [128, 16, 512], BF16)
    NAF = 7168
    NAB = 27936
    arenaF = B.sb("arenaF", [128, NAF], F32)
    arenaB = B.sb("arenaB", [128, NAB], BF16)
    ARENA_NAMES = ["gtmp", "UT", "T12", "xpad", "cacc", "xgT", "BT", "CT", "xg", "Bg", "sz", "dt", "dtA", "prep",
                   "w12", "xs", "Rb", "Lt", "Gt", "cbm", "y1", "y2", "y3", "yzb", "Hf", "Hb", "Htmp", "Hfb", "Hbe",
                   "gfs", "Macc", "Mb", "MT", "aT", "rl", "otile", "gFbc", "hTM", "HmT", "UmT", "xst", "dg", "yd", "hTalt"]

    class Ar:
        def __init__(self, t, n):
            self.t, self.n, self.off = t, n, 0

        def take(self, n, inner=None):
            assert self.off + n <= self.n, (self.off, n, self.n)
            ap = self.t[:, self.off:self.off + n]
            self.off += n
            if inner is not None:
                ap = ap.rearrange(inner[0], **inner[1])
            return ap
    aF = Ar(arenaF, NAF)
    aB = Ar(arenaB, NAB)

    def phase():
        aF.off = 0
        aB.off = 0
        B.P.barrier("dve", lambda e: e.memset(dummy[:, :], 0.0), ARENA_NAMES)

    psb = [B.pst("psb%d" % i, [128, 512], F32) for i in range(4)]
    psS = B.pst("psS", [128, 1024], F32)
    pstb = [B.pst("pstb%d" % i, [128, 1024], BF16) for i in range(2)]

    def bank():
        i = B.mm_rr
        B.mm_rr = (i + 1) % 6
        if i < 4:
            return psb[i], ("psb", i)
        return psS[:, (i - 4) * 512:(i - 3) * 512], ("psS", i - 4)

    def tbank():
        i = B.tp_rr
        B.tp_rr = (i + 1) % 2
        return pstb[i], ("pstb", i)

    def load_w(src, r0, kt, c0, ncols):
        i = B.w_rr
        B.w_rr = (i + 1) % NSLOT
        s = wsl[i]
        srcap = src[r0:r0 + kt * 128, c0:c0 + ncols].rearrange("(kt p) n -> p kt n", p=128)
        B.dma(s[:, 0:kt, 0:ncols], srcap, r=[], w=[("wsl", i)], q="pool")
        return s, ("wsl", i)

    B.dma(ident[:], ident_d[:, :], r=[], w=[("ident", 0)], q="pool")
    B.dma(tris[:], tris_d[:, :, :], r=[], w=[("tris", 0)])
    B.dma(masks[:], tris_d[:, 0:2, :], r=[], w=[("masks", 0)], q="pool")
    B.cp("dve", trisr[:, 0:2, :], tris[:, 2:4, :], r=[("tris", 0)], w=[("trisr", 0)])
    B.dma(arenaF[:, 2048:2176], ident_d[:, :], r=[], w=[("gtmp", 9)])
    B.cp("dve", trisr[:, 2, :], arenaF[:, 2048:2176], r=[("gtmp", 9)], w=[("trisr", 0)])
    for m in range(3):
        B.dma(dft[:, m, :, :], dft_d[m].rearrange("(kt p) n -> p kt n", p=128), r=[], w=[("dft", m)], q="pool")
    B.dma(convw[:], convw_d[:, :, :], r=[], w=[("convw", 0)])
    B.dma(convb[:], convb_d[:, :], r=[], w=[("convb", 0)])
    B.dma(dtb[:], dtb_d[:, :], r=[], w=[("dtb", 0)])
    B.dma(Abc[:], alog_d[:, :], r=[], w=[("Abc", 0)])
    B.dma(Dbc[:], dskip_d[:, :], r=[], w=[("Dbc", 0)])
    B.dma(ssdg[:], ssdg_d[:, :], r=[], w=[("ssdg", 0)])
    B.dma(cv[:], cvec.rearrange("p k r -> p (k r)"), r=[], w=[("cv", 0)])
    B.dma(hmask[:], hmask_d[:, :], r=[], w=[("hmask", 0)])
    B.dma(omask[:], omask_d[:, :], r=[], w=[("omask", 0)])
    B.act(Abc[:], Abc[:], AF.Exp, r=[("Abc", 0)], w=[("Abc", 0)])
    B.ts("dve", Abc[:], Abc[:], -1.0, None, ALU.mult, None, r=[("Abc", 0)], w=[("Abc", 0)])
    B.act(scb[:], cv[:], AF.Silu, r=[("cv", 0)], w=[("scb", 0)])
    scv = scb[:].rearrange("p (k r) -> p k r", r=2)
    DFT_ALL = [("dft", 0), ("dft", 1), ("dft", 2)]

    modp = htm[0][:, :].bitcast(F32)[0:2, 0:512]
    bmod = htm[1][:, :].bitcast(F32)[0:2, 0:512]

    def mod_piece(cb_):
        B.dma(bmod, b_mod2[:, cb_ * 512:(cb_ + 1) * 512], r=[], w=[("htm", 1)])
        s, sk = load_w(w_mod, 0, 8, cb_ * 512, 512)
        ps, pk = bank()
        for k in range(8):
            B.mm(ps[0:2, :], scv[:, k, :], s[:, k, :], k == 0, k == 7, r=[sk, ("scb", 0)], w=[pk])
        B.tt("dve", modp, ps[0:2, :], bmod, ALU.add, r=[pk, ("htm", 1)], w=[("htm", 0)])
        B.dma(modscr[:, cb_ * 512:(cb_ + 1) * 512], modp, r=[("htm", 0)], w=[("modscr", cb_)])

    def mod_part(row, vs, temps=None):
        if temps is None:
            temps = (aF.take(D), aF.take(D))
        t0, t1 = temps
        for v in vs:
            B.dma(t0[:, :], modscr[row:row + 1, v * D:(v + 1) * D].partition_broadcast(128),
                  r=[("modscr", 2 * v), ("modscr", 2 * v + 1)], w=[("gtmp", 7)])
            if v in (0, 3):
                B.cp("dve", modb[:, 0 if v == 0 else 2, :], t0[:, :], r=[("gtmp", 7)], w=[("modb", v)])
            elif v in (1, 4):
                B.dma(t1[:, :], (g1bc_d if v == 1 else g2bc_d)[:, :], r=[], w=[("gtmp", 8)])
                B.stt(modb[:, 1 if v == 1 else 3, :], t0[:, :], 1.0, t1[:, :], ALU.add, ALU.mult,
                      r=[("gtmp", 7), ("gtmp", 8)], w=[("modb", v)])
            else:
                B.cp("dve", modg[:, 0 if v == 2 else 1, :], t0[:, :], r=[("gtmp", 7)], w=[("modg", v)])

    def rms_stats(src, col, r, dim=D):
        B.act(junk[:, 0:src.shape[1]], src, AF.Square, r=r, w=[("junk", 0), ("st_ss", col)],
              accum=st_ss[:, col:col + 1])
        B.act(st_ss[:, col:col + 1], st_ss[:, col:col + 1], AF.Sqrt, r=[("st_ss", col)], w=[("st_ss", col)],
              bias=EPS, scale=1.0 / dim)
        B.recip(st_rs[:, col:col + 1], st_ss[:, col:col + 1], r=[("st_ss", col)], w=[("st_rs", col)])

    def tile_T(src_tm, skey, dstT, dname, t):
        tb, tk = tbank()
        tbv = tb[:].rearrange("p (k c) -> p k c", c=128)
        for k in range(8):
            B.tr(tbv[:, k, :], src_tm[:, k * 128:(k + 1) * 128], ident[:], r=[skey, ("ident", 0)], w=[tk])
        B.cp("act", dstT[:, :, t * 128:(t + 1) * 128], tbv, r=[tk], w=[(dname, t)])

    def norm_mod_T(t, vshift, vscale, dstT, dname, gtmp):
        rms_stats(xres[:, t, :], t, r=[("xres", t)])
        hb_ = htm[t % 2]
        hk = ("htm", t % 2)
        B.stt(gtmp[:, :], xres[:, t, :], st_rs[:, t:t + 1], modb[:, vscale, :], ALU.mult, ALU.mult,
              r=[("xres", t), ("st_rs", t), ("modb", vscale)], w=[("gtmp", 0)])
        B.tt("dve", hb_[:], gtmp[:, :], modb[:, vshift, :], ALU.add, r=[("gtmp", 0), ("modb", vshift)], w=[hk])
        tile_T(hb_, hk, dstT, dname, t)

    hT_all = [("hT", t) for t in range(4)]
    R3 = ("p (h q) -> p h q", dict(q=64))

    def dt_prep(kind, om, dt, dtA, prep, w12, hsrc=None, hname="hT"):
        if hsrc is None:
            hsrc = hT
        s, sk = load_w(w_in, 0, 8, C_DT, 64)
        for t in range(4):
            ps, pk = bank()
            for k in range(8):
                B.mm(ps[:, 0:64], hsrc[:, k, t * 128:(t + 1) * 128], s[:, k, 0:64], k == 0, k == 7,
                     r=[sk, (hname, t)], w=[pk])
            B.tt("dve", dt[:, t, :], ps[:, 0:64], dtb[:], ALU.add, r=[pk, ("dtb", 0)], w=[("dt", t)])
            B.act(dt[:, t, :], dt[:, t, :], AF.Exp, r=[("dt", t)], w=[("dt", t)])
            B.act(dt[:, t, :], dt[:, t, :], AF.Ln, r=[("dt", t)], w=[("dt", t)], bias=1.0)
            if kind == "O":
                for d in range(2):
                    B.ts("dve", dt[:, t, d * 32:(d + 1) * 32], dt[:, t, d * 32:(d + 1) * 32],
                         omask[:, om * 2 + d:om * 2 + d + 1], None, ALU.mult, None,
                         r=[("dt", t), ("omask", 0)], w=[("dt", t)])
            B.tt("dve", dtA[:, t, :], dt[:, t, :], Abc[:], ALU.mult, r=[("dt", t), ("Abc", 0)], w=[("dtA", t)])
            if kind != "O":
                B.act(lndt[:, t, :], dt[:, t, :], AF.Ln, r=[("dt", t)], w=[("lndt", t)], bias=1e-18)
            ps, pk = bank()
            pv = ps[:, 0:192].rearrange("p (a b) -> p a b", b=64)
            for d in range(2):
                B.mm(pv[:, 0, d * 32:(d + 1) * 32], tris[:, 2 + d, :], dtA[:, t, d * 32:(d + 1) * 32], True, True,
                     r=[("tris", 0), ("dtA", t)], w=[pk])
                B.mm(pv[:, 1, d * 32:(d + 1) * 32], tris[:, d, :], dtA[:, t, d * 32:(d + 1) * 32], True, True,
                     r=[("tris", 0), ("dtA", t)], w=[pk])
            B.mm(pv[:, 2, :], tris[:, 4, :], dtA[:, t, :], True, True, r=[("tris", 0), ("dtA", t)], w=[pk])
            B.act(prep[:, t, :, :], pv, AF.Exp, r=[pk], w=[("prep", t)])
            B.cp("dve", w12[:, t, 0, :], dt[:, t, :], r=[("dt", t)], w=[("w12", t)])
            B.tt("dve", w12[:, t, 1, :], dt[:, t, :], prep[:, t, 0, :], ALU.mult,
                 r=[("dt", t), ("prep", t)], w=[("w12", t)])
        if kind == "O":
            for d, order in ((0, (3, 2, 1, 0)), (1, (0, 1, 2, 3))):
                cs = slice(d * 32, (d + 1) * 32)
                for n_, t in enumerate(order):
                    if n_ == 0:
                        continue
                    tp_ = order[n_ - 1]
                    if n_ == 1:
                        src_ = prep[:, tp_, 2, cs]
                        srck = [("prep", tp_)]
                    else:
                        B.tt("dve", suft[:, cs], (prep[:, order[0], 2, cs] if n_ == 2 else suft[:, cs]),
                             prep[:, tp_, 2, cs], ALU.mult,
                             r=[("prep", order[0]), ("prep", tp_), ("suft", d)], w=[("suft", d)])
                        src_ = suft[:, cs]
                        srck = [("suft", d)]
                    B.tt("dve", w12[:, t, 1, cs], w12[:, t, 1, cs], src_, ALU.mult,
                         r=[("w12", t)] + srck, w=[("w12", t)])
            B.tt("dve", Dst[:, om, :], prep[:, 0, 2, :], prep[:, 1, 2, :], ALU.mult,
                 r=[("prep", 0), ("prep", 1)], w=[("Dst", om)])
            B.tt("dve", Dst[:, om, :], Dst[:, om, :], prep[:, 2, 2, :], ALU.mult,
                 r=[("Dst", om), ("prep", 2)], w=[("Dst", om)])
            B.tt("dve", Dst[:, om, :], Dst[:, om, :], prep[:, 3, 2, :], ALU.mult,
                 r=[("Dst", om), ("prep", 3)], w=[("Dst", om)])

    def block_body(kind, runs, out_rows, state_out, bs=0, om=0, hooks=None):
        hooks = hooks or {}
        nrun = len(runs)
        L = 512 // nrun
        phase()
        xpad = [aB.take(528) for _ in range(2)]
        dg = [aB.take(640, ("p (k c) -> p k c", dict(c=128))) for _ in range(2)]
        dt = aF.take(256, ("p (t c) -> p t c", dict(c=64)))
        dtA = aF.take(256, ("p (t c) -> p t c", dict(c=64)))
        prep = aF.take(768, ("p (t a c) -> p t a c", dict(a=3, c=64)))
        w12 = aF.take(512, ("p (t a c) -> p t a c", dict(a=2, c=64)))
        Rb = Rbr
        ybuf = [[aF.take(512) for _ in range(3)] for _ in range(2)]
        Hf = aF.take(512)
        Hb = aF.take(512)
        Htmp = aF.take(512)
        xgT = aB.take(2048, ("p (c t) -> p c t", dict(t=512)))
        BT2 = [aB.take(512) for _ in range(2)]
        CT2 = [aB.take(512) for _ in range(2)]
        xg2 = [aB.take(2048, ("p (c t) -> p c t", dict(t=512))) for _ in range(2)]
        Bg2 = [aB.take(512, ("p (c t) -> p c t", dict(t=128))) for _ in range(2)]
        sz2 = [aB.take(2048, ("p (c t) -> p c t", dict(t=512))) for _ in range(2)]
        xs2 = [aB.take(1536, ("p (c t) -> p c t", dict(t=512))) for _ in range(2)]
        Lt1 = aB.take(1024)
        Gt = [[aB.take(1024) for _ in range(2)] for _ in range(2)]
        cbm = [aB.take(256, ("p (c t) -> p c t", dict(t=128))) for _ in range(2)]
        yzb2 = [aB.take(512) for _ in range(2)]
        Hfb = aB.take(512)
        Hbe = aB.take(2048, ("p (c t) -> p c t", dict(t=512)))
        if kind == "P":
            for i in range(2):
                B.memset("dve", xpad[i][:, :], 0.0, w=[("xpad", i)])
        dt_prep(kind, om, dt, dtA, prep, w12)
        v3 = lambda ap: ap.rearrange(R3[0], **R3[1])

        def load_group(g_):
            a_ = load_w(w_in, 0, 8, C_GRP + g_ * 1280, 512)
            b_ = load_w(w_in, 0, 8, C_GRP + g_ * 1280 + 512, 256)
            c_ = load_w(w_in, 0, 8, C_GRP + g_ * 1280 + 768, 512)
            return a_, b_, c_
        wq = {0: load_group(0)}
        pre_gates_box = []

        def make_head(g):
            gp = g % 2
            BT, CT, xg, Bg, sz = BT2[gp], CT2[gp], xg2[gp], Bg2[gp], sz2[gp]
            ops_ = []

            def ct_stage1(ct):
                (sx, sxk), (sbc, sbck) = wq[g][0], wq[g][1]
                if ct < 4:
                    sl, slk, c0 = sx, sxk, ct * 128
                else:
                    sl, slk, c0 = sbc, sbck, (ct - 4) * 128
                gct = g * 6 + ct
                ps, pk = bank()
                for k in range(8):
                    B.mm(ps[:, :], sl[:, k, c0:c0 + 128], hT[:, k, :], k == 0, k == 7, r=[slk] + hT_all, w=[pk])
                xp_ = xpad[ct % 2]
                xk = ("xpad", ct % 2)
                xv = xp_[:, 0:nrun * (L + 4)].rearrange("p (r l) -> p r l", l=L + 4)
                B.cp("act", xv[:, :, 2:L + 2], ps[:, :].rearrange("p (r l) -> p r l", l=L), r=[pk], w=[xk])
                if kind != "P":
                    ps2, pk2 = bank()
                    for k in range(8):
                        B.mm(ps2[:, 0:4], sl[:, k, c0:c0 + 128], hTh[:, k, bs * 4:bs * 4 + 4], k == 0, k == 7,
                             r=[slk, ("hTh", 0)], w=[pk2])
                    B.cp("act", xp_[:, 0:2], ps2[:, 0:2], r=[pk2], w=[xk])
                    B.cp("act", xp_[:, 514:516], ps2[:, 2:4], r=[pk2], w=[xk])
                dg_ = dg[ct % 2]
                dgk = ("dg", ct % 2)
                for kk in range(5):
                    B.ts("dve", dg_[:, kk, :], ident[:, :], convw[:, gct, kk:kk + 1], None, ALU.mult, None,
                         r=[("ident", 0), ("convw", 0)], w=[dgk])

            def ct_stage2(ct):
                gct = g * 6 + ct
                xp_ = xpad[ct % 2]
                xk = ("xpad", ct % 2)
                xv = xp_[:, 0:nrun * (L + 4)].rearrange("p (r l) -> p r l", l=L + 4)
                dg_ = dg[ct % 2]
                dgk = ("dg", ct % 2)
                psc, pck = bank()
                for kk in range(5):
                    B.mm(psc[:, :].rearrange("p (r l) -> p r l", l=L), dg_[:, kk, :], xv[:, :, kk:kk + L],
                         kk == 0, kk == 4, r=[dgk, xk], w=[pck])
                if ct < 4:
                    dst_, dstk = xgT[:, ct, :], ("xgT", ct)
                elif ct == 4:
                    dst_, dstk = BT[:, :], ("BT", gp)
                else:
                    dst_, dstk = CT[:, :], ("CT", gp)
                B.act(dst_, psc[:, :], AF.Silu, r=[pck, ("convb", 0)], w=[dstk], bias=convb[:, gct:gct + 1])

            def t_stage(t):
                szw, szk = wq[g][2]
                tb, tk = tbank()
                tbv = tb[:].rearrange("p (k c) -> p k c", c=128)
                for ct in range(4):
                    B.tr(tbv[:, ct, :], xgT[:, ct, t * 128:(t + 1) * 128], ident[:],
                         r=[("xgT", ct), ("ident", 0)], w=[tk])
                B.tr(tbv[:, 4, :], BT[:, t * 128:(t + 1) * 128], ident[:], r=[("BT", gp), ("ident", 0)], w=[tk])
                B.cp("act", xg[:, t, :], tb[:, 0:512], r=[tk], w=[("xg", gp, t)])
                B.cp("act", Bg[:, t, :], tb[:, 512:640], r=[tk], w=[("Bg", gp, t)])
                ps, pk = bank()
                for k in range(8):
                    B.mm(ps[:, :], hT[:, k, t * 128:(t + 1) * 128], szw[:, k, :], k == 0, k == 7,
                         r=[szk, ("hT", t)], w=[pk])
                B.act(sz[:, t, :], ps[:, :], AF.Silu, r=[pk], w=[("sz", gp, t)])

            def first():
                if "group" in hooks:
                    hooks["group"](g)
                ct_stage1(0)
            ops_.append(first)
            for ct in range(6):
                def _f(ct=ct):
                    if ct + 1 < 6:
                        ct_stage1(ct + 1)
                    ct_stage2(ct)
                ops_.append(_f)
            def pre1():
                if g + 1 < NG:
                    g_ = g + 1
                    wq[g_] = [load_w(w_in, 0, 8, C_GRP + g_ * 1280, 512),
                              load_w(w_in, 0, 8, C_GRP + g_ * 1280 + 512, 256), None]
                else:
                    pre_gates_box.extend(load_w(w_in, 0, 8, C_GATE + gi_ * 512, 512) for gi_ in range(2))
            ops_.append(pre1)
            for t in range(4):
                ops_.append(lambda t=t: t_stage(t))

            def pre2():
                if g + 1 < NG:
                    g_ = g + 1
                    wq[g_][2] = load_w(w_in, 0, 8, C_GRP + g_ * 1280 + 768, 512)
                else:
                    pre_gates_box.append(load_w(w_in, 0, 8, C_GATE + 2 * 512, 512))
            ops_.append(pre2)
            return ops_

        def make_sweeps(g):
            gp = g % 2
            BT, CT, xg, Bg, sz = BT2[gp], CT2[gp], xg2[gp], Bg2[gp], sz2[gp]
            ops_ = []

            def xscale(t, j):
                d = j - 1
                xs = xs2[t % 2]
                wv = w12[:, t, 1, d * 32 + g * 8: d * 32 + g * 8 + 8]
                B.tt("pool", v3(xs[:, j, :]), v3(xg[:, t, :]), bc_last(wv, 64), ALU.mult,
                     r=[("xg", gp, t), ("w12", t)], w=[("xs", t % 2, j)])

            def state_update(H, Hk, t, d):
                xscale(t, 1 + d)
                xs = xs2[t % 2]
                ps, pk = bank()
                B.mm(ps[:, :], Bg[:, t, :], xs[:, 1 + d, :], True, True, r=[("Bg", gp, t), ("xs", t % 2, 1 + d)], w=[pk])
                cdv = prep[:, t, 2, d * 32 + g * 8: d * 32 + g * 8 + 8]
                B.tt("dve", v3(Htmp[:, :]), v3(H[:, :]), bc_last(cdv, 64), ALU.mult, r=[Hk, ("prep", t)],
                     w=[("Htmp", 0)])
                B.tt("dve", H[:, :], ps[:, :], Htmp[:, :], ALU.add, r=[pk, ("Htmp", 0)], w=[Hk])

            def chain_init(H, Hk, d):
                B.dma(H[:, :], h0_d[d, :, g * 512:(g + 1) * 512], r=[], w=[Hk])
                for m in (range(3) if d == 0 else reversed(range(3))):
                    y3 = ybuf[0][2]
                    B.dma(y3[:, :], Escr[m, d, g, :, :], r=[("Escr", m, d, g)], w=[("y3", 0)])
                    dv_ = Dst[:, m, d * 32 + g * 8: d * 32 + g * 8 + 8]
                    B.tt("dve", v3(Htmp[:, :]), v3(H[:, :]), bc_last(dv_, 64), ALU.mult, r=[Hk, ("Dst", m)],
                         w=[("Htmp", 0)])
                    B.tt("dve", H[:, :], Htmp[:, :], y3[:, :], ALU.add, r=[("Htmp", 0), ("y3", 0)], w=[Hk])

            for ri, run in enumerate(runs):
                def _init():
                    if kind == "S":
                        chain_init(Hb, ("Hb", 0), 1)
                    else:
                        B.memset("dve", Hb[:, :], 0.0, w=[("Hb", 0)])
                ops_.append(_init)
                for t in reversed(run):
                    def _st(t=t):
                        B.cp("act", Hbe[:, t, :], Hb[:, :], r=[("Hb", 0)], w=[("Hbe", t)])
                        state_update(Hb, ("Hb", 0), t, 1)
                    ops_.append(_st)
                if state_out is not None:
                    def _out(ri=ri):
                        B.dma(hb_o[state_out[ri], :, g * 512:(g + 1) * 512], Hb[:, :], r=[("Hb", 0)],
                              w=[("hb_o", 0)])
                    ops_.append(_out)

            def stage_a(t):
                par = t % 2
                tsl = slice(t * 128, (t + 1) * 128)
                ps, pk = bank()
                B.mm(ps[:, 0:128], BT[:, tsl], CT[:, tsl], True, True, r=[("BT", gp), ("CT", gp)], w=[pk])
                for d in range(2):
                    B.tt("dve", cbm[par][:, d, :], ps[:, 0:128], masks[:, d, :], ALU.mult,
                         r=[pk, ("masks", 0)], w=[("cbm", par, d)])
                for d in range(2):
                    dv = dtA[:, t, d * 32 + g * 8: d * 32 + g * 8 + 8]
                    B.tt("pool", Rb[d][:, :].rearrange("p (h i) -> p h i", i=128), bc_last(dv, 128),
                         bc_mid(tris[:, d, :], 8), ALU.mult, r=[("dtA", t), ("tris", 0)], w=[("Rb", d)])
                    for hh in range(2):
                        B.mm(psS[:, hh * 512:(hh + 1) * 512], trisr[:, d, :], Rb[d][:, hh * 512:(hh + 1) * 512],
                             True, False, r=[("trisr", 0), ("Rb", d)], w=[("psS", hh)])
                        hd0 = d * 32 + g * 8 + hh * 4
                        B.mm(psS[:, hh * 512:(hh + 1) * 512].rearrange("p (h i) -> p h i", i=128), trisr[:, 2, :],
                             bc_last(lndt[:, t, hd0:hd0 + 4], 128),
                             False, True, r=[("trisr", 0), ("lndt", t)], w=[("psS", hh)])
                    B.act(Lt1[:, :], psS[:, :], AF.Exp, r=[("psS", 0), ("psS", 1)], w=[("Lt", 0)])
                    B.tt("dve", Gt[par][d][:, :].rearrange("p (h i) -> p h i", i=128),
                         Lt1[:, :].rearrange("p (h i) -> p h i", i=128), bc_mid(cbm[par][:, d, :], 8), ALU.mult,
                         r=[("Lt", 0), ("cbm", par, d)], w=[("Gt", par, d)])

            def stage_b(t):
                par = t % 2
                tsl = slice(t * 128, (t + 1) * 128)
                psy, pyk = bank()
                xD = xs2[par][:, 0, :]
                B.tt("pool", v3(xD), v3(xg[:, t, :]), bc_last(Dbc[:, g * 8:g * 8 + 8], 64), ALU.mult,
                     r=[("xg", gp, t), ("Dbc", 0)], w=[("xs", par, 0)])
                B.mm(psy[:, :], ident[:, :], xD, True, False, r=[("ident", 0), ("xs", par, 0)], w=[pyk], skip=True)
                n = 0
                for h in range(8):
                    for d in range(2):
                        B.mm(psy[:, h * 64:(h + 1) * 64], Gt[par][d][:, h * 128:(h + 1) * 128],
                             xg[:, t, h * 64:(h + 1) * 64], False, n == 15,
                             r=[("Gt", par, d), ("xg", gp, t)], w=[pyk], skip=True)
                        n += 1
                pof, pofk = bank()
                B.mm(pof[:, :], CT[:, tsl], Hfb[:, :], True, True, r=[("CT", gp), ("Hfb", 0)], w=[pofk])
                pob, pobk = bank()
                B.mm(pob[:, :], CT[:, tsl], Hbe[:, t, :], True, True, r=[("CT", gp), ("Hbe", t)], w=[pobk])
                state_update(Hf, ("Hf", 0), t, 0)
                B.cp("act", Hfb[:, :], Hf[:, :], r=[("Hf", 0)], w=[("Hfb", 0)])
                ef = prep[:, t, 1, g * 8: g * 8 + 8]
                eb = prep[:, t, 1, 32 + g * 8: 32 + g * 8 + 8]
                y1, y2, y3 = ybuf[par]
                yk = lambda nm: (nm, par)
                B.tt("dve", v3(y1[:, :]), v3(pof[:, :]), bc_last(ef, 64), ALU.mult, r=[pofk, ("prep", t)], w=[yk("y1")])
                B.tt("dve", v3(y2[:, :]), v3(pob[:, :]), bc_last(eb, 64), ALU.mult, r=[pobk, ("prep", t)], w=[yk("y2")])
                B.tt("dve", y3[:, :], psy[:, :], y1[:, :], ALU.add, r=[pyk, yk("y1")], w=[yk("y3")])
                B.tt("dve", y3[:, :], y3[:, :], y2[:, :], ALU.add, r=[yk("y3"), yk("y2")], w=[yk("y3")])
                B.tt("dve", y1[:, :], y3[:, :], sz[:, t, :], ALU.mult, r=[yk("y3"), ("sz", gp, t)], w=[yk("y1")])
                B.act(junk[:, 0:512], y1[:, :], AF.Square, r=[yk("y1")], w=[("junk", 0), ("ssq", t, g)],
                      accum=ssq[:, t, g:g + 1])
                B.cp("act", yzb2[par][:, :], y1[:, :], r=[yk("y1")], w=[("yzb", par)])

            def stage_b2(t):
                par = t % 2
                tsl = slice(t * 128, (t + 1) * 128)
                tb, tk = tbank()
                tbv = tb[:].rearrange("p (k c) -> p k c", c=128)
                for ct in range(4):
                    B.tr(tbv[:, ct, :], yzb2[par][:, ct * 128:(ct + 1) * 128], ident[:],
                         r=[("yzb", par), ("ident", 0)], w=[tk])
                B.cp("act", yzT[:, g * 4:g * 4 + 4, tsl], tbv[:, 0:4, :], r=[tk], w=[("yzT", g, t)])

            steps = [(ri, t) for ri, run in enumerate(runs) for t in run]
            ops_.append(lambda: stage_a(steps[0][1]))
            for si, (ri, t) in enumerate(steps):
                run = runs[ri]
                if t == run[0]:
                    def _fi():
                        if kind == "S":
                            chain_init(Hf, ("Hf", 0), 0)
                            B.cp("act", Hfb[:, :], Hf[:, :], r=[("Hf", 0)], w=[("Hfb", 0)])
                        else:
                            B.memset("dve", Hf[:, :], 0.0, w=[("Hf", 0)])
                            B.memset("dve", Hfb[:, :], 0.0, w=[("Hfb", 0)])
                    ops_.append(_fi)
                if si + 1 < len(steps):
                    ops_.append(lambda si=si: stage_a(steps[si + 1][1]))
                ops_.append(lambda t=t: stage_b(t))
                if si > 0:
                    ops_.append(lambda si=si: stage_b2(steps[si - 1][1]))
                if t == run[-1] and state_out is not None:
                    def _fo(ri=ri):
                        B.dma(hf_o[state_out[ri], :, g * 512:(g + 1) * 512], Hf[:, :], r=[("Hf", 0)],
                              w=[("hf_o", 0)])
                    ops_.append(_fo)
            ops_.append(lambda: stage_b2(steps[-1][1]))
            return ops_

        for f_ in make_head(0):
            f_()
        for g in range(NG):
            sw = make_sweeps(g)
            hd = make_head(g + 1) if g + 1 < NG else []
            i_h = 0
            for i_s, f_ in enumerate(sw):
                f_()
                while i_h < len(hd) and i_h * len(sw) <= (i_s + 1) * len(hd) - 1 and i_s >= 1:
                    hd[i_h]()
                    i_h += 1
            while i_h < len(hd):
                hd[i_h]()
                i_h += 1
        pre_gates = pre_gates_box
        for t in range(4):
            B.tt("dve", ssq[:, t, 0:2], ssq[:, t, 0:2], ssq[:, t, 2:4], ALU.add,
                 r=[("ssq", t, g_) for g_ in range(4)], w=[("ssq", t, 0), ("ssq", t, 1)])
            B.tt("dve", ssq[:, t, 0:1], ssq[:, t, 0:1], ssq[:, t, 1:2], ALU.add,
                 r=[("ssq", t, 0), ("ssq", t, 1)], w=[("ssq", t, 0)])
            B.act(ssq[:, t, 0:1], ssq[:, t, 0:1], AF.Sqrt, r=[("ssq", t, 0)], w=[("ssq", t, 0)],
                  bias=EPS, scale=1.0 / DIN)
            B.recip(ry[:, t:t + 1], ssq[:, t, 0:1], r=[("ssq", t, 0)], w=[("ry", t)])
        phase()
        Macc = aF.take(4096, ("p (t c) -> p t c", dict(c=D)))
        y1 = aF.take(512)
        gfs = aB.take(8192, ("p (t a c) -> p t a c", dict(a=2, c=D)))
        Mb = aB.take(1024)
        MT = aB.take(4096, ("p (k t) -> p k t", dict(t=512)))
        mtmp = (aF.take(D), aF.take(D))
        for gi in range(4):
            s, sk = pre_gates[gi] if gi < 3 else load_w(w_in, 0, 8, C_GATE + gi * 512, 512)
            for t in range(4):
                ps, pk = bank()
                for k in range(8):
                    B.mm(ps[:, :], hT[:, k, t * 128:(t + 1) * 128], s[:, k, :], k == 0, k == 7,
                         r=[sk, ("hT", t)], w=[pk])
                B.act(gfs[:, t, gi // 2, (gi % 2) * 512:(gi % 2 + 1) * 512], ps[:, :], AF.Sigmoid,
                      r=[pk], w=[("gfs", t, gi)])
        if "merge_start" in hooks:
            hooks["merge_start"](mtmp)
        for half in range(2):
            hs = slice(half * 512, (half + 1) * 512)
            s, sk = load_w(w_fourier, 0, 8, half * 512, 512)
            for t in range(4):
                ps, pk = bank()
                for k in range(8):
                    B.mm(ps[:, :], YT[:, k, t * 128:(t + 1) * 128], s[:, k, :], k == 0, k == 7,
                         r=[sk] + [("YT", kk) for kk in range(8)], w=[pk])
                B.tt("dve", Macc[:, t, hs], ps[:, :], gfs[:, t, 0, hs], ALU.mult,
                     r=[pk, ("gfs", t, half)], w=[("Macc", t, half)])
        for half in range(2):
            hs = slice(half * 512, (half + 1) * 512)
            sl2 = []
            for kh in range(2):
                s, sk = load_w(w_ssd_out, kh * 1024, 8, half * 512, 512)
                for k in range(8):
                    B.ts("dve", s[:, k, :], s[:, k, :], ssdg[:, kh * 8 + k:kh * 8 + k + 1], None, ALU.mult, None,
                         r=[sk, ("ssdg", 0)], w=[sk])
                sl2.append((s, sk))
            for t in range(4):
                ps, pk = bank()
                for kk in range(16):
                    s, sk = sl2[kk // 8]
                    B.mm(ps[:, :], yzT[:, kk, t * 128:(t + 1) * 128], s[:, kk % 8, :], kk == 0, kk == 15,
                         r=[sk] + [("yzT", g_, t) for g_ in range(4)], w=[pk])
                B.stt(y1[:, :], ps[:, :], ry[:, t:t + 1], gfs[:, t, 1, hs], ALU.mult, ALU.mult,
                      r=[pk, ("ry", t), ("gfs", t, 2 + half)], w=[("y1", 0)])
                B.tt("dve", Macc[:, t, hs], Macc[:, t, hs], y1[:, :], ALU.add,
                     r=[("Macc", t, half), ("y1", 0)], w=[("Macc", t, half)])
        for t in range(4):
            B.cp("act", Mb[:, :], Macc[:, t, :], r=[("Macc", t, 0), ("Macc", t, 1)], w=[("Mb", 0)])
            tile_T(Mb, ("Mb", 0), MT, "MT", t)
        for half in range(2):
            hs = slice(half * 512, (half + 1) * 512)
            s, sk = load_w(w_out, 0, 8, half * 512, 512)
            for t in range(4):
                ps, pk = bank()
                for k in range(8):
                    B.mm(ps[:, :], MT[:, k, t * 128:(t + 1) * 128], s[:, k, :], k == 0, k == 7,
                         r=[sk, ("MT", t)], w=[pk])
                B.tt("dve", y1[:, :], ps[:, :], modg[:, 0, hs], ALU.mult, r=[pk, ("modg", 2)], w=[("y1", 0)])
                B.tt("dve", xres[:, t, hs], xres[:, t, hs], y1[:, :], ALU.add, r=[("xres", t), ("y1", 0)],
                     w=[("xres", t)])
        phase()
        gtmp = aF.take(D)
        y1 = aF.take(512)
        otile = [aF.take(D) for _ in range(2)]
        gFbc = aF.take(D)
        aT = aB.take(16384, ("p (k t) -> p k t", dict(t=512)))
        rl = [aB.take(512) for _ in range(2)]
        B.dma(gFbc[:, :], gFbc_d[:, :], r=[], w=[("gFbc", 0)])
        mtmp = (aF.take(D), aF.take(D))
        if "mlp_start" in hooks:
            hooks["mlp_start"](mtmp)
        for t in range(4):
            norm_mod_T(t, 2, 3, hT, "hT", gtmp)
        if "after_norm2" in hooks:
            hooks["after_norm2"](mtmp)
        for fb in range(8):
            s, sk = load_w(w_ff1, 0, 8, fb * 512, 512)
            for ft in range(4):
                ps, pk = bank()
                for k in range(8):
                    B.mm(ps[:, :], s[:, k, ft * 128:(ft + 1) * 128], hT[:, k, :], k == 0, k == 7,
                         r=[sk] + hT_all, w=[pk])
                r_ = rl[ft % 2]
                rk = ("rl", ft % 2)
                B.act(r_[:, :], ps[:, :], AF.Relu, r=[pk], w=[rk])
                B.tt("dve", aT[:, fb * 4 + ft, :], r_[:, :], r_[:, :], ALU.mult, r=[rk], w=[("aT", fb * 4 + ft)])
        if "after_ff1" in hooks:
            hooks["after_ff1"](mtmp)
        for half in range(2):
            hs = slice(half * 512, (half + 1) * 512)
            for kq in range(4):
                s, sk = load_w(w_ff2, kq * 1024, 8, half * 512, 512)
                for t in range(4):
                    for k in range(8):
                        B.mm(psb[t][:, :], aT[:, kq * 8 + k, t * 128:(t + 1) * 128], s[:, k, :],
                             kq == 0 and k == 0, kq == 3 and k == 7,
                             r=[sk, ("aT", kq * 8 + k)], w=[("psb", t)])
            for t in range(4):
                B.tt("dve", y1[:, :], psb[t][:, :], modg[:, 1, hs], ALU.mult, r=[("psb", t), ("modg", 5)],
                     w=[("y1", 0)])
                B.tt("dve", xres[:, t, hs], xres[:, t, hs], y1[:, :], ALU.add, r=[("xres", t), ("y1", 0)],
                     w=[("xres", t)])
        for t in range(4):
            rms_stats(xres[:, t, :], 4 + t, r=[("xres", t)])
            ot = otile[t % 2]
            ok_ = ("otile", t % 2)
            B.stt(ot[:, :], xres[:, t, :], st_rs[:, 4 + t:5 + t], gFbc[:, :], ALU.mult, ALU.mult,
                  r=[("xres", t), ("st_rs", 4 + t), ("gFbc", 0)], w=[ok_])
            B.dma(out_rows(t), ot[:, :], r=[ok_], w=[("out", t)])

    def other_block(om, hcur, hname, next_norm):
        phase()
        L = 512
        hk_all = [(hname, t) for t in range(4)]
        xpad = [aB.take(528) for _ in range(2)]
        dg = [aB.take(640, ("p (k c) -> p k c", dict(c=128))) for _ in range(2)]
        dt = aF.take(256, ("p (t c) -> p t c", dict(c=64)))
        dtA = aF.take(256, ("p (t c) -> p t c", dict(c=64)))
        prep = aF.take(768, ("p (t a c) -> p t a c", dict(a=3, c=64)))
        w12 = aF.take(512, ("p (t a c) -> p t a c", dict(a=2, c=64)))
        Eb = [aF.take(512) for _ in range(2)]
        xgT = [aB.take(2048, ("p (c t) -> p c t", dict(t=512))) for _ in range(2)]
        BT = [aB.take(512) for _ in range(2)]
        xg = [aB.take(2048, ("p (c t) -> p c t", dict(t=512))) for _ in range(2)]
        Bg = [aB.take(512, ("p (c t) -> p c t", dict(t=128))) for _ in range(2)]
        xsd = [[aB.take(512) for _ in range(4)] for _ in range(2)]
        dt_prep("O", om, dt, dtA, prep, w12, hcur, hname)

        def load_group(g_):
            return (load_w(w_in, 0, 8, C_GRP + g_ * 1280, 512), load_w(w_in, 0, 8, C_GRP + g_ * 1280 + 512, 128))
        pre = [load_group(0)]

        def head(g):
            gp = g % 2
            (sx, sxk), (sbc, sbck) = pre[0]

            def st1(ct):
                if ct < 4:
                    sl, slk, c0 = sx, sxk, ct * 128
                else:
                    sl, slk, c0 = sbc, sbck, 0
                gct = g * 6 + ct
                ps, pk = bank()
                for k in range(8):
                    B.mm(ps[:, :], sl[:, k, c0:c0 + 128], hcur[:, k, :], k == 0, k == 7, r=[slk] + hk_all, w=[pk])
                xp_ = xpad[ct % 2]
                xk = ("xpad", ct % 2)
                B.cp("act", xp_[:, 2:L + 2], ps[:, :], r=[pk], w=[xk])
                ps2, pk2 = bank()
                for k in range(8):
                    B.mm(ps2[:, 0:4], sl[:, k, c0:c0 + 128], hTh[:, k, om * 4:om * 4 + 4], k == 0, k == 7,
                         r=[slk, ("hTh", 0)], w=[pk2])
                B.cp("act", xp_[:, 0:2], ps2[:, 0:2], r=[pk2], w=[xk])
                B.cp("act", xp_[:, 514:516], ps2[:, 2:4], r=[pk2], w=[xk])
                dg_ = dg[ct % 2]
                dgk = ("dg", ct % 2)
                for kk in range(5):
                    B.ts("dve", dg_[:, kk, :], ident[:, :], convw[:, gct, kk:kk + 1], None, ALU.mult, None,
                         r=[("ident", 0), ("convw", 0)], w=[dgk])

            def st2(ct):
                gct = g * 6 + ct
                xp_ = xpad[ct % 2]
                xk = ("xpad", ct % 2)
                dg_ = dg[ct % 2]
                dgk = ("dg", ct % 2)
                psc, pck = bank()
                for kk in range(5):
                    B.mm(psc[:, :], dg_[:, kk, :], xp_[:, kk:kk + L], kk == 0, kk == 4, r=[dgk, xk], w=[pck])
                if ct < 4:
                    dst_, dstk = xgT[gp][:, ct, :], ("xgT", gp, ct)
                else:
                    dst_, dstk = BT[gp][:, :], ("BT", gp)
                B.act(dst_, psc[:, :], AF.Silu, r=[pck, ("convb", 0)], w=[dstk], bias=convb[:, gct:gct + 1])

            st1(0)
            for ct in range(5):
                if ct + 1 < 5:
                    st1(ct + 1)
                st2(ct)
            if g + 1 < NG:
                pre[0] = load_group(g + 1)
            for t in range(4):
                tb, tk = tbank()
                tbv = tb[:].rearrange("p (k c) -> p k c", c=128)
                for ct in range(4):
                    B.tr(tbv[:, ct, :], xgT[gp][:, ct, t * 128:(t + 1) * 128], ident[:],
                         r=[("xgT", gp, ct), ("ident", 0)], w=[tk])
                B.tr(tbv[:, 4, :], BT[gp][:, t * 128:(t + 1) * 128], ident[:], r=[("BT", gp), ("ident", 0)], w=[tk])
                B.cp("act", xg[gp][:, t, :], tb[:, 0:512], r=[tk], w=[("xg", gp, t)])
                B.cp("act", Bg[gp][:, t, :], tb[:, 512:640], r=[tk], w=[("Bg", gp, t)])

        def tail(g):
            gp = g % 2
            for d in range(2):
                ps, pk = bank()
                for t in range(4):
                    wv = w12[:, t, 1, d * 32 + g * 8: d * 32 + g * 8 + 8]
                    eng = "pool" if (t + d) % 2 == 0 else "dve"
                    B.tt(eng, xsd[d][t][:, :].rearrange(R3[0], **R3[1]), xg[gp][:, t, :].rearrange(R3[0], **R3[1]),
                         bc_last(wv, 64), ALU.mult, r=[("xg", gp, t), ("w12", t)], w=[("xs", d, t)])
                    B.mm(ps[:, :], Bg[gp][:, t, :], xsd[d][t][:, :], t == 0, t == 3,
                         r=[("Bg", gp, t), ("xs", d, t)], w=[pk])
                B.cp("act", Eb[d][:, :], ps[:, :], r=[pk], w=[("Hf" if d == 0 else "Hb", 0)])
                B.dma(Escr[om, d, g, :, :], Eb[d][:, :], r=[("Hf" if d == 0 else "Hb", 0)], w=[("Escr", om, d, g)])

        head(0)
        for g in range(NG):
            if g + 1 < NG:
                head(g + 1)
            if g == NG - 2:
                next_norm()
            tail(g)

    def prompt_block(bi):
        seqs = [2 * bi, 2 * bi + 1]
        runs = [(0, 1), (2, 3)]
        phase()
        gtmp = aF.take(D)
        UT = aB.take(4096, ("p (k t) -> p k t", dict(t=512)))
        T12 = [aB.take(1024, ("p (k t) -> p k t", dict(t=512))) for _ in range(2)]
        for t in range(4):
            B.dma(xres[:, t, :], xp[seqs[t // 2], (t % 2) * 128:(t % 2 + 1) * 128, :], r=[], w=[("xres", t)])
        for t in range(4):
            norm_mod_T(t, 0, 1, hT, "hT", gtmp)
        for half in range(2):
            s, sk = load_w(w_in, 0, 8, C_UF + half * 512, 512)
            for ct in range(4):
                ps, pk = bank()
                for k in range(8):
                    B.mm(ps[:, :], s[:, k, ct * 128:(ct + 1) * 128], hT[:, k, :], k == 0, k == 7,
                         r=[sk] + hT_all, w=[pk])
                B.cp("act", UT[:, half * 4 + ct, :], ps[:, :], r=[pk], w=[("UT", half * 4 + ct)])
        it = 0
        for q in range(2):
            for g in range(NG):
                tb_ = T12[it % 2]
                tk_ = ("T12", it % 2)
                it += 1
                for pt in range(2):
                    ps, pk = bank()
                    for cs in range(2):
                        for ci in range(2):
                            B.mm(ps[:, cs * 256:(cs + 1) * 256],
                                 UT[:, 2 * g + ci, q * 256 + pt * 128: q * 256 + (pt + 1) * 128],
                                 dft[:, cs, ci, :], ci == 0, ci == 1,
                                 r=[("UT", 2 * g + ci)] + DFT_ALL, w=[pk])
                    B.cp("dve", tb_[:, pt, :], ps[:, :], r=[pk], w=[tk_])
                ps, pk = bank()
                for c2 in range(2):
                    n = 0
                    for pt in range(2):
                        for cs in range(2):
                            B.mm(ps[:, c2 * 256:(c2 + 1) * 256],
                                 tb_[:, pt, cs * 256 + c2 * 128: cs * 256 + (c2 + 1) * 128],
                                 dft[:, 0 if cs == 0 else 2, pt, :], n == 0, n == 3,
                                 r=[tk_] + DFT_ALL, w=[pk])
                            n += 1
                B.cp("act", YT[:, 2 * g:2 * g + 2, q * 256:(q + 1) * 256],
                     ps[:, :].rearrange("p (a b) -> p a b", b=256), r=[pk],
                     w=[("YT", 2 * g), ("YT", 2 * g + 1)])
        if bi == 0:
            hooks = {"merge_start": lambda tt_: ([mod_piece(c_) for c_ in range(4, 10)], mod_part(0, (2,), tt_)),
                     "mlp_start": lambda tt_: (mod_piece(10), mod_piece(11), mod_part(0, (3, 4), tt_)),
                     "after_ff1": lambda tt_: mod_part(0, (5,), tt_)}
        elif bi == n_pblocks - 1 and sample:
            hooks = {"merge_start": lambda tt_: mod_part(1, (0, 1), tt_),
                     "mlp_start": lambda tt_: mod_part(1, (2,), tt_),
                     "after_norm2": lambda tt_: mod_part(1, (3, 4), tt_)}
        else:
            hooks = {}
        block_body("P", runs,
                   out_rows=lambda t: yp[seqs[t // 2], (t % 2) * 128:(t % 2 + 1) * 128, :],
                   state_out=seqs, hooks=hooks)

    def norm_tile(src, skey, col, dst, dkey, gtmp, mask=None):
        rms_stats(src, col, r=[skey])
        if mask is not None:
            B.tt("dve", st_rs[:, col:col + 1], st_rs[:, col:col + 1], mask, ALU.mult,
                 r=[("st_rs", col), ("hmask", 0)], w=[("st_rs", col)])
        B.stt(gtmp[:, :], src, st_rs[:, col:col + 1], modb[:, 1, :], ALU.mult, ALU.mult,
              r=[skey, ("st_rs", col), ("modb", 1)], w=[("gtmp", 0)])
        if mask is not None:
            B.stt(dst, modb[:, 0, :], mask, gtmp[:, :], ALU.mult, ALU.add,
                  r=[("gtmp", 0), ("modb", 0), ("hmask", 0)], w=[dkey])
        else:
            B.tt("dve", dst, gtmp[:, :], modb[:, 0, :], ALU.add, r=[("gtmp", 0), ("modb", 0)], w=[dkey])

    def sample_block():
        phase()
        mod_part(1, (5,))
        phase()
        gtmp = aF.take(D)
        HmT = [aF.take(2048).bitcast(BF16).rearrange("p (k t) -> p k t", t=512) for _ in range(2)]
        UcT = aF.take(2048).bitcast(BF16).rearrange("p (k t) -> p k t", t=512)
        UsT = yzT[:, 0:8, :]
        hTM = aB.take(16384, ("p (t c) -> p t c", dict(c=D)))
        B.dma(xres[:, 0, :], xhalo_d[:, :], r=[], w=[("xres", 0)])
        norm_tile(xres[:, 0, :], ("xres", 0), 8, htm[0][:], ("htm", 0), gtmp, mask=hmask[:, 0:1])
        tb, tk = tbank()
        tbv = tb[:].rearrange("p (k c) -> p k c", c=128)
        for k in range(8):
            B.tr(tbv[:, k, :], htm[0][:, k * 128:(k + 1) * 128], ident[:], r=[("htm", 0), ("ident", 0)], w=[tk])
        B.cp("act", hTh[:, :, :], tbv, r=[tk], w=[("hTh", 0)])
        for i in range(16):
            xt = i % 4
            B.dma(xres[:, xt, :], xs_all[i, :, :], r=[], w=[("xres", xt)])
            norm_tile(xres[:, xt, :], ("xres", xt), xt, hTM[:, i, :], ("hTM", i), gtmp)
        hTM_all = [("hTM", i) for i in range(16)]
        for m in range(2):
            sl_ = [load_w(dftp_d[m], kh * 1024, 8, 0, 512) for kh in range(2)]
            for dtl in range(8):
                ps, pk = bank()
                for pt in range(16):
                    sw, swk = sl_[pt // 8]
                    B.mm(ps[:, :], hTM[:, pt, dtl * 128:(dtl + 1) * 128], sw[:, pt % 8, :], pt == 0, pt == 15,
                         r=[swk] + hTM_all, w=[pk])
                B.cp("act", HmT[m][:, dtl, :], ps[:, :], r=[pk], w=[("HmT", m, dtl)])
        for half in range(2):
            s, sk = load_w(w_in, 0, 8, C_UF + half * 512, 512)
            for m in range(2):
                dstU = UcT if m == 0 else UsT
                for ct in range(4):
                    ps, pk = bank()
                    for k in range(8):
                        B.mm(ps[:, :], s[:, k, ct * 128:(ct + 1) * 128], HmT[m][:, k, :], k == 0, k == 7,
                             r=[sk] + [("HmT", m, kk) for kk in range(8)], w=[pk])
                    B.cp("act", dstU[:, half * 4 + ct, :], ps[:, :], r=[pk],
                         w=[("UmT", m, half * 4 + ct), ("yzT", 0, 0)] if m == 1 else [("UmT", m, half * 4 + ct)])
        for g in range(NG):
            for c2 in range(2):
                ps, pk = bank()
                n = 0
                for m in range(2):
                    srcU = UcT if m == 0 else UsT
                    for ci in range(2):
                        B.mm(ps[:, :], dft[:, 0 if m == 0 else 2, ci, c2 * 128:(c2 + 1) * 128],
                             srcU[:, 2 * g + ci, :], n == 0, n == 3,
                             r=DFT_ALL + [("UmT", m, 2 * g + ci)], w=[pk])
                        n += 1
                B.cp("act", YT[:, 2 * g + c2, :], ps[:, :], r=[pk], w=[("YT", 2 * g + c2)])
        hTalt = arenaB[:, NAB - 4096:NAB].rearrange("p (k t) -> p k t", t=512)
        hbufs = [(hTalt, "hTalt"), (hT, "hT")]

        def norm_other(om, dst, dname):
            gtmp2 = aF.take(D)
            xst = [aF.take(D) for _ in range(2)]
            for t in range(4):
                i = om * 4 + t
                B.dma(xst[t % 2][:, :], xs_all[i, :, :], r=[], w=[("xst", t % 2)])
                norm_tile(xst[t % 2][:, :], ("xst", t % 2), t, htm[t % 2][:], ("htm", t % 2), gtmp2)
                tile_T(htm[t % 2], ("htm", t % 2), dst, dname, t)

        def norm_own():
            gtmp3 = aF.take(D)
            for t in range(4):
                norm_mod_T(t, 0, 1, hT, "hT", gtmp3)

        phase()
        norm_other(0, *hbufs[0])
        for om in range(3):
            hc, hn = hbufs[om % 2]
            if om < 2:
                nxt = (lambda om_=om: norm_other(om_ + 1, *hbufs[(om_ + 1) % 2]))
            else:
                nxt = norm_own
            other_block(om, hc, hn, nxt)
        block_body("S", [(0, 1, 2, 3)], out_rows=lambda t: ys[t * 128:(t + 1) * 128, :], state_out=None, bs=3)

    for cb_ in range(4):
        mod_piece(cb_)
    phase()
    mod_part(0, (0, 1))
    for bi in range(n_pblocks):
        prompt_block(bi)
    if sample:
        sample_block()

    B.sbuf_left = nc.sbuf_bytes_remaining
    B.P.emit()
    B.stack.close()
    return nc, B


def _consts():
    k = np.arange(128)
    tri = (k[:, None] <= k[None, :]).astype(np.float32)
    triL = (k[:, None] >= k[None, :]).astype(np.float32)
    SL = (k[:, None] > k[None, :]).astype(np.float32)
    SU = (k[:, None] < k[None, :]).astype(np.float32)
    ones = np.ones((128, 128), np.float32)
    tris = np.stack([tri, triL, SL, SU, ones], axis=1)
    n = np.arange(256)
    ang = 2.0 * np.pi * np.outer(n, n) / 256.0
    C = (np.cos(ang) / 16.0).astype(np.float32)
    S = (np.sin(ang) / 16.0).astype(np.float32)
    dft = np.stack([C, S, -S], axis=0)
    return tris, dft, np.eye(128, dtype=np.float32)


_CACHE = {}


def kernel(x_prompt, x_sample, state_ssm_fwd, state_ssm_bwd, c, c_ctx, w_mod, b_mod, norm1_g,
           w_in, w_fourier, conv_w, conv_b, dt_bias, A_log, D_skip, ssd_norm_g, w_ssd_out,
           w_out, norm2_g, w_ff1, w_ff2, final_norm_g):
    f = lambda a: np.ascontiguousarray(np.asarray(a, dtype=np.float32))
    x_prompt = f(x_prompt); x_sample = f(x_sample)
    w_in0 = f(w_in)[0]
    cols = [np.arange(0, 1024)]
    for g in range(4):
        cols.append(np.arange(3072 + g * 512, 3072 + (g + 1) * 512))
        cols.append(np.arange(5120 + g * 128, 5120 + (g + 1) * 128))
        cols.append(np.arange(5632 + g * 128, 5632 + (g + 1) * 128))
        cols.append(np.arange(1024 + g * 512, 1024 + (g + 1) * 512))
    cols.append(np.arange(6144, 6208))
    cols.append(np.arange(6208, 8256))
    cols = np.concatenate(cols)
    w_in_r = np.ascontiguousarray(w_in0[:, cols])
    cw = f(conv_w)[0]; cbv = f(conv_b)[0]
    ch = []
    for g in range(4):
        ch.append(np.arange(g * 512, (g + 1) * 512))
        ch.append(np.arange(2048 + g * 128, 2048 + (g + 1) * 128))
        ch.append(np.arange(2560 + g * 128, 2560 + (g + 1) * 128))
    ch = np.concatenate(ch)
    convw = np.ascontiguousarray(cw[:, ch].T.reshape(24, 128, 5).transpose(1, 0, 2))
    convb = np.ascontiguousarray(cbv[ch].reshape(24, 128).T)
    rep = lambda v: np.ascontiguousarray(np.broadcast_to(np.asarray(v, np.float32).reshape(1, -1), (128, np.asarray(v).size)))
    tris, dft, ident = _consts()
    common = {
        "w_mod": f(w_mod)[0], "b_mod2": np.ascontiguousarray(np.broadcast_to(f(b_mod)[0][None], (2, 6144))),
        "g1bc": rep(f(norm1_g)[0]), "g2bc": rep(f(norm2_g)[0]), "gFbc": rep(f(final_norm_g)),
        "w_in_r": w_in_r, "w_fourier": f(w_fourier)[0], "w_ssd_out": f(w_ssd_out)[0], "w_out": f(w_out)[0],
        "w_ff1": f(w_ff1)[0], "w_ff2": f(w_ff2)[0], "convw": convw, "convb": convb,
        "dtb": rep(f(dt_bias)[0].reshape(-1)), "alog": rep(f(A_log)[0].reshape(-1)), "dskip": rep(f(D_skip)[0]),
        "ssdg": np.ascontiguousarray(f(ssd_norm_g)[0].reshape(16, 128).T),
        "ident": ident, "tris": tris, "dft256": dft,
    }
    cc = f(c); cctx = f(c_ctx)
    sf = f(state_ssm_fwd); sb_ = f(state_ssm_bwd)
    in_maps = []
    for core in range(8):
        s = core // 4
        cv = np.stack([cctx, cc[s]], axis=-1)
        cv = np.ascontiguousarray(cv.reshape(8, 128, 2).transpose(1, 0, 2))
        m = dict(common)
        m["xp"] = np.ascontiguousarray(x_prompt[4 * core:4 * core + 4])
        m["cvec"] = cv
        j = core % 4
        others = [mm for mm in range(4) if mm != j]
        order = others + [j]
        xs = x_sample[s]
        m["xs_all"] = np.ascontiguousarray(np.concatenate([xs[512 * mm:512 * (mm + 1)] for mm in order], axis=0).reshape(16, 128, 1024))
        xh = np.zeros((128, 1024), np.float32); hm = np.zeros((128, 1), np.float32)
        for bs, mm in enumerate(order):
            if mm > 0:
                xh[4 * bs:4 * bs + 2] = xs[512 * mm - 2:512 * mm]; hm[4 * bs:4 * bs + 2] = 1.0
            if mm < 3:
                xh[4 * bs + 2:4 * bs + 4] = xs[512 * mm + 512:512 * mm + 514]; hm[4 * bs + 2:4 * bs + 4] = 1.0
        m["xhalo"] = xh; m["hmask"] = hm
        om_ = np.zeros((128, 6), np.float32)
        for o_, mm in enumerate(others):
            om_[:, 2 * o_] = 1.0 if mm < j else 0.0
            om_[:, 2 * o_ + 1] = 1.0 if mm > j else 0.0
        m["omask"] = om_
        pos = np.concatenate([np.arange(512 * mm, 512 * (mm + 1)) for mm in order])
        posq = np.arange(512 * j, 512 * (j + 1))
        ang = 2.0 * np.pi * (np.outer(pos // 64, posq // 64) / 32.0 + np.outer(pos % 64, posq % 64) / 64.0)
        nrm = 1.0 / np.sqrt(2048.0)
        m["dftp"] = np.stack([np.cos(ang) * nrm, np.sin(ang) * nrm], axis=0).astype(np.float32)
        m["h0"] = np.ascontiguousarray(np.stack([sf[s, 0].transpose(2, 0, 1).reshape(128, 2048),
                                                 sb_[s, 0].transpose(2, 0, 1).reshape(128, 2048)], axis=0))
        in_maps.append(m)
    if "nc" not in _CACHE:
        _CACHE["nc"] = build_program()[0]
    res = run_bass_kernel_spmd(_CACHE["nc"], in_maps, core_ids=list(range(8)))
    y_prompt = np.concatenate([r["yp"] for r in res.results], axis=0)
    y_sample = np.zeros_like(x_sample)
    for core in range(8):
        s, j = core // 4, core % 4
        y_sample[s, 512 * j:512 * (j + 1)] = res.results[core]["ys"]
    def states(key):
        a = np.concatenate([r[key] for r in res.results], axis=0)
        return np.ascontiguousarray(a.reshape(32, 128, 32, 64).transpose(0, 2, 3, 1))[:, None]
    return (y_prompt.astype(np.float32), y_sample.astype(np.float32), states("hf").astype(np.float32),
            states("hb").astype(np.float32))
```

```python
import contextlib
import numpy as np
import concourse.bass as bass
import concourse.mybir as mybir
from concourse.bass_utils import run_bass_kernel_spmd

F32 = mybir.dt.float32
BF16 = mybir.dt.bfloat16
F32R = mybir.dt.float32r
AF = mybir.ActivationFunctionType
ALU = mybir.AluOpType

ENGS = ["pe", "act", "dve", "pool", "sp"]
EPS = 1e-6
BUILD_SAMPLE = True


class Prog:
    def __init__(self, nc, n_dma_sems=14):
        self.nc = nc
        self.ops = []
        self.n_dma_sems = n_dma_sems

    def op(self, eng, fn, r=(), w=(), dma=False):
        rr = set()
        ww = set()
        for k in r:
            rr.add(k)
            rr.add((k[0], "*"))
        for k in w:
            ww.add(k)
            rr.add((k[0], "*"))
        self.ops.append(dict(eng=eng, fn=fn, r=tuple(rr), w=tuple(ww), dma=dma))

    def barrier(self, eng, fn, names):
        self.ops.append(dict(eng=eng, fn=fn, r=(), w=tuple((n, "*") for n in names), dma=False))

    def emit(self):
        nc = self.nc
        ops = self.ops
        last_w = {}
        readers = {}
        for i, o in enumerate(ops):
            deps = set()
            for k in o["r"]:
                if k in last_w:
                    deps.add(last_w[k])
            for k in o["w"]:
                if k in last_w:
                    deps.add(last_w[k])
                for rr in readers.get(k, ()):
                    deps.add(rr)
            deps.discard(i)
            o["deps"] = deps
            for k in o["r"]:
                readers.setdefault(k, []).append(i)
            for k in o["w"]:
                last_w[k] = i
                readers[k] = []
        stack = contextlib.ExitStack()
        eng_sem = {e: stack.enter_context(nc.semaphore("s_" + e)) for e in ENGS}
        dma_rings = {}
        for e in ("sp", "act", "pool"):
            dma_rings[e] = [stack.enter_context(nc.semaphore("d_%s%d" % (e, j)))
                            for j in range(self.n_dma_sems)]
        ring_pos = {e: 0 for e in dma_rings}
        ring_cnt = {e: [0] * self.n_dma_sems for e in dma_rings}
        ring_last = {e: [None] * self.n_dma_sems for e in dma_rings}
        for i, o in enumerate(ops):
            if o["dma"]:
                e = o["eng"]
                j = ring_pos[e]
                ring_pos[e] = (j + 1) % self.n_dma_sems
                ring_cnt[e][j] += 16
                o["dsem"] = dma_rings[e][j]
                o["dval"] = ring_cnt[e][j]
                o["dprev"] = ring_last[e][j]
                ring_last[e][j] = i
        for i, o in enumerate(ops):
            cdeps = {}
            ddeps = set()
            for d in o["deps"]:
                od = ops[d]
                if od["dma"]:
                    ddeps.add(d)
                else:
                    e = od["eng"]
                    if e == "pe" and o["eng"] == "pe":
                        continue
                    if e not in cdeps or cdeps[e] < d:
                        cdeps[e] = d
            if o["dma"] and o["dprev"] is not None:
                ddeps.add(o["dprev"])
            o["cdeps"] = cdeps
            o["ddeps"] = ddeps
        signal = set()
        for o in ops:
            for e, d in o["cdeps"].items():
                signal.add(d)
        cnt = {e: 0 for e in ENGS}
        for i, o in enumerate(ops):
            if not o["dma"] and i in signal:
                cnt[o["eng"]] += 1
                o["sig"] = cnt[o["eng"]]
        per_eng = {e: [i for i, o in enumerate(ops) if o["eng"] == e] for e in ENGS}
        self.stats = {e: len(per_eng[e]) for e in ENGS}

        def run_engine(ename, eobj):
            known = {e: 0 for e in ENGS}
            dknown = set()
            for i in per_eng[ename]:
                o = ops[i]
                for e, d in o["cdeps"].items():
                    v = ops[d]["sig"]
                    if known[e] < v:
                        eobj.wait_ge(eng_sem[e], v)
                        known[e] = v
                for d in sorted(o["ddeps"]):
                    if d not in dknown:
                        eobj.wait_ge(ops[d]["dsem"], ops[d]["dval"])
                        dknown.add(d)
                ins = o["fn"](eobj)
                if o["dma"]:
                    ins.then_inc(o["dsem"], 16)
                elif "sig" in o:
                    ins.then_inc(eng_sem[ename], 1)
            if ename in dma_rings:
                for j, s in enumerate(dma_rings[ename]):
                    if ring_cnt[ename][j] > 0:
                        eobj.wait_ge(s, ring_cnt[ename][j])

        with nc.Block() as block:
            @block.tensor
            def _(e):
                run_engine("pe", e)

            @block.scalar
            def _(e):
                run_engine("act", e)

            @block.vector
            def _(e):
                run_engine("dve", e)

            @block.gpsimd
            def _(e):
                run_engine("pool", e)

            @block.sync
            def _(e):
                run_engine("sp", e)
        stack.close()


def bc_last(ap2d, n):
    p, a = ap2d.shape
    return ap2d.unsqueeze(2).to_broadcast([p, a, n])


def bc_mid(ap2d, n):
    p, b = ap2d.shape
    return ap2d.unsqueeze(1).to_broadcast([p, n, b])


class Builder:
    def __init__(self, nc):
        self.nc = nc
        self.P = Prog(nc)
        self.stack = contextlib.ExitStack()
        self.mm_rr = 0
        self.tp_rr = 0
        self.w_rr = 0
        self.dmaq = 0

    def sb(self, name, shape, dt):
        return self.stack.enter_context(self.nc.sbuf_tensor("sb_" + name, list(shape), dt))

    def pst(self, name, shape, dt):
        return self.stack.enter_context(self.nc.psum_tensor("ps_" + name, list(shape), dt))

    def dram_in(self, name, shape):
        return self.nc.dram_tensor(name, list(shape), F32, kind="ExternalInput").ap()

    def dram_out(self, name, shape):
        return self.nc.dram_tensor(name, list(shape), F32, kind="ExternalOutput").ap()

    def mm(self, out, lhsT, rhs, start, stop, r, w, skip=False):
        self.P.op("pe", lambda e: e.matmul(out, lhsT=lhsT, rhs=rhs, start=start, stop=stop,
                                           skip_group_check=skip), r=r, w=w)

    def tr(self, out, in_, ident, r, w):
        self.P.op("pe", lambda e: e.transpose(out=out, in_=in_, identity=ident), r=r, w=w)

    def act(self, out, in_, func, r, w, bias=None, scale=None, accum=None, eng="act"):
        kw = {}
        if bias is not None:
            kw["bias"] = bias
        if scale is not None:
            kw["scale"] = scale
        if accum is not None:
            kw["accum_out"] = accum
        self.P.op("act", lambda e: e.activation(out=out, in_=in_, func=func, **kw), r=r, w=w)

    def tt(self, eng, out, in0, in1, op, r, w):
        self.P.op(eng, lambda e: e.tensor_tensor(out=out, in0=in0, in1=in1, op=op), r=r, w=w)

    def ts(self, eng, out, in0, s1, s2, op0, op1, r, w):
        if op1 is None:
            self.P.op(eng, lambda e: e.tensor_scalar(out=out, in0=in0, scalar1=s1, scalar2=None, op0=op0),
                      r=r, w=w)
        else:
            self.P.op(eng, lambda e: e.tensor_scalar(out=out, in0=in0, scalar1=s1, scalar2=s2,
                                                     op0=op0, op1=op1), r=r, w=w)

    def stt(self, out, in0, scalar, in1, op0, op1, r, w):
        self.P.op("dve", lambda e: e.scalar_tensor_tensor(out=out, in0=in0, scalar=scalar, in1=in1,
                                                          op0=op0, op1=op1), r=r, w=w)

    def cp(self, eng, out, in_, r, w):
        if eng == "act":
            self.P.op("act", lambda e: e.copy(out=out, in_=in_), r=r, w=w)
        else:
            self.P.op(eng, lambda e: e.tensor_copy(out=out, in_=in_), r=r, w=w)

    def memset(self, eng, ap, val, w):
        self.P.op(eng, lambda e: e.memset(ap, val), w=w)

    def recip(self, out, in_, r, w):
        self.P.op("dve", lambda e: e.reciprocal(out=out, in_=in_), r=r, w=w)

    def dma(self, out, in_, r, w, q="sp"):
        self.P.op(q, lambda e: e.dma_start(out=out, in_=in_), r=r, w=w, dma=True)


D = 1024
NG = 4
DIN = 2048
DFF = 4096
NW = 8256
C_UF = 0
C_GRP = 1024
C_DT = 1024 + 4 * 1280
C_GATE = C_DT + 64


def build_program(n_pblocks=2, sample=True):
    nc = bass.Bass("TRN2", target_bir_lowering=False)
    B = Builder(nc)

    xp = B.dram_in("xp", [4, 256, D])
    cvec = B.dram_in("cvec", [128, 8, 2])
    w_mod = B.dram_in("w_mod", [D, 6 * D])
    b_mod2 = B.dram_in("b_mod2", [2, 6 * D])
    g1bc_d = B.dram_in("g1bc", [128, D])
    g2bc_d = B.dram_in("g2bc", [128, D])
    gFbc_d = B.dram_in("gFbc", [128, D])
    w_in = B.dram_in("w_in_r", [D, NW])
    w_fourier = B.dram_in("w_fourier", [D, D])
    w_ssd_out = B.dram_in("w_ssd_out", [DIN, D])
    w_out = B.dram_in("w_out", [D, D])
    w_ff1 = B.dram_in("w_ff1", [D, DFF])
    w_ff2 = B.dram_in("w_ff2", [DFF, D])
    convw_d = B.dram_in("convw", [128, 24, 5])
    convb_d = B.dram_in("convb", [128, 24])
    dtb_d = B.dram_in("dtb", [128, 64])
    alog_d = B.dram_in("alog", [128, 64])
    dskip_d = B.dram_in("dskip", [128, 32])
    ssdg_d = B.dram_in("ssdg", [128, 16])
    ident_d = B.dram_in("ident", [128, 128])
    tris_d = B.dram_in("tris", [128, 5, 128])
    dft_d = B.dram_in("dft256", [3, 256, 256])
    modscr = nc.dram_tensor("modscr", [2, 6 * D], F32, kind="Internal").ap()
    xs_all = B.dram_in("xs_all", [16, 128, D])
    xhalo_d = B.dram_in("xhalo", [128, D])
    hmask_d = B.dram_in("hmask", [128, 1])
    omask_d = B.dram_in("omask", [128, 6])
    dftp_d = B.dram_in("dftp", [2, 2048, 512])
    h0_d = B.dram_in("h0", [2, 128, DIN])
    Escr = nc.dram_tensor("Escr", [3, 2, 4, 128, 512], F32, kind="Internal").ap()

    yp = B.dram_out("yp", [4, 256, D])
    hf_o = B.dram_out("hf", [4, 128, DIN])
    hb_o = B.dram_out("hb", [4, 128, DIN])
    ys = B.dram_out("ys", [512, D])

    ident = B.sb("ident", [128, 128], BF16)
    tris = B.sb("tris", [128, 5, 128], F32)
    trisr = B.sb("trisr", [128, 3, 128], F32R)
    lndt = B.sb("lndt", [128, 4, 64], F32R)
    Rbr = [B.sb("Rbr%d" % i, [128, 1024], F32R) for i in range(2)]
    masks = B.sb("masks", [128, 2, 128], BF16)
    dft = B.sb("dft", [128, 3, 2, 256], BF16)
    convw = B.sb("convw", [128, 24, 5], F32)
    convb = B.sb("convb", [128, 24], F32)
    dtb = B.sb("dtb", [128, 64], F32)
    Abc = B.sb("Abc", [128, 64], F32)
    Dbc = B.sb("Dbc", [128, 32], F32)
    ssdg = B.sb("ssdg", [128, 16], F32)
    cv = B.sb("cv", [128, 16], F32)
    scb = B.sb("scb", [128, 16], BF16)
    modb = B.sb("modb", [128, 4, D], BF16)
    modg = B.sb("modg", [128, 2, D], F32)
    dummy = B.sb("dummy", [128, 2], F32)
    hTh = B.sb("hTh", [128, 8, 128], BF16)
    hmask = B.sb("hmask", [128, 1], F32)
    omask = B.sb("omask", [128, 6], F32)
    Dst = B.sb("Dst", [128, 3, 64], F32)
    suft = B.sb("suft", [128, 64], F32)

    NSLOT = 4
    wsl = [B.sb("wsl%d" % i, [128, 8, 512], BF16) for i in range(NSLOT)]

    xres = B.sb("xres", [128, 4, D], F32)
    htm = [B.sb("htm%d" % i, [128, D], BF16) for i in range(2)]
    junk = B.sb("junk", [128, D], BF16)
    st_ss = B.sb("st_ss", [128, 12], F32)
    st_rs = B.sb("st_rs", [128, 12], F32)
    ssq = B.sb("ssq", [128, 4, 4], F32)
    ry = B.sb("ry", [128, 4], F32)
    hT = B.sb("hT", [128, 8, 512], BF16)
    YT = B.sb("YT", [128, 8, 512], BF16)
    yzT = B.sb("yzT", [128, 16, 512], BF16)
    NAF = 7168
    NAB = 27936
    arenaF = B.sb("arenaF", [128, NAF], F32)
    arenaB = B.sb("arenaB", [128, NAB], BF16)
    ARENA_NAMES = ["gtmp", "UT", "T12", "xpad", "cacc", "xgT", "BT", "CT", "xg", "Bg", "sz", "dt", "dtA", "prep",
                   "w12", "xs", "Rb", "Lt", "Gt", "cbm", "y1", "y2", "y3", "yzb", "Hf", "Hb", "Htmp", "Hfb", "Hbe",
                   "gfs", "Macc", "Mb", "MT", "aT", "rl", "otile", "gFbc", "hTM", "HmT", "UmT", "xst", "dg", "yd", "hTalt"]

    class Ar:
        def __init__(self, t, n):
            self.t, self.n, self.off = t, n, 0

        def take(self, n, inner=None):
            assert self.off + n <= self.n, (self.off, n, self.n)
            ap = self.t[:, self.off:self.off + n]
            self.off += n
            if inner is not None:
                ap = ap.rearrange(inner[0], **inner[1])
            return ap
    aF = Ar(arenaF, NAF)
    aB = Ar(arenaB, NAB)

    def phase():
        aF.off = 0
        aB.off = 0
        B.P.barrier("dve", lambda e: e.memset(dummy[:, :], 0.0), ARENA_NAMES)

    psb = [B.pst("psb%d" % i, [128, 512], F32) for i in range(4)]
    psS = B.pst("psS", [128, 1024], F32)
    pstb = [B.pst("pstb%d" % i, [128, 1024], BF16) for i in range(2)]

    def bank():
        i = B.mm_rr
        B.mm_rr = (i + 1) % 6
        if i < 4:
            return psb[i], ("psb", i)
        return psS[:, (i - 4) * 512:(i - 3) * 512], ("psS", i - 4)

    def tbank():
        i = B.tp_rr
        B.tp_rr = (i + 1) % 2
        return pstb[i], ("pstb", i)

    def load_w(src, r0, kt, c0, ncols):
        i = B.w_rr
        B.w_rr = (i + 1) % NSLOT
        s = wsl[i]
        srcap = src[r0:r0 + kt * 128, c0:c0 + ncols].rearrange("(kt p) n -> p kt n", p=128)
        B.dma(s[:, 0:kt, 0:ncols], srcap, r=[], w=[("wsl", i)], q="pool")
        return s, ("wsl", i)

    B.dma(ident[:], ident_d[:, :], r=[], w=[("ident", 0)], q="pool")
    B.dma(tris[:], tris_d[:, :, :], r=[], w=[("tris", 0)])
    B.dma(masks[:], tris_d[:, 0:2, :], r=[], w=[("masks", 0)], q="pool")
    B.cp("dve", trisr[:, 0:2, :], tris[:, 2:4, :], r=[("tris", 0)], w=[("trisr", 0)])
    B.dma(arenaF[:, 2048:2176], ident_d[:, :], r=[], w=[("gtmp", 9)])
    B.cp("dve", trisr[:, 2, :], arenaF[:, 2048:2176], r=[("gtmp", 9)], w=[("trisr", 0)])
    for m in range(3):
        B.dma(dft[:, m, :, :], dft_d[m].rearrange("(kt p) n -> p kt n", p=128), r=[], w=[("dft", m)], q="pool")
    B.dma(convw[:], convw_d[:, :, :], r=[], w=[("convw", 0)])
    B.dma(convb[:], convb_d[:, :], r=[], w=[("convb", 0)])
    B.dma(dtb[:], dtb_d[:, :], r=[], w=[("dtb", 0)])
    B.dma(Abc[:], alog_d[:, :], r=[], w=[("Abc", 0)])
    B.dma(Dbc[:], dskip_d[:, :], r=[], w=[("Dbc", 0)])
    B.dma(ssdg[:], ssdg_d[:, :], r=[], w=[("ssdg", 0)])
    B.dma(cv[:], cvec.rearrange("p k r -> p (k r)"), r=[], w=[("cv", 0)])
    B.dma(hmask[:], hmask_d[:, :], r=[], w=[("hmask", 0)])
    B.dma(omask[:], omask_d[:, :], r=[], w=[("omask", 0)])
    B.act(Abc[:], Abc[:], AF.Exp, r=[("Abc", 0)], w=[("Abc", 0)])
    B.ts("dve", Abc[:], Abc[:], -1.0, None, ALU.mult, None, r=[("Abc", 0)], w=[("Abc", 0)])
    B.act(scb[:], cv[:], AF.Silu, r=[("cv", 0)], w=[("scb", 0)])
    scv = scb[:].rearrange("p (k r) -> p k r", r=2)
    DFT_ALL = [("dft", 0), ("dft", 1), ("dft", 2)]

    modp = htm[0][:, :].bitcast(F32)[0:2, 0:512]
    bmod = htm[1][:, :].bitcast(F32)[0:2, 0:512]

    def mod_piece(cb_):
        B.dma(bmod, b_mod2[:, cb_ * 512:(cb_ + 1) * 512], r=[], w=[("htm", 1)])
        s, sk = load_w(w_mod, 0, 8, cb_ * 512, 512)
        ps, pk = bank()
        for k in range(8):
            B.mm(ps[0:2, :], scv[:, k, :], s[:, k, :], k == 0, k == 7, r=[sk, ("scb", 0)], w=[pk])
        B.tt("dve", modp, ps[0:2, :], bmod, ALU.add, r=[pk, ("htm", 1)], w=[("htm", 0)])
        B.dma(modscr[:, cb_ * 512:(cb_ + 1) * 512], modp, r=[("htm", 0)], w=[("modscr", cb_)])

    def mod_part(row, vs, temps=None):
        if temps is None:
            temps = (aF.take(D), aF.take(D))
        t0, t1 = temps
        for v in vs:
            B.dma(t0[:, :], modscr[row:row + 1, v * D:(v + 1) * D].partition_broadcast(128),
                  r=[("modscr", 2 * v), ("modscr", 2 * v + 1)], w=[("gtmp", 7)])
            if v in (0, 3):
                mi = 0 if v == 0 else 2
                B.cp("dve", modb[:, mi, :], t0[:, :], r=[("gtmp", 7)], w=[("modb", mi)])
            elif v in (1, 4):
                mi = 1 if v == 1 else 3
                B.dma(t1[:, :], (g1bc_d if v == 1 else g2bc_d)[:, :], r=[], w=[("gtmp", 8)])
                B.stt(modb[:, mi, :], t0[:, :], 1.0, t1[:, :], ALU.add, ALU.mult,
                      r=[("gtmp", 7), ("gtmp", 8)], w=[("modb", mi)])
            else:
                B.cp("dve", modg[:, 0 if v == 2 else 1, :], t0[:, :], r=[("gtmp", 7)], w=[("modg", v)])

    def rms_stats(src, col, r, dim=D):
        B.act(junk[:, 0:src.shape[1]], src, AF.Square, r=r, w=[("junk", 0), ("st_ss", col)],
              accum=st_ss[:, col:col + 1])
        B.act(st_ss[:, col:col + 1], st_ss[:, col:col + 1], AF.Sqrt, r=[("st_ss", col)], w=[("st_ss", col)],
              bias=EPS, scale=1.0 / dim)
        B.recip(st_rs[:, col:col + 1], st_ss[:, col:col + 1], r=[("st_ss", col)], w=[("st_rs", col)])

    def tile_T(src_tm, skey, dstT, dname, t):
        tb, tk = tbank()
        tbv = tb[:].rearrange("p (k c) -> p k c", c=128)
        for k in range(8):
            B.tr(tbv[:, k, :], src_tm[:, k * 128:(k + 1) * 128], ident[:], r=[skey, ("ident", 0)], w=[tk])
        B.cp("act", dstT[:, :, t * 128:(t + 1) * 128], tbv, r=[tk], w=[(dname, t)])

    def norm_mod_T(t, vshift, vscale, dstT, dname, gtmp):
        rms_stats(xres[:, t, :], t, r=[("xres", t)])
        hb_ = htm[t % 2]
        hk = ("htm", t % 2)
        B.stt(gtmp[:, :], xres[:, t, :], st_rs[:, t:t + 1], modb[:, vscale, :], ALU.mult, ALU.mult,
              r=[("xres", t), ("st_rs", t), ("modb", vscale)], w=[("gtmp", 0)])
        B.tt("dve", hb_[:], gtmp[:, :], modb[:, vshift, :], ALU.add, r=[("gtmp", 0), ("modb", vshift)], w=[hk])
        tile_T(hb_, hk, dstT, dname, t)

    hT_all = [("hT", t) for t in range(4)]
    R3 = ("p (h q) -> p h q", dict(q=64))

    def dt_prep(kind, om, dt, dtA, prep, w12, hsrc=None, hname="hT"):
        if hsrc is None:
            hsrc = hT
        s, sk = load_w(w_in, 0, 8, C_DT, 64)
        for t in range(4):
            ps, pk = bank()
            for k in range(8):
                B.mm(ps[:, 0:64], hsrc[:, k, t * 128:(t + 1) * 128], s[:, k, 0:64], k == 0, k == 7,
                     r=[sk, (hname, t)], w=[pk])
            B.tt("dve", dt[:, t, :], ps[:, 0:64], dtb[:], ALU.add, r=[pk, ("dtb", 0)], w=[("dt", t)])
            B.act(dt[:, t, :], dt[:, t, :], AF.Exp, r=[("dt", t)], w=[("dt", t)])
            B.act(dt[:, t, :], dt[:, t, :], AF.Ln, r=[("dt", t)], w=[("dt", t)], bias=1.0)
            if kind == "O":
                for d in range(2):
                    B.ts("dve", dt[:, t, d * 32:(d + 1) * 32], dt[:, t, d * 32:(d + 1) * 32],
                         omask[:, om * 2 + d:om * 2 + d + 1], None, ALU.mult, None,
                         r=[("dt", t), ("omask", 0)], w=[("dt", t)])
            B.tt("dve", dtA[:, t, :], dt[:, t, :], Abc[:], ALU.mult, r=[("dt", t), ("Abc", 0)], w=[("dtA", t)])
            if kind != "O":
                B.act(lndt[:, t, :], dt[:, t, :], AF.Ln, r=[("dt", t)], w=[("lndt", t)], bias=1e-18)
            ps, pk = bank()
            pv = ps[:, 0:192].rearrange("p (a b) -> p a b", b=64)
            for d in range(2):
                B.mm(pv[:, 0, d * 32:(d + 1) * 32], tris[:, 2 + d, :], dtA[:, t, d * 32:(d + 1) * 32], True, True,
                     r=[("tris", 0), ("dtA", t)], w=[pk])
                B.mm(pv[:, 1, d * 32:(d + 1) * 32], tris[:, d, :], dtA[:, t, d * 32:(d + 1) * 32], True, True,
                     r=[("tris", 0), ("dtA", t)], w=[pk])
            B.mm(pv[:, 2, :], tris[:, 4, :], dtA[:, t, :], True, True, r=[("tris", 0), ("dtA", t)], w=[pk])
            B.act(prep[:, t, :, :], pv, AF.Exp, r=[pk], w=[("prep", t)])
            B.cp("dve", w12[:, t, 0, :], dt[:, t, :], r=[("dt", t)], w=[("w12", t)])
            B.tt("dve", w12[:, t, 1, :], dt[:, t, :], prep[:, t, 0, :], ALU.mult,
                 r=[("dt", t), ("prep", t)], w=[("w12", t)])
        if kind == "O":
            for d, order in ((0, (3, 2, 1, 0)), (1, (0, 1, 2, 3))):
                cs = slice(d * 32, (d + 1) * 32)
                for n_, t in enumerate(order):
                    if n_ == 0:
                        continue
                    tp_ = order[n_ - 1]
                    if n_ == 1:
                        src_ = prep[:, tp_, 2, cs]
                        srck = [("prep", tp_)]
                    else:
                        B.tt("dve", suft[:, cs], (prep[:, order[0], 2, cs] if n_ == 2 else suft[:, cs]),
                             prep[:, tp_, 2, cs], ALU.mult,
                             r=[("prep", order[0]), ("prep", tp_), ("suft", d)], w=[("suft", d)])
                        src_ = suft[:, cs]
                        srck = [("suft", d)]
                    B.tt("dve", w12[:, t, 1, cs], w12[:, t, 1, cs], src_, ALU.mult,
                         r=[("w12", t)] + srck, w=[("w12", t)])
            B.tt("dve", Dst[:, om, :], prep[:, 0, 2, :], prep[:, 1, 2, :], ALU.mult,
                 r=[("prep", 0), ("prep", 1)], w=[("Dst", om)])
            B.tt("dve", Dst[:, om, :], Dst[:, om, :], prep[:, 2, 2, :], ALU.mult,
                 r=[("Dst", om), ("prep", 2)], w=[("Dst", om)])
            B.tt("dve", Dst[:, om, :], Dst[:, om, :], prep[:, 3, 2, :], ALU.mult,
                 r=[("Dst", om), ("prep", 3)], w=[("Dst", om)])

    def block_body(kind, runs, out_rows, state_out, bs=0, om=0, hooks=None):
        hooks = hooks or {}
        nrun = len(runs)
        L = 512 // nrun
        phase()
        xpad = [aB.take(528) for _ in range(2)]
        dg = [aB.take(640, ("p (k c) -> p k c", dict(c=128))) for _ in range(2)]
        dt = aF.take(256, ("p (t c) -> p t c", dict(c=64)))
        dtA = aF.take(256, ("p (t c) -> p t c", dict(c=64)))
        prep = aF.take(768, ("p (t a c) -> p t a c", dict(a=3, c=64)))
        w12 = aF.take(512, ("p (t a c) -> p t a c", dict(a=2, c=64)))
        Rb = Rbr
        ybuf = [[aF.take(512) for _ in range(3)] for _ in range(2)]
        Hf = aF.take(512)
        Hb = aF.take(512)
        Htmp = aF.take(512)
        xgT = aB.take(2048, ("p (c t) -> p c t", dict(t=512)))
        BT2 = [aB.take(512) for _ in range(2)]
        CT2 = [aB.take(512) for _ in range(2)]
        xg2 = [aB.take(2048, ("p (c t) -> p c t", dict(t=512))) for _ in range(2)]
        Bg2 = [aB.take(512, ("p (c t) -> p c t", dict(t=128))) for _ in range(2)]
        sz2 = [aB.take(2048, ("p (c t) -> p c t", dict(t=512))) for _ in range(2)]
        xs2 = [aB.take(1536, ("p (c t) -> p c t", dict(t=512))) for _ in range(2)]
        Lt1 = aB.take(1024)
        Gt = [[aB.take(1024) for _ in range(2)] for _ in range(2)]
        cbm = [aB.take(256, ("p (c t) -> p c t", dict(t=128))) for _ in range(2)]
        yzb2 = [aB.take(512) for _ in range(2)]
        Hfb = aB.take(512)
        Hbe = aB.take(2048, ("p (c t) -> p c t", dict(t=512)))
        if kind == "P":
            for i in range(2):
                B.memset("dve", xpad[i][:, :], 0.0, w=[("xpad", i)])
        dt_prep(kind, om, dt, dtA, prep, w12)
        v3 = lambda ap: ap.rearrange(R3[0], **R3[1])

        def load_group(g_):
            a_ = load_w(w_in, 0, 8, C_GRP + g_ * 1280, 512)
            b_ = load_w(w_in, 0, 8, C_GRP + g_ * 1280 + 512, 256)
            c_ = load_w(w_in, 0, 8, C_GRP + g_ * 1280 + 768, 512)
            return a_, b_, c_
        wq = {0: load_group(0)}
        pre_gates_box = []

        def make_head(g):
            gp = g % 2
            BT, CT, xg, Bg, sz = BT2[gp], CT2[gp], xg2[gp], Bg2[gp], sz2[gp]
            ops_ = []

            def ct_stage1(ct):
                (sx, sxk), (sbc, sbck) = wq[g][0], wq[g][1]
                if ct < 4:
                    sl, slk, c0 = sx, sxk, ct * 128
                else:
                    sl, slk, c0 = sbc, sbck, (ct - 4) * 128
                gct = g * 6 + ct
                ps, pk = bank()
                for k in range(8):
                    B.mm(ps[:, :], sl[:, k, c0:c0 + 128], hT[:, k, :], k == 0, k == 7, r=[slk] + hT_all, w=[pk])
                xp_ = xpad[ct % 2]
                xk = ("xpad", ct % 2)
                xv = xp_[:, 0:nrun * (L + 4)].rearrange("p (r l) -> p r l", l=L + 4)
                B.cp("act", xv[:, :, 2:L + 2], ps[:, :].rearrange("p (r l) -> p r l", l=L), r=[pk], w=[xk])
                if kind != "P":
                    ps2, pk2 = bank()
                    for k in range(8):
                        B.mm(ps2[:, 0:4], sl[:, k, c0:c0 + 128], hTh[:, k, bs * 4:bs * 4 + 4], k == 0, k == 7,
                             r=[slk, ("hTh", 0)], w=[pk2])
                    B.cp("act", xp_[:, 0:2], ps2[:, 0:2], r=[pk2], w=[xk])
                    B.cp("act", xp_[:, 514:516], ps2[:, 2:4], r=[pk2], w=[xk])
                dg_ = dg[ct % 2]
                dgk = ("dg", ct % 2)
                for kk in range(5):
                    B.ts("dve", dg_[:, kk, :], ident[:, :], convw[:, gct, kk:kk + 1], None, ALU.mult, None,
                         r=[("ident", 0), ("convw", 0)], w=[dgk])

            def ct_stage2(ct):
                gct = g * 6 + ct
                xp_ = xpad[ct % 2]
                xk = ("xpad", ct % 2)
                xv = xp_[:, 0:nrun * (L + 4)].rearrange("p (r l) -> p r l", l=L + 4)
                dg_ = dg[ct % 2]
                dgk = ("dg", ct % 2)
                psc, pck = bank()
                for kk in range(5):
                    B.mm(psc[:, :].rearrange("p (r l) -> p r l", l=L), dg_[:, kk, :], xv[:, :, kk:kk + L],
                         kk == 0, kk == 4, r=[dgk, xk], w=[pck])
                if ct < 4:
                    dst_, dstk = xgT[:, ct, :], ("xgT", ct)
                elif ct == 4:
                    dst_, dstk = BT[:, :], ("BT", gp)
                else:
                    dst_, dstk = CT[:, :], ("CT", gp)
                B.act(dst_, psc[:, :], AF.Silu, r=[pck, ("convb", 0)], w=[dstk], bias=convb[:, gct:gct + 1])

            def t_stage(t):
                szw, szk = wq[g][2]
                tb, tk = tbank()
                tbv = tb[:].rearrange("p (k c) -> p k c", c=128)
                for ct in range(4):
                    B.tr(tbv[:, ct, :], xgT[:, ct, t * 128:(t + 1) * 128], ident[:],
                         r=[("xgT", ct), ("ident", 0)], w=[tk])
                B.tr(tbv[:, 4, :], BT[:, t * 128:(t + 1) * 128], ident[:], r=[("BT", gp), ("ident", 0)], w=[tk])
                B.cp("act", xg[:, t, :], tb[:, 0:512], r=[tk], w=[("xg", gp, t)])
                B.cp("act", Bg[:, t, :], tb[:, 512:640], r=[tk], w=[("Bg", gp, t)])
                ps, pk = bank()
                for k in range(8):
                    B.mm(ps[:, :], hT[:, k, t * 128:(t + 1) * 128], szw[:, k, :], k == 0, k == 7,
                         r=[szk, ("hT", t)], w=[pk])
                B.act(sz[:, t, :], ps[:, :], AF.Silu, r=[pk], w=[("sz", gp, t)])

            def first():
                if "group" in hooks:
                    hooks["group"](g)
                ct_stage1(0)
            ops_.append(first)
            for ct in range(6):
                def _f(ct=ct):
                    if ct + 1 < 6:
                        ct_stage1(ct + 1)
                    ct_stage2(ct)
                ops_.append(_f)
            def pre1():
                if g + 1 < NG:
                    g_ = g + 1
                    wq[g_] = [load_w(w_in, 0, 8, C_GRP + g_ * 1280, 512),
                              load_w(w_in, 0, 8, C_GRP + g_ * 1280 + 512, 256), None]
                else:
                    pre_gates_box.extend(load_w(w_in, 0, 8, C_GATE + gi_ * 512, 512) for gi_ in range(2))
            ops_.append(pre1)
            for t in range(4):
                ops_.append(lambda t=t: t_stage(t))

            def pre2():
                if g + 1 < NG:
                    g_ = g + 1
                    wq[g_][2] = load_w(w_in, 0, 8, C_GRP + g_ * 1280 + 768, 512)
                else:
                    pre_gates_box.append(load_w(w_in, 0, 8, C_GATE + 2 * 512, 512))
            ops_.append(pre2)
            return ops_

        def make_sweeps(g):
            gp = g % 2
            BT, CT, xg, Bg, sz = BT2[gp], CT2[gp], xg2[gp], Bg2[gp], sz2[gp]
            ops_ = []

            def xscale(t, j):
                d = j - 1
                xs = xs2[t % 2]
                wv = w12[:, t, 1, d * 32 + g * 8: d * 32 + g * 8 + 8]
                B.tt("pool", v3(xs[:, j, :]), v3(xg[:, t, :]), bc_last(wv, 64), ALU.mult,
                     r=[("xg", gp, t), ("w12", t)], w=[("xs", t % 2, j)])

            def state_update(H, Hk, t, d):
                xscale(t, 1 + d)
                xs = xs2[t % 2]
                ps, pk = bank()
                B.mm(ps[:, :], Bg[:, t, :], xs[:, 1 + d, :], True, True, r=[("Bg", gp, t), ("xs", t % 2, 1 + d)], w=[pk])
                cdv = prep[:, t, 2, d * 32 + g * 8: d * 32 + g * 8 + 8]
                B.tt("dve", v3(Htmp[:, :]), v3(H[:, :]), bc_last(cdv, 64), ALU.mult, r=[Hk, ("prep", t)],
                     w=[("Htmp", 0)])
                B.tt("dve", H[:, :], ps[:, :], Htmp[:, :], ALU.add, r=[pk, ("Htmp", 0)], w=[Hk])

            def chain_init(H, Hk, d):
                B.dma(H[:, :], h0_d[d, :, g * 512:(g + 1) * 512], r=[], w=[Hk])
                for m in (range(3) if d == 0 else reversed(range(3))):
                    y3 = ybuf[0][2]
                    B.dma(y3[:, :], Escr[m, d, g, :, :], r=[("Escr", m, d, g)], w=[("y3", 0)])
                    dv_ = Dst[:, m, d * 32 + g * 8: d * 32 + g * 8 + 8]
                    B.tt("dve", v3(Htmp[:, :]), v3(H[:, :]), bc_last(dv_, 64), ALU.mult, r=[Hk, ("Dst", m)],
                         w=[("Htmp", 0)])
                    B.tt("dve", H[:, :], Htmp[:, :], y3[:, :], ALU.add, r=[("Htmp", 0), ("y3", 0)], w=[Hk])

            for ri, run in enumerate(runs):
                def _init():
                    if kind == "S":
                        chain_init(Hb, ("Hb", 0), 1)
                    else:
                        B.memset("dve", Hb[:, :], 0.0, w=[("Hb", 0)])
                ops_.append(_init)
                for t in reversed(run):
                    def _st(t=t):
                        B.cp("act", Hbe[:, t, :], Hb[:, :], r=[("Hb", 0)], w=[("Hbe", t)])
                        state_update(Hb, ("Hb", 0), t, 1)
                    ops_.append(_st)
                if state_out is not None:
                    def _out(ri=ri):
                        B.dma(hb_o[state_out[ri], :, g * 512:(g + 1) * 512], Hb[:, :], r=[("Hb", 0)],
                              w=[("hb_o", 0)])
                    ops_.append(_out)

            def stage_a(t):
                par = t % 2
                tsl = slice(t * 128, (t + 1) * 128)
                ps, pk = bank()
                B.mm(ps[:, 0:128], BT[:, tsl], CT[:, tsl], True, True, r=[("BT", gp), ("CT", gp)], w=[pk])
                for d in range(2):
                    B.tt("dve", cbm[par][:, d, :], ps[:, 0:128], masks[:, d, :], ALU.mult,
                         r=[pk, ("masks", 0)], w=[("cbm", par, d)])
                for d in range(2):
                    dv = dtA[:, t, d * 32 + g * 8: d * 32 + g * 8 + 8]
                    B.tt("pool", Rb[d][:, :].rearrange("p (h i) -> p h i", i=128), bc_last(dv, 128),
                         bc_mid(tris[:, d, :], 8), ALU.mult, r=[("dtA", t), ("tris", 0)], w=[("Rb", d)])
                    for hh in range(2):
                        B.mm(psS[:, hh * 512:(hh + 1) * 512], trisr[:, d, :], Rb[d][:, hh * 512:(hh + 1) * 512],
                             True, False, r=[("trisr", 0), ("Rb", d)], w=[("psS", hh)])
                        hd0 = d * 32 + g * 8 + hh * 4
                        B.mm(psS[:, hh * 512:(hh + 1) * 512].rearrange("p (h i) -> p h i", i=128), trisr[:, 2, :],
                             bc_last(lndt[:, t, hd0:hd0 + 4], 128),
                             False, True, r=[("trisr", 0), ("lndt", t)], w=[("psS", hh)])
                    B.act(Lt1[:, :], psS[:, :], AF.Exp, r=[("psS", 0), ("psS", 1)], w=[("Lt", 0)])
                    B.tt("dve", Gt[par][d][:, :].rearrange("p (h i) -> p h i", i=128),
                         Lt1[:, :].rearrange("p (h i) -> p h i", i=128), bc_mid(cbm[par][:, d, :], 8), ALU.mult,
                         r=[("Lt", 0), ("cbm", par, d)], w=[("Gt", par, d)])

            def stage_b(t):
                par = t % 2
                tsl = slice(t * 128, (t + 1) * 128)
                psy, pyk = bank()
                xD = xs2[par][:, 0, :]
                B.tt("pool", v3(xD), v3(xg[:, t, :]), bc_last(Dbc[:, g * 8:g * 8 + 8], 64), ALU.mult,
                     r=[("xg", gp, t), ("Dbc", 0)], w=[("xs", par, 0)])
                B.mm(psy[:, :], ident[:, :], xD, True, False, r=[("ident", 0), ("xs", par, 0)], w=[pyk], skip=True)
                n = 0
                for h in range(8):
                    for d in range(2):
                        B.mm(psy[:, h * 64:(h + 1) * 64], Gt[par][d][:, h * 128:(h + 1) * 128],
                             xg[:, t, h * 64:(h + 1) * 64], False, n == 15,
                             r=[("Gt", par, d), ("xg", gp, t)], w=[pyk], skip=True)
                        n += 1
                pof, pofk = bank()
                B.mm(pof[:, :], CT[:, tsl], Hfb[:, :], True, True, r=[("CT", gp), ("Hfb", 0)], w=[pofk])
                pob, pobk = bank()
                B.mm(pob[:, :], CT[:, tsl], Hbe[:, t, :], True, True, r=[("CT", gp), ("Hbe", t)], w=[pobk])
                state_update(Hf, ("Hf", 0), t, 0)
                B.cp("act", Hfb[:, :], Hf[:, :], r=[("Hf", 0)], w=[("Hfb", 0)])
                ef = prep[:, t, 1, g * 8: g * 8 + 8]
                eb = prep[:, t, 1, 32 + g * 8: 32 + g * 8 + 8]
                y1, y2, y3 = ybuf[par]
                yk = lambda nm: (nm, par)
                B.tt("dve", v3(y1[:, :]), v3(pof[:, :]), bc_last(ef, 64), ALU.mult, r=[pofk, ("prep", t)], w=[yk("y1")])
                B.tt("dve", v3(y2[:, :]), v3(pob[:, :]), bc_last(eb, 64), ALU.mult, r=[pobk, ("prep", t)], w=[yk("y2")])
                B.tt("dve", y3[:, :], psy[:, :], y1[:, :], ALU.add, r=[pyk, yk("y1")], w=[yk("y3")])
                B.tt("dve", y3[:, :], y3[:, :], y2[:, :], ALU.add, r=[yk("y3"), yk("y2")], w=[yk("y3")])
                B.tt("dve", y1[:, :], y3[:, :], sz[:, t, :], ALU.mult, r=[yk("y3"), ("sz", gp, t)], w=[yk("y1")])
                B.act(junk[:, 0:512], y1[:, :], AF.Square, r=[yk("y1")], w=[("junk", 0), ("ssq", t, g)],
                      accum=ssq[:, t, g:g + 1])
                B.cp("act", yzb2[par][:, :], y1[:, :], r=[yk("y1")], w=[("yzb", par)])

            def stage_b2(t):
                par = t % 2
                tsl = slice(t * 128, (t + 1) * 128)
                tb, tk = tbank()
                tbv = tb[:].rearrange("p (k c) -> p k c", c=128)
                for ct in range(4):
                    B.tr(tbv[:, ct, :], yzb2[par][:, ct * 128:(ct + 1) * 128], ident[:],
                         r=[("yzb", par), ("ident", 0)], w=[tk])
                B.cp("act", yzT[:, g * 4:g * 4 + 4, tsl], tbv[:, 0:4, :], r=[tk], w=[("yzT", g, t)])

            steps = [(ri, t) for ri, run in enumerate(runs) for t in run]
            ops_.append(lambda: stage_a(steps[0][1]))
            for si, (ri, t) in enumerate(steps):
                run = runs[ri]
                if t == run[0]:
                    def _fi():
                        if kind == "S":
                            chain_init(Hf, ("Hf", 0), 0)
                            B.cp("act", Hfb[:, :], Hf[:, :], r=[("Hf", 0)], w=[("Hfb", 0)])
                        else:
                            B.memset("dve", Hf[:, :], 0.0, w=[("Hf", 0)])
                            B.memset("dve", Hfb[:, :], 0.0, w=[("Hfb", 0)])
                    ops_.append(_fi)
                if si + 1 < len(steps):
                    ops_.append(lambda si=si: stage_a(steps[si + 1][1]))
                ops_.append(lambda t=t: stage_b(t))
                if si > 0:
                    ops_.append(lambda si=si: stage_b2(steps[si - 1][1]))
                if t == run[-1] and state_out is not None:
                    def _fo(ri=ri):
                        B.dma(hf_o[state_out[ri], :, g * 512:(g + 1) * 512], Hf[:, :], r=[("Hf", 0)],
                              w=[("hf_o", 0)])
                    ops_.append(_fo)
            ops_.append(lambda: stage_b2(steps[-1][1]))
            return ops_

        for f_ in make_head(0):
            f_()
        for g in range(NG):
            sw = make_sweeps(g)
            hd = make_head(g + 1) if g + 1 < NG else []
            i_h = 0
            for i_s, f_ in enumerate(sw):
                f_()
                while i_h < len(hd) and i_h * len(sw) <= (i_s + 1) * len(hd) - 1 and i_s >= 1:
                    hd[i_h]()
                    i_h += 1
            while i_h < len(hd):
                hd[i_h]()
                i_h += 1
        pre_gates = pre_gates_box
        for t in range(4):
            B.tt("dve", ssq[:, t, 0:2], ssq[:, t, 0:2], ssq[:, t, 2:4], ALU.add,
                 r=[("ssq", t, g_) for g_ in range(4)], w=[("ssq", t, 0), ("ssq", t, 1)])
            B.tt("dve", ssq[:, t, 0:1], ssq[:, t, 0:1], ssq[:, t, 1:2], ALU.add,
                 r=[("ssq", t, 0), ("ssq", t, 1)], w=[("ssq", t, 0)])
            B.act(ssq[:, t, 0:1], ssq[:, t, 0:1], AF.Sqrt, r=[("ssq", t, 0)], w=[("ssq", t, 0)],
                  bias=EPS, scale=1.0 / DIN)
            B.recip(ry[:, t:t + 1], ssq[:, t, 0:1], r=[("ssq", t, 0)], w=[("ry", t)])
        phase()
        Macc = aF.take(4096, ("p (t c) -> p t c", dict(c=D)))
        y1 = aF.take(512)
        gfs = aB.take(8192, ("p (t a c) -> p t a c", dict(a=2, c=D)))
        Mb = aB.take(1024)
        MT = aB.take(4096, ("p (k t) -> p k t", dict(t=512)))
        mtmp = (aF.take(D), aF.take(D))
        for gi in range(4):
            s, sk = pre_gates[gi] if gi < 3 else load_w(w_in, 0, 8, C_GATE + gi * 512, 512)
            for t in range(4):
                ps, pk = bank()
                for k in range(8):
                    B.mm(ps[:, :], hT[:, k, t * 128:(t + 1) * 128], s[:, k, :], k == 0, k == 7,
                         r=[sk, ("hT", t)], w=[pk])
                B.act(gfs[:, t, gi // 2, (gi % 2) * 512:(gi % 2 + 1) * 512], ps[:, :], AF.Sigmoid,
                      r=[pk], w=[("gfs", t, gi)])
        if "merge_start" in hooks:
            hooks["merge_start"](mtmp)
        for half in range(2):
            hs = slice(half * 512, (half + 1) * 512)
            s, sk = load_w(w_fourier, 0, 8, half * 512, 512)
            for t in range(4):
                ps, pk = bank()
                for k in range(8):
                    B.mm(ps[:, :], YT[:, k, t * 128:(t + 1) * 128], s[:, k, :], k == 0, k == 7,
                         r=[sk] + [("YT", kk) for kk in range(8)], w=[pk])
                B.tt("dve", Macc[:, t, hs], ps[:, :], gfs[:, t, 0, hs], ALU.mult,
                     r=[pk, ("gfs", t, half)], w=[("Macc", t, half)])
        for half in range(2):
            hs = slice(half * 512, (half + 1) * 512)
            sl2 = []
            for kh in range(2):
                s, sk = load_w(w_ssd_out, kh * 1024, 8, half * 512, 512)
                for k in range(8):
                    B.ts("dve", s[:, k, :], s[:, k, :], ssdg[:, kh * 8 + k:kh * 8 + k + 1], None, ALU.mult, None,
                         r=[sk, ("ssdg", 0)], w=[sk])
                sl2.append((s, sk))
            for t in range(4):
                ps, pk = bank()
                for kk in range(16):
                    s, sk = sl2[kk // 8]
                    B.mm(ps[:, :], yzT[:, kk, t * 128:(t + 1) * 128], s[:, kk % 8, :], kk == 0, kk == 15,
                         r=[sk] + [("yzT", g_, t) for g_ in range(4)], w=[pk])
                B.stt(y1[:, :], ps[:, :], ry[:, t:t + 1], gfs[:, t, 1, hs], ALU.mult, ALU.mult,
                      r=[pk, ("ry", t), ("gfs", t, 2 + half)], w=[("y1", 0)])
                B.tt("dve", Macc[:, t, hs], Macc[:, t, hs], y1[:, :], ALU.add,
                     r=[("Macc", t, half), ("y1", 0)], w=[("Macc", t, half)])
        for t in range(4):
            B.cp("act", Mb[:, :], Macc[:, t, :], r=[("Macc", t, 0), ("Macc", t, 1)], w=[("Mb", 0)])
            tile_T(Mb, ("Mb", 0), MT, "MT", t)
        for half in range(2):
            hs = slice(half * 512, (half + 1) * 512)
            s, sk = load_w(w_out, 0, 8, half * 512, 512)
            for t in range(4):
                ps, pk = bank()
                for k in range(8):
                    B.mm(ps[:, :], MT[:, k, t * 128:(t + 1) * 128], s[:, k, :], k == 0, k == 7,
                         r=[sk, ("MT", t)], w=[pk])
                B.tt("dve", y1[:, :], ps[:, :], modg[:, 0, hs], ALU.mult, r=[pk, ("modg", 2)], w=[("y1", 0)])
                B.tt("dve", xres[:, t, hs], xres[:, t, hs], y1[:, :], ALU.add, r=[("xres", t), ("y1", 0)],
                     w=[("xres", t)])
        phase()
        gtmp = aF.take(D)
        y1 = aF.take(512)
        otile = [aF.take(D) for _ in range(2)]
        gFbc = aF.take(D)
        aT = aB.take(16384, ("p (k t) -> p k t", dict(t=512)))
        rl = [aB.take(512) for _ in range(2)]
        B.dma(gFbc[:, :], gFbc_d[:, :], r=[], w=[("gFbc", 0)])
        mtmp = (aF.take(D), aF.take(D))
        if "mlp_start" in hooks:
            hooks["mlp_start"](mtmp)
        for t in range(4):
            norm_mod_T(t, 2, 3, hT, "hT", gtmp)
        if "after_norm2" in hooks:
            hooks["after_norm2"](mtmp)
        for fb in range(8):
            s, sk = load_w(w_ff1, 0, 8, fb * 512, 512)
            for ft in range(4):
                ps, pk = bank()
                for k in range(8):
                    B.mm(ps[:, :], s[:, k, ft * 128:(ft + 1) * 128], hT[:, k, :], k == 0, k == 7,
                         r=[sk] + hT_all, w=[pk])
                r_ = rl[ft % 2]
                rk = ("rl", ft % 2)
                B.act(r_[:, :], ps[:, :], AF.Relu, r=[pk], w=[rk])
                B.tt("dve", aT[:, fb * 4 + ft, :], r_[:, :], r_[:, :], ALU.mult, r=[rk], w=[("aT", fb * 4 + ft)])
        if "after_ff1" in hooks:
            hooks["after_ff1"](mtmp)
        for half in range(2):
            hs = slice(half * 512, (half + 1) * 512)
            for kq in range(4):
                s, sk = load_w(w_ff2, kq * 1024, 8, half * 512, 512)
                for t in range(4):
                    for k in range(8):
                        B.mm(psb[t][:, :], aT[:, kq * 8 + k, t * 128:(t + 1) * 128], s[:, k, :],
                             kq == 0 and k == 0, kq == 3 and k == 7,
                             r=[sk, ("aT", kq * 8 + k)], w=[("psb", t)])
            for t in range(4):
                B.tt("dve", y1[:, :], psb[t][:, :], modg[:, 1, hs], ALU.mult, r=[("psb", t), ("modg", 5)],
                     w=[("y1", 0)])
                B.tt("dve", xres[:, t, hs], xres[:, t, hs], y1[:, :], ALU.add, r=[("xres", t), ("y1", 0)],
                     w=[("xres", t)])
        for t in range(4):
            rms_stats(xres[:, t, :], 4 + t, r=[("xres", t)])
            ot = otile[t % 2]
            ok_ = ("otile", t % 2)
            B.stt(ot[:, :], xres[:, t, :], st_rs[:, 4 + t:5 + t], gFbc[:, :], ALU.mult, ALU.mult,
                  r=[("xres", t), ("st_rs", 4 + t), ("gFbc", 0)], w=[ok_])
            B.dma(out_rows(t), ot[:, :], r=[ok_], w=[("out", t)])

    def other_block(om, hcur, hname, next_norm):
        phase()
        L = 512
        hk_all = [(hname, t) for t in range(4)]
        xpad = [aB.take(528) for _ in range(2)]
        dg = [aB.take(640, ("p (k c) -> p k c", dict(c=128))) for _ in range(2)]
        dt = aF.take(256, ("p (t c) -> p t c", dict(c=64)))
        dtA = aF.take(256, ("p (t c) -> p t c", dict(c=64)))
        prep = aF.take(768, ("p (t a c) -> p t a c", dict(a=3, c=64)))
        w12 = aF.take(512, ("p (t a c) -> p t a c", dict(a=2, c=64)))
        Eb = [aF.take(512) for _ in range(2)]
        xgT = [aB.take(2048, ("p (c t) -> p c t", dict(t=512))) for _ in range(2)]
        BT = [aB.take(512) for _ in range(2)]
        xg = [aB.take(2048, ("p (c t) -> p c t", dict(t=512))) for _ in range(2)]
        Bg = [aB.take(512, ("p (c t) -> p c t", dict(t=128))) for _ in range(2)]
        xsd = [[aB.take(512) for _ in range(4)] for _ in range(2)]
        dt_prep("O", om, dt, dtA, prep, w12, hcur, hname)

        def load_group(g_):
            return (load_w(w_in, 0, 8, C_GRP + g_ * 1280, 512), load_w(w_in, 0, 8, C_GRP + g_ * 1280 + 512, 128))
        pre = [load_group(0)]

        def head(g):
            gp = g % 2
            (sx, sxk), (sbc, sbck) = pre[0]

            def st1(ct):
                if ct < 4:
                    sl, slk, c0 = sx, sxk, ct * 128
                else:
                    sl, slk, c0 = sbc, sbck, 0
                gct = g * 6 + ct
                ps, pk = bank()
                for k in range(8):
                    B.mm(ps[:, :], sl[:, k, c0:c0 + 128], hcur[:, k, :], k == 0, k == 7, r=[slk] + hk_all, w=[pk])
                xp_ = xpad[ct % 2]
                xk = ("xpad", ct % 2)
                B.cp("act", xp_[:, 2:L + 2], ps[:, :], r=[pk], w=[xk])
                ps2, pk2 = bank()
                for k in range(8):
                    B.mm(ps2[:, 0:4], sl[:, k, c0:c0 + 128], hTh[:, k, om * 4:om * 4 + 4], k == 0, k == 7,
                         r=[slk, ("hTh", 0)], w=[pk2])
                B.cp("act", xp_[:, 0:2], ps2[:, 0:2], r=[pk2], w=[xk])
                B.cp("act", xp_[:, 514:516], ps2[:, 2:4], r=[pk2], w=[xk])
                dg_ = dg[ct % 2]
                dgk = ("dg", ct % 2)
                for kk in range(5):
                    B.ts("dve", dg_[:, kk, :], ident[:, :], convw[:, gct, kk:kk + 1], None, ALU.mult, None,
                         r=[("ident", 0), ("convw", 0)], w=[dgk])

            def st2(ct):
                gct = g * 6 + ct
                xp_ = xpad[ct % 2]
                xk = ("xpad", ct % 2)
                dg_ = dg[ct % 2]
                dgk = ("dg", ct % 2)
                psc, pck = bank()
                for kk in range(5):
                    B.mm(psc[:, :], dg_[:, kk, :], xp_[:, kk:kk + L], kk == 0, kk == 4, r=[dgk, xk], w=[pck])
                if ct < 4:
                    dst_, dstk = xgT[gp][:, ct, :], ("xgT", gp, ct)
                else:
                    dst_, dstk = BT[gp][:, :], ("BT", gp)
                B.act(dst_, psc[:, :], AF.Silu, r=[pck, ("convb", 0)], w=[dstk], bias=convb[:, gct:gct + 1])

            st1(0)
            for ct in range(5):
                if ct + 1 < 5:
                    st1(ct + 1)
                st2(ct)
            if g + 1 < NG:
                pre[0] = load_group(g + 1)
            for t in range(4):
                tb, tk = tbank()
                tbv = tb[:].rearrange("p (k c) -> p k c", c=128)
                for ct in range(4):
                    B.tr(tbv[:, ct, :], xgT[gp][:, ct, t * 128:(t + 1) * 128], ident[:],
                         r=[("xgT", gp, ct), ("ident", 0)], w=[tk])
                B.tr(tbv[:, 4, :], BT[gp][:, t * 128:(t + 1) * 128], ident[:], r=[("BT", gp), ("ident", 0)], w=[tk])
                B.cp("act", xg[gp][:, t, :], tb[:, 0:512], r=[tk], w=[("xg", gp, t)])
                B.cp("act", Bg[gp][:, t, :], tb[:, 512:640], r=[tk], w=[("Bg", gp, t)])

        def tail(g):
            gp = g % 2
            for d in range(2):
                ps, pk = bank()
                for t in range(4):
                    wv = w12[:, t, 1, d * 32 + g * 8: d * 32 + g * 8 + 8]
                    eng = "pool" if (t + d) % 2 == 0 else "dve"
                    B.tt(eng, xsd[d][t][:, :].rearrange(R3[0], **R3[1]), xg[gp][:, t, :].rearrange(R3[0], **R3[1]),
                         bc_last(wv, 64), ALU.mult, r=[("xg", gp, t), ("w12", t)], w=[("xs", d, t)])
                    B.mm(ps[:, :], Bg[gp][:, t, :], xsd[d][t][:, :], t == 0, t == 3,
                         r=[("Bg", gp, t), ("xs", d, t)], w=[pk])
                B.cp("act", Eb[d][:, :], ps[:, :], r=[pk], w=[("Hf" if d == 0 else "Hb", 0)])
                B.dma(Escr[om, d, g, :, :], Eb[d][:, :], r=[("Hf" if d == 0 else "Hb", 0)], w=[("Escr", om, d, g)])

        head(0)
        for g in range(NG):
            if g + 1 < NG:
                head(g + 1)
            if g == NG - 2:
                next_norm()
            tail(g)

    def prompt_block(bi):
        seqs = [2 * bi, 2 * bi + 1]
        runs = [(0, 1), (2, 3)]
        phase()
        gtmp = aF.take(D)
        UT = aB.take(4096, ("p (k t) -> p k t", dict(t=512)))
        T12 = [aB.take(1024, ("p (k t) -> p k t", dict(t=512))) for _ in range(2)]
        for t in range(4):
            B.dma(xres[:, t, :], xp[seqs[t // 2], (t % 2) * 128:(t % 2 + 1) * 128, :], r=[], w=[("xres", t)])
        for t in range(4):
            norm_mod_T(t, 0, 1, hT, "hT", gtmp)
        for half in range(2):
            s, sk = load_w(w_in, 0, 8, C_UF + half * 512, 512)
            for ct in range(4):
                ps, pk = bank()
                for k in range(8):
                    B.mm(ps[:, :], s[:, k, ct * 128:(ct + 1) * 128], hT[:, k, :], k == 0, k == 7,
                         r=[sk] + hT_all, w=[pk])
                B.cp("act", UT[:, half * 4 + ct, :], ps[:, :], r=[pk], w=[("UT", half * 4 + ct)])
        it = 0
        for q in range(2):
            for g in range(NG):
                tb_ = T12[it % 2]
                tk_ = ("T12", it % 2)
                it += 1
                for pt in range(2):
                    ps, pk = bank()
                    for cs in range(2):
                        for ci in range(2):
                            B.mm(ps[:, cs * 256:(cs + 1) * 256],
                                 UT[:, 2 * g + ci, q * 256 + pt * 128: q * 256 + (pt + 1) * 128],
                                 dft[:, cs, ci, :], ci == 0, ci == 1,
                                 r=[("UT", 2 * g + ci)] + DFT_ALL, w=[pk])
                    B.cp("dve", tb_[:, pt, :], ps[:, :], r=[pk], w=[tk_])
                ps, pk = bank()
                for c2 in range(2):
                    n = 0
                    for pt in range(2):
                        for cs in range(2):
                            B.mm(ps[:, c2 * 256:(c2 + 1) * 256],
                                 tb_[:, pt, cs * 256 + c2 * 128: cs * 256 + (c2 + 1) * 128],
                                 dft[:, 0 if cs == 0 else 2, pt, :], n == 0, n == 3,
                                 r=[tk_] + DFT_ALL, w=[pk])
                            n += 1
                B.cp("act", YT[:, 2 * g:2 * g + 2, q * 256:(q + 1) * 256],
                     ps[:, :].rearrange("p (a b) -> p a b", b=256), r=[pk],
                     w=[("YT", 2 * g), ("YT", 2 * g + 1)])
        if bi == 0:
            hooks = {"merge_start": lambda tt_: ([mod_piece(c_) for c_ in range(4, 10)], mod_part(0, (2,), tt_)),
                     "mlp_start": lambda tt_: (mod_piece(10), mod_piece(11), mod_part(0, (3, 4), tt_)),
                     "after_ff1": lambda tt_: mod_part(0, (5,), tt_)}
        elif bi == n_pblocks - 1 and sample:
            hooks = {"merge_start": lambda tt_: mod_part(1, (0, 1), tt_),
                     "mlp_start": lambda tt_: mod_part(1, (2,), tt_),
                     "after_norm2": lambda tt_: mod_part(1, (3, 4), tt_)}
        else:
            hooks = {}
        block_body("P", runs,
                   out_rows=lambda t: yp[seqs[t // 2], (t % 2) * 128:(t % 2 + 1) * 128, :],
                   state_out=seqs, hooks=hooks)

    def norm_tile(src, skey, col, dst, dkey, gtmp, mask=None):
        rms_stats(src, col, r=[skey])
        if mask is not None:
            B.tt("dve", st_rs[:, col:col + 1], st_rs[:, col:col + 1], mask, ALU.mult,
                 r=[("st_rs", col), ("hmask", 0)], w=[("st_rs", col)])
        B.stt(gtmp[:, :], src, st_rs[:, col:col + 1], modb[:, 1, :], ALU.mult, ALU.mult,
              r=[skey, ("st_rs", col), ("modb", 1)], w=[("gtmp", 0)])
        if mask is not None:
            B.stt(dst, modb[:, 0, :], mask, gtmp[:, :], ALU.mult, ALU.add,
                  r=[("gtmp", 0), ("modb", 0), ("hmask", 0)], w=[dkey])
        else:
            B.tt("dve", dst, gtmp[:, :], modb[:, 0, :], ALU.add, r=[("gtmp", 0), ("modb", 0)], w=[dkey])

    def sample_block():
        phase()
        mod_part(1, (5,))
        phase()
        gtmp = aF.take(D)
        HmT = [aF.take(2048).bitcast(BF16).rearrange("p (k t) -> p k t", t=512) for _ in range(2)]
        UcT = aF.take(2048).bitcast(BF16).rearrange("p (k t) -> p k t", t=512)
        UsT = yzT[:, 0:8, :]
        hTM = aB.take(16384, ("p (t c) -> p t c", dict(c=D)))
        B.dma(xres[:, 0, :], xhalo_d[:, :], r=[], w=[("xres", 0)])
        norm_tile(xres[:, 0, :], ("xres", 0), 8, htm[0][:], ("htm", 0), gtmp, mask=hmask[:, 0:1])
        tb, tk = tbank()
        tbv = tb[:].rearrange("p (k c) -> p k c", c=128)
        for k in range(8):
            B.tr(tbv[:, k, :], htm[0][:, k * 128:(k + 1) * 128], ident[:], r=[("htm", 0), ("ident", 0)], w=[tk])
        B.cp("act", hTh[:, :, :], tbv, r=[tk], w=[("hTh", 0)])
        for i in range(16):
            xt = i % 4
            B.dma(xres[:, xt, :], xs_all[i, :, :], r=[], w=[("xres", xt)])
            norm_tile(xres[:, xt, :], ("xres", xt), xt, hTM[:, i, :], ("hTM", i), gtmp)
        hTM_all = [("hTM", i) for i in range(16)]
        for m in range(2):
            sl_ = [load_w(dftp_d[m], kh * 1024, 8, 0, 512) for kh in range(2)]
            for dtl in range(8):
                ps, pk = bank()
                for pt in range(16):
                    sw, swk = sl_[pt // 8]
                    B.mm(ps[:, :], hTM[:, pt, dtl * 128:(dtl + 1) * 128], sw[:, pt % 8, :], pt == 0, pt == 15,
                         r=[swk] + hTM_all, w=[pk])
                B.cp("act", HmT[m][:, dtl, :], ps[:, :], r=[pk], w=[("HmT", m, dtl)])
        for half in range(2):
            s, sk = load_w(w_in, 0, 8, C_UF + half * 512, 512)
            for m in range(2):
                dstU = UcT if m == 0 else UsT
                for ct in range(4):
                    ps, pk = bank()
                    for k in range(8):
                        B.mm(ps[:, :], s[:, k, ct * 128:(ct + 1) * 128], HmT[m][:, k, :], k == 0, k == 7,
                             r=[sk] + [("HmT", m, kk) for kk in range(8)], w=[pk])
                    B.cp("act", dstU[:, half * 4 + ct, :], ps[:, :], r=[pk],
                         w=[("UmT", m, half * 4 + ct), ("yzT", 0, 0)] if m == 1 else [("UmT", m, half * 4 + ct)])
        for g in range(NG):
            for c2 in range(2):
                ps, pk = bank()
                n = 0
                for m in range(2):
                    srcU = UcT if m == 0 else UsT
                    for ci in range(2):
                        B.mm(ps[:, :], dft[:, 0 if m == 0 else 2, ci, c2 * 128:(c2 + 1) * 128],
                             srcU[:, 2 * g + ci, :], n == 0, n == 3,
                             r=DFT_ALL + [("UmT", m, 2 * g + ci)], w=[pk])
                        n += 1
                B.cp("act", YT[:, 2 * g + c2, :], ps[:, :], r=[pk], w=[("YT", 2 * g + c2)])
        hTalt = arenaB[:, NAB - 4096:NAB].rearrange("p (k t) -> p k t", t=512)
        hbufs = [(hTalt, "hTalt"), (hT, "hT")]

        def norm_other(om, dst, dname):
            gtmp2 = aF.take(D)
            xst = [aF.take(D) for _ in range(2)]
            for t in range(4):
                i = om * 4 + t
                B.dma(xst[t % 2][:, :], xs_all[i, :, :], r=[], w=[("xst", t % 2)])
                norm_tile(xst[t % 2][:, :], ("xst", t % 2), t, htm[t % 2][:], ("htm", t % 2), gtmp2)
                tile_T(htm[t % 2], ("htm", t % 2), dst, dname, t)

        def norm_own():
            gtmp3 = aF.take(D)
            for t in range(4):
                norm_mod_T(t, 0, 1, hT, "hT", gtmp3)

        phase()
        norm_other(0, *hbufs[0])
        for om in range(3):
            hc, hn = hbufs[om % 2]
            if om < 2:
                nxt = (lambda om_=om: norm_other(om_ + 1, *hbufs[(om_ + 1) % 2]))
            else:
                nxt = norm_own
            other_block(om, hc, hn, nxt)
        block_body("S", [(0, 1, 2, 3)], out_rows=lambda t: ys[t * 128:(t + 1) * 128, :], state_out=None, bs=3)

    for cb_ in range(4):
        mod_piece(cb_)
    phase()
    mod_part(0, (0, 1))
    for bi in range(n_pblocks):
        prompt_block(bi)
    if sample:
        sample_block()

    B.sbuf_left = nc.sbuf_bytes_remaining
    B.P.emit()
    B.stack.close()
    return nc, B


def _consts():
    k = np.arange(128)
    tri = (k[:, None] <= k[None, :]).astype(np.float32)
    triL = (k[:, None] >= k[None, :]).astype(np.float32)
    SL = (k[:, None] > k[None, :]).astype(np.float32)
    SU = (k[:, None] < k[None, :]).astype(np.float32)
    ones = np.ones((128, 128), np.float32)
    tris = np.stack([tri, triL, SL, SU, ones], axis=1)
    n = np.arange(256)
    ang = 2.0 * np.pi * np.outer(n, n) / 256.0
    C = (np.cos(ang) / 16.0).astype(np.float32)
    S = (np.sin(ang) / 16.0).astype(np.float32)
    dft = np.stack([C, S, -S], axis=0)
    return tris, dft, np.eye(128, dtype=np.float32)


_CACHE = {}


def kernel(x_prompt, x_sample, state_ssm_fwd, state_ssm_bwd, c, c_ctx, w_mod, b_mod, norm1_g,
           w_in, w_fourier, conv_w, conv_b, dt_bias, A_log, D_skip, ssd_norm_g, w_ssd_out,
           w_out, norm2_g, w_ff1, w_ff2, final_norm_g):
    f = lambda a: np.ascontiguousarray(np.asarray(a, dtype=np.float32))
    x_prompt = f(x_prompt); x_sample = f(x_sample)
    w_in0 = f(w_in)[0]
    cols = [np.arange(0, 1024)]
    for g in range(4):
        cols.append(np.arange(3072 + g * 512, 3072 + (g + 1) * 512))
        cols.append(np.arange(5120 + g * 128, 5120 + (g + 1) * 128))
        cols.append(np.arange(5632 + g * 128, 5632 + (g + 1) * 128))
        cols.append(np.arange(1024 + g * 512, 1024 + (g + 1) * 512))
    cols.append(np.arange(6144, 6208))
    cols.append(np.arange(6208, 8256))
    cols = np.concatenate(cols)
    w_in_r = np.ascontiguousarray(w_in0[:, cols])
    cw = f(conv_w)[0]; cbv = f(conv_b)[0]
    ch = []
    for g in range(4):
        ch.append(np.arange(g * 512, (g + 1) * 512))
        ch.append(np.arange(2048 + g * 128, 2048 + (g + 1) * 128))
        ch.append(np.arange(2560 + g * 128, 2560 + (g + 1) * 128))
    ch = np.concatenate(ch)
    convw = np.ascontiguousarray(cw[:, ch].T.reshape(24, 128, 5).transpose(1, 0, 2))
    convb = np.ascontiguousarray(cbv[ch].reshape(24, 128).T)
    rep = lambda v: np.ascontiguousarray(np.broadcast_to(np.asarray(v, np.float32).reshape(1, -1), (128, np.asarray(v).size)))
    tris, dft, ident = _consts()
    common = {
        "w_mod": f(w_mod)[0], "b_mod2": np.ascontiguousarray(np.broadcast_to(f(b_mod)[0][None], (2, 6144))),
        "g1bc": rep(f(norm1_g)[0]), "g2bc": rep(f(norm2_g)[0]), "gFbc": rep(f(final_norm_g)),
        "w_in_r": w_in_r, "w_fourier": f(w_fourier)[0], "w_ssd_out": f(w_ssd_out)[0], "w_out": f(w_out)[0],
        "w_ff1": f(w_ff1)[0], "w_ff2": f(w_ff2)[0], "convw": convw, "convb": convb,
        "dtb": rep(f(dt_bias)[0].reshape(-1)), "alog": rep(f(A_log)[0].reshape(-1)), "dskip": rep(f(D_skip)[0]),
        "ssdg": np.ascontiguousarray(f(ssd_norm_g)[0].reshape(16, 128).T),
        "ident": ident, "tris": tris, "dft256": dft,
    }
    cc = f(c); cctx = f(c_ctx)
    sf = f(state_ssm_fwd); sb_ = f(state_ssm_bwd)
    in_maps = []
    for core in range(8):
        s = core // 4
        cv = np.stack([cctx, cc[s]], axis=-1)
        cv = np.ascontiguousarray(cv.reshape(8, 128, 2).transpose(1, 0, 2))
        m = dict(common)
        m["xp"] = np.ascontiguousarray(x_prompt[4 * core:4 * core + 4])
        m["cvec"] = cv
        j = core % 4
        others = [mm for mm in range(4) if mm != j]
        order = others + [j]
        xs = x_sample[s]
        m["xs_all"] = np.ascontiguousarray(np.concatenate([xs[512 * mm:512 * (mm + 1)] for mm in order], axis=0).reshape(16, 128, 1024))
        xh = np.zeros((128, 1024), np.float32); hm = np.zeros((128, 1), np.float32)
        for bs, mm in enumerate(order):
            if mm > 0:
                xh[4 * bs:4 * bs + 2] = xs[512 * mm - 2:512 * mm]; hm[4 * bs:4 * bs + 2] = 1.0
            if mm < 3:
                xh[4 * bs + 2:4 * bs + 4] = xs[512 * mm + 512:512 * mm + 514]; hm[4 * bs + 2:4 * bs + 4] = 1.0
        m["xhalo"] = xh; m["hmask"] = hm
        om_ = np.zeros((128, 6), np.float32)
        for o_, mm in enumerate(others):
            om_[:, 2 * o_] = 1.0 if mm < j else 0.0
            om_[:, 2 * o_ + 1] = 1.0 if mm > j else 0.0
        m["omask"] = om_
        pos = np.concatenate([np.arange(512 * mm, 512 * (mm + 1)) for mm in order])
        posq = np.arange(512 * j, 512 * (j + 1))
        ang = 2.0 * np.pi * (np.outer(pos // 64, posq // 64) / 32.0 + np.outer(pos % 64, posq % 64) / 64.0)
        nrm = 1.0 / np.sqrt(2048.0)
        m["dftp"] = np.stack([np.cos(ang) * nrm, np.sin(ang) * nrm], axis=0).astype(np.float32)
        m["h0"] = np.ascontiguousarray(np.stack([sf[s, 0].transpose(2, 0, 1).reshape(128, 2048),
                                                 sb_[s, 0].transpose(2, 0, 1).reshape(128, 2048)], axis=0))
        in_maps.append(m)
    if "nc" not in _CACHE:
        _CACHE["nc"] = build_program()[0]
    res = run_bass_kernel_spmd(_CACHE["nc"], in_maps, core_ids=list(range(8)))
    y_prompt = np.concatenate([r["yp"] for r in res.results], axis=0)
    y_sample = np.zeros_like(x_sample)
    for core in range(8):
        s, j = core // 4, core % 4
        y_sample[s, 512 * j:512 * (j + 1)] = res.results[core]["ys"]
    def states(key):
        a = np.concatenate([r[key] for r in res.results], axis=0)
        return np.ascontiguousarray(a.reshape(32, 128, 32, 64).transpose(0, 2, 3, 1))[:, None]
    return (y_prompt.astype(np.float32), y_sample.astype(np.float32), states("hf").astype(np.float32),
            states("hb").astype(np.float32))
```

```python
import contextlib
import numpy as np
import concourse.bass as bass
import concourse.mybir as mybir
from concourse.bass_utils import run_bass_kernel_spmd

F32 = mybir.dt.float32
BF16 = mybir.dt.bfloat16
F32R = mybir.dt.float32r
AF = mybir.ActivationFunctionType
ALU = mybir.AluOpType

ENGS = ["pe", "act", "dve", "pool", "sp"]
EPS = 1e-6
BUILD_SAMPLE = True


class Prog:
    def __init__(self, nc, n_dma_sems=14):
        self.nc = nc
        self.ops = []
        self.n_dma_sems = n_dma_sems

    def op(self, eng, fn, r=(), w=(), dma=False):
        rr = set()
        ww = set()
        for k in r:
            rr.add(k)
            rr.add((k[0], "*"))
        for k in w:
            ww.add(k)
            rr.add((k[0], "*"))
        self.ops.append(dict(eng=eng, fn=fn, r=tuple(rr), w=tuple(ww), dma=dma))

    def barrier(self, eng, fn, names):
        self.ops.append(dict(eng=eng, fn=fn, r=(), w=tuple((n, "*") for n in names), dma=False))

    def emit(self):
        nc = self.nc
        ops = self.ops
        last_w = {}
        readers = {}
        for i, o in enumerate(ops):
            deps = set()
            for k in o["r"]:
                if k in last_w:
                    deps.add(last_w[k])
            for k in o["w"]:
                if k in last_w:
                    deps.add(last_w[k])
                for rr in readers.get(k, ()):
                    deps.add(rr)
            deps.discard(i)
            o["deps"] = deps
            for k in o["r"]:
                readers.setdefault(k, []).append(i)
            for k in o["w"]:
                last_w[k] = i
                readers[k] = []
        stack = contextlib.ExitStack()
        eng_sem = {e: stack.enter_context(nc.semaphore("s_" + e)) for e in ENGS}
        dma_rings = {}
        for e in ("sp", "act", "pool"):
            dma_rings[e] = [stack.enter_context(nc.semaphore("d_%s%d" % (e, j)))
                            for j in range(self.n_dma_sems)]
        ring_pos = {e: 0 for e in dma_rings}
        ring_cnt = {e: [0] * self.n_dma_sems for e in dma_rings}
        ring_last = {e: [None] * self.n_dma_sems for e in dma_rings}
        for i, o in enumerate(ops):
            if o["dma"]:
                e = o["eng"]
                j = ring_pos[e]
                ring_pos[e] = (j + 1) % self.n_dma_sems
                ring_cnt[e][j] += 16
                o["dsem"] = dma_rings[e][j]
                o["dval"] = ring_cnt[e][j]
                o["dprev"] = ring_last[e][j]
                ring_last[e][j] = i
        for i, o in enumerate(ops):
            cdeps = {}
            ddeps = set()
            for d in o["deps"]:
                od = ops[d]
                if od["dma"]:
                    ddeps.add(d)
                else:
                    e = od["eng"]
                    if e == "pe" and o["eng"] == "pe":
                        continue
                    if e not in cdeps or cdeps[e] < d:
                        cdeps[e] = d
            if o["dma"] and o["dprev"] is not None:
                ddeps.add(o["dprev"])
            o["cdeps"] = cdeps
            o["ddeps"] = ddeps
        signal = set()
        for o in ops:
            for e, d in o["cdeps"].items():
                signal.add(d)
        cnt = {e: 0 for e in ENGS}
        for i, o in enumerate(ops):
            if not o["dma"] and i in signal:
                cnt[o["eng"]] += 1
                o["sig"] = cnt[o["eng"]]
        per_eng = {e: [i for i, o in enumerate(ops) if o["eng"] == e] for e in ENGS}
        self.stats = {e: len(per_eng[e]) for e in ENGS}

        def run_engine(ename, eobj):
            known = {e: 0 for e in ENGS}
            dknown = set()
            for i in per_eng[ename]:
                o = ops[i]
                for e, d in o["cdeps"].items():
                    v = ops[d]["sig"]
                    if known[e] < v:
                        eobj.wait_ge(eng_sem[e], v)
                        known[e] = v
                for d in sorted(o["ddeps"]):
                    if d not in dknown:
                        eobj.wait_ge(ops[d]["dsem"], ops[d]["dval"])
                        dknown.add(d)
                ins = o["fn"](eobj)
                if o["dma"]:
                    ins.then_inc(o["dsem"], 16)
                elif "sig" in o:
                    ins.then_inc(eng_sem[ename], 1)
            if ename in dma_rings:
                for j, s in enumerate(dma_rings[ename]):
                    if ring_cnt[ename][j] > 0:
                        eobj.wait_ge(s, ring_cnt[ename][j])

        with nc.Block() as block:
            @block.tensor
            def _(e):
                run_engine("pe", e)

            @block.scalar
            def _(e):
                run_engine("act", e)

            @block.vector
            def _(e):
                run_engine("dve", e)

            @block.gpsimd
            def _(e):
                run_engine("pool", e)

            @block.sync
            def _(e):
                run_engine("sp", e)
        stack.close()


def bc_last(ap2d, n):
    p, a = ap2d.shape
    return ap2d.unsqueeze(2).to_broadcast([p, a, n])


def bc_mid(ap2d, n):
    p, b = ap2d.shape
    return ap2d.unsqueeze(1).to_broadcast([p, n, b])


class Builder:
    def __init__(self, nc):
        self.nc = nc
        self.P = Prog(nc)
        self.stack = contextlib.ExitStack()
        self.mm_rr = 0
        self.tp_rr = 0
        self.w_rr = 0
        self.dmaq = 0

    def sb(self, name, shape, dt):
        return self.stack.enter_context(self.nc.sbuf_tensor("sb_" + name, list(shape), dt))

    def pst(self, name, shape, dt):
        return self.stack.enter_context(self.nc.psum_tensor("ps_" + name, list(shape), dt))

    def dram_in(self, name, shape):
        return self.nc.dram_tensor(name, list(shape), F32, kind="ExternalInput").ap()

    def dram_out(self, name, shape):
        return self.nc.dram_tensor(name, list(shape), F32, kind="ExternalOutput").ap()

    def mm(self, out, lhsT, rhs, start, stop, r, w, skip=False):
        self.P.op("pe", lambda e: e.matmul(out, lhsT=lhsT, rhs=rhs, start=start, stop=stop,
                                           skip_group_check=skip), r=r, w=w)

    def tr(self, out, in_, ident, r, w):
        self.P.op("pe", lambda e: e.transpose(out=out, in_=in_, identity=ident), r=r, w=w)

    def act(self, out, in_, func, r, w, bias=None, scale=None, accum=None, eng="act"):
        kw = {}
        if bias is not None:
            kw["bias"] = bias
        if scale is not None:
            kw["scale"] = scale
        if accum is not None:
            kw["accum_out"] = accum
        self.P.op("act", lambda e: e.activation(out=out, in_=in_, func=func, **kw), r=r, w=w)

    def tt(self, eng, out, in0, in1, op, r, w):
        self.P.op(eng, lambda e: e.tensor_tensor(out=out, in0=in0, in1=in1, op=op), r=r, w=w)

    def ts(self, eng, out, in0, s1, s2, op0, op1, r, w):
        if op1 is None:
            self.P.op(eng, lambda e: e.tensor_scalar(out=out, in0=in0, scalar1=s1, scalar2=None, op0=op0),
                      r=r, w=w)
        else:
            self.P.op(eng, lambda e: e.tensor_scalar(out=out, in0=in0, scalar1=s1, scalar2=s2,
                                                     op0=op0, op1=op1), r=r, w=w)

    def stt(self, out, in0, scalar, in1, op0, op1, r, w):
        self.P.op("dve", lambda e: e.scalar_tensor_tensor(out=out, in0=in0, scalar=scalar, in1=in1,
                                                          op0=op0, op1=op1), r=r, w=w)

    def cp(self, eng, out, in_, r, w):
        if eng == "act":
            self.P.op("act", lambda e: e.copy(out=out, in_=in_), r=r, w=w)
        else:
            self.P.op(eng, lambda e: e.tensor_copy(out=out, in_=in_), r=r, w=w)

    def memset(self, eng, ap, val, w):
        self.P.op(eng, lambda e: e.memset(ap, val), w=w)

    def recip(self, out, in_, r, w):
        self.P.op("dve", lambda e: e.reciprocal(out=out, in_=in_), r=r, w=w)

    def dma(self, out, in_, r, w, q="sp"):
        self.P.op(q, lambda e: e.dma_start(out=out, in_=in_), r=r, w=w, dma=True)


D = 1024
NG = 4
DIN = 2048
DFF = 4096
NW = 8256
C_UF = 0
C_GRP = 1024
C_DT = 1024 + 4 * 1280
C_GATE = C_DT + 64


def build_program(n_pblocks=2, sample=True):
    nc = bass.Bass("TRN2", target_bir_lowering=False)
    B = Builder(nc)

    xp = B.dram_in("xp", [4, 256, D])
    cvec = B.dram_in("cvec", [128, 8, 2])
    w_mod = B.dram_in("w_mod", [D, 6 * D])
    b_mod2 = B.dram_in("b_mod2", [2, 6 * D])
    g1bc_d = B.dram_in("g1bc", [128, D])
    g2bc_d = B.dram_in("g2bc", [128, D])
    gFbc_d = B.dram_in("gFbc", [128, D])
    w_in = B.dram_in("w_in_r", [D, NW])
    w_fourier = B.dram_in("w_fourier", [D, D])
    w_ssd_out = B.dram_in("w_ssd_out", [DIN, D])
    w_out = B.dram_in("w_out", [D, D])
    w_ff1 = B.dram_in("w_ff1", [D, DFF])
    w_ff2 = B.dram_in("w_ff2", [DFF, D])
    convw_d = B.dram_in("convw", [128, 24, 5])
    convb_d = B.dram_in("convb", [128, 24])
    dtb_d = B.dram_in("dtb", [128, 64])
    alog_d = B.dram_in("alog", [128, 64])
    dskip_d = B.dram_in("dskip", [128, 32])
    ssdg_d = B.dram_in("ssdg", [128, 16])
    ident_d = B.dram_in("ident", [128, 128])
    tris_d = B.dram_in("tris", [128, 5, 128])
    dft_d = B.dram_in("dft256", [3, 256, 256])
    modscr = nc.dram_tensor("modscr", [2, 6 * D], F32, kind="Internal").ap()
    xs_all = B.dram_in("xs_all", [16, 128, D])
    xhalo_d = B.dram_in("xhalo", [128, D])
    hmask_d = B.dram_in("hmask", [128, 1])
    omask_d = B.dram_in("omask", [128, 6])
    dftp_d = B.dram_in("dftp", [2, 2048, 512])
    h0_d = B.dram_in("h0", [2, 128, DIN])
    Escr = nc.dram_tensor("Escr", [3, 2, 4, 128, 512], F32, kind="Internal").ap()

    yp = B.dram_out("yp", [4, 256, D])
    hf_o = B.dram_out("hf", [4, 128, DIN])
    hb_o = B.dram_out("hb", [4, 128, DIN])
    ys = B.dram_out("ys", [512, D])

    ident = B.sb("ident", [128, 128], BF16)
    tris = B.sb("tris", [128, 5, 128], F32)
    trisr = B.sb("trisr", [128, 3, 128], F32R)
    lndt = B.sb("lndt", [128, 4, 64], F32R)
    Rbr = [B.sb("Rbr%d" % i, [128, 1024], F32R) for i in range(2)]
    masks = B.sb("masks", [128, 2, 128], BF16)
    dft = B.sb("dft", [128, 3, 2, 256], BF16)
    convw = B.sb("convw", [128, 24, 5], F32)
    convb = B.sb("convb", [128, 24], F32)
    dtb = B.sb("dtb", [128, 64], F32)
    Abc = B.sb("Abc", [128, 64], F32)
    Dbc = B.sb("Dbc", [128, 32], F32)
    ssdg = B.sb("ssdg", [128, 16], F32)
    cv = B.sb("cv", [128, 16], F32)
    scb = B.sb("scb", [128, 16], BF16)
    modb = B.sb("modb", [128, 4, D], BF16)
    modg = B.sb("modg", [128, 2, D], F32)
    dummy = B.sb("dummy", [128, 2], F32)
    hTh = B.sb("hTh", [128, 8, 128], BF16)
    hmask = B.sb("hmask", [128, 1], F32)
    omask = B.sb("omask", [128, 6], F32)
    Dst = B.sb("Dst", [128, 3, 64], F32)
    suft = B.sb("suft", [128, 64], F32)

    NSLOT = 4
    wsl = [B.sb("wsl%d" % i, [128, 8, 512], BF16) for i in range(NSLOT)]

    xres = B.sb("xres", [128, 4, D], F32)
    htm = [B.sb("htm%d" % i, [128, D], BF16) for i in range(2)]
    junk = B.sb("junk", [128, D], BF16)
    st_ss = B.sb("st_ss", [128, 12], F32)
    st_rs = B.sb("st_rs", [128, 12], F32)
    ssq = B.sb("ssq", [128, 4, 4], F32)
    ry = B.sb("ry", [128, 4], F32)
    hT = B.sb("hT", [128, 8, 512], BF16)
    YT = B.sb("YT", [128, 8, 512], BF16)
    yzT = B.sb("yzT", [128, 16, 512], BF16)
    NAF = 7168
    NAB = 27936
    arenaF = B.sb("arenaF", [128, NAF], F32)
    arenaB = B.sb("arenaB", [128, NAB], BF16)
    ARENA_NAMES = ["gtmp", "UT", "T12", "xpad", "cacc", "xgT", "BT", "CT", "xg", "Bg", "sz", "dt", "dtA", "prep",
                   "w12", "xs", "Rb", "Lt", "Gt", "cbm", "y1", "y2", "y3", "yzb", "Hf", "Hb", "Htmp", "Hfb", "Hbe",
                   "gfs", "Macc", "Mb", "MT", "aT", "rl", "otile", "gFbc", "hTM", "HmT", "UmT", "xst", "dg", "yd", "hTalt"]

    class Ar:
        def __init__(self, t, n):
            self.t, self.n, self.off = t, n, 0

        def take(self, n, inner=None):
            assert self.off + n <= self.n, (self.off, n, self.n)
            ap = self.t[:, self.off:self.off + n]
            self.off += n
            if inner is not None:
                ap = ap.rearrange(inner[0], **inner[1])
            return ap
    aF = Ar(arenaF, NAF)
    aB = Ar(arenaB, NAB)

    def phase():
        aF.off = 0
        aB.off = 0
        B.P.barrier("dve", lambda e: e.memset(dummy[:, :], 0.0), ARENA_NAMES)

    psb = [B.pst("psb%d" % i, [128, 512], F32) for i in range(4)]
    psS = B.pst("psS", [128, 1024], F32)
    pstb = [B.pst("pstb%d" % i, [128, 1024], BF16) for i in range(2)]

    def bank():
        i = B.mm_rr
        B.mm_rr = (i + 1) % 6
        if i < 4:
            return psb[i], ("psb", i)
        return psS[:, (i - 4) * 512:(i - 3) * 512], ("psS", i - 4)

    def tbank():
        i = B.tp_rr
        B.tp_rr = (i + 1) % 2
        return pstb[i], ("pstb", i)

    def load_w(src, r0, kt, c0, ncols):
        i = B.w_rr
        B.w_rr = (i + 1) % NSLOT
        s = wsl[i]
        srcap = src[r0:r0 + kt * 128, c0:c0 + ncols].rearrange("(kt p) n -> p kt n", p=128)
        B.dma(s[:, 0:kt, 0:ncols], srcap, r=[], w=[("wsl", i)], q="pool")
        return s, ("wsl", i)

    B.dma(ident[:], ident_d[:, :], r=[], w=[("ident", 0)], q="pool")
    B.dma(tris[:], tris_d[:, :, :], r=[], w=[("tris", 0)])
    B.dma(masks[:], tris_d[:, 0:2, :], r=[], w=[("masks", 0)], q="pool")
    B.cp("dve", trisr[:, 0:2, :], tris[:, 2:4, :], r=[("tris", 0)], w=[("trisr", 0)])
    B.dma(arenaF[:, 2048:2176], ident_d[:, :], r=[], w=[("gtmp", 9)])
    B.cp("dve", trisr[:, 2, :], arenaF[:, 2048:2176], r=[("gtmp", 9)], w=[("trisr", 0)])
    for m in range(3):
        B.dma(dft[:, m, :, :], dft_d[m].rearrange("(kt p) n -> p kt n", p=128), r=[], w=[("dft", m)], q="pool")
    B.dma(convw[:], convw_d[:, :, :], r=[], w=[("convw", 0)])
    B.dma(convb[:], convb_d[:, :], r=[], w=[("convb", 0)])
    B.dma(dtb[:], dtb_d[:, :], r=[], w=[("dtb", 0)])
    B.dma(Abc[:], alog_d[:, :], r=[], w=[("Abc", 0)])
    B.dma(Dbc[:], dskip_d[:, :], r=[], w=[("Dbc", 0)])
    B.dma(ssdg[:], ssdg_d[:, :], r=[], w=[("ssdg", 0)])
    B.dma(cv[:], cvec.rearrange("p k r -> p (k r)"), r=[], w=[("cv", 0)])
    B.dma(hmask[:], hmask_d[:, :], r=[], w=[("hmask", 0)])
    B.dma(omask[:], omask_d[:, :], r=[], w=[("omask", 0)])
    B.act(Abc[:], Abc[:], AF.Exp, r=[("Abc", 0)], w=[("Abc", 0)])
    B.ts("dve", Abc[:], Abc[:], -1.0, None, ALU.mult, None, r=[("Abc", 0)], w=[("Abc", 0)])
    B.act(scb[:], cv[:], AF.Silu, r=[("cv", 0)], w=[("scb", 0)])
    scv = scb[:].rearrange("p (k r) -> p k r", r=2)
    DFT_ALL = [("dft", 0), ("dft", 1), ("dft", 2)]

    modp = htm[0][:, :].bitcast(F32)[0:2, 0:512]
    bmod = htm[1][:, :].bitcast(F32)[0:2, 0:512]

    def mod_piece(cb_):
        B.dma(bmod, b_mod2[:, cb_ * 512:(cb_ + 1) * 512], r=[], w=[("htm", 1)])
        s, sk = load_w(w_mod, 0, 8, cb_ * 512, 512)
        ps, pk = bank()
        for k in range(8):
            B.mm(ps[0:2, :], scv[:, k, :], s[:, k, :], k == 0, k == 7, r=[sk, ("scb", 0)], w=[pk])
        B.tt("dve", modp, ps[0:2, :], bmod, ALU.add, r=[pk, ("htm", 1)], w=[("htm", 0)])
        B.dma(modscr[:, cb_ * 512:(cb_ + 1) * 512], modp, r=[("htm", 0)], w=[("modscr", cb_)])

    def mod_part(row, vs, temps=None):
        if temps is None:
            temps = (aF.take(D), aF.take(D))
        t0, t1 = temps
        for v in vs:
            B.dma(t0[:, :], modscr[row:row + 1, v * D:(v + 1) * D].partition_broadcast(128),
                  r=[("modscr", 2 * v), ("modscr", 2 * v + 1)], w=[("gtmp", 7)])
            if v in (0, 3):
                mi = 0 if v == 0 else 2
                B.cp("dve", modb[:, mi, :], t0[:, :], r=[("gtmp", 7)], w=[("modb", mi)])
            elif v in (1, 4):
                mi = 1 if v == 1 else 3
                B.dma(t1[:, :], (g1bc_d if v == 1 else g2bc_d)[:, :], r=[], w=[("gtmp", 8)])
                B.stt(modb[:, mi, :], t0[:, :], 1.0, t1[:, :], ALU.add, ALU.mult,
                      r=[("gtmp", 7), ("gtmp", 8)], w=[("modb", mi)])
            else:
                B.cp("dve", modg[:, 0 if v == 2 else 1, :], t0[:, :], r=[("gtmp", 7)], w=[("modg", v)])

    def rms_stats(src, col, r, dim=D):
        B.act(junk[:, 0:src.shape[1]], src, AF.Square, r=r, w=[("junk", 0), ("st_ss", col)],
              accum=st_ss[:, col:col + 1])
        B.act(st_ss[:, col:col + 1], st_ss[:, col:col + 1], AF.Sqrt, r=[("st_ss", col)], w=[("st_ss", col)],
              bias=EPS, scale=1.0 / dim)
        B.recip(st_rs[:, col:col + 1], st_ss[:, col:col + 1], r=[("st_ss", col)], w=[("st_rs", col)])

    def tile_T(src_tm, skey, dstT, dname, t):
        tb, tk = tbank()
        tbv = tb[:].rearrange("p (k c) -> p k c", c=128)
        for k in range(8):
            B.tr(tbv[:, k, :], src_tm[:, k * 128:(k + 1) * 128], ident[:], r=[skey, ("ident", 0)], w=[tk])
        B.cp("act", dstT[:, :, t * 128:(t + 1) * 128], tbv, r=[tk], w=[(dname, t)])

    def norm_mod_T(t, vshift, vscale, dstT, dname, gtmp, stats=True):
        if stats:
            rms_stats(xres[:, t, :], t, r=[("xres", t)])
        hb_ = htm[t % 2]
        hk = ("htm", t % 2)
        B.stt(gtmp[:, :], xres[:, t, :], st_rs[:, t:t + 1], modb[:, vscale, :], ALU.mult, ALU.mult,
              r=[("xres", t), ("st_rs", t), ("modb", vscale)], w=[("gtmp", 0)])
        B.tt("dve", hb_[:], gtmp[:, :], modb[:, vshift, :], ALU.add, r=[("gtmp", 0), ("modb", vshift)], w=[hk])
        tile_T(hb_, hk, dstT, dname, t)

    hT_all = [("hT", t) for t in range(4)]
    R3 = ("p (h q) -> p h q", dict(q=64))

    def dt_prep(kind, om, dt, dtA, prep, w12, hsrc=None, hname="hT"):
        if hsrc is None:
            hsrc = hT
        s, sk = load_w(w_in, 0, 8, C_DT, 64)
        for t in range(4):
            ps, pk = bank()
            for k in range(8):
                B.mm(ps[:, 0:64], hsrc[:, k, t * 128:(t + 1) * 128], s[:, k, 0:64], k == 0, k == 7,
                     r=[sk, (hname, t)], w=[pk])
            B.tt("dve", dt[:, t, :], ps[:, 0:64], dtb[:], ALU.add, r=[pk, ("dtb", 0)], w=[("dt", t)])
            B.act(dt[:, t, :], dt[:, t, :], AF.Exp, r=[("dt", t)], w=[("dt", t)])
            B.act(dt[:, t, :], dt[:, t, :], AF.Ln, r=[("dt", t)], w=[("dt", t)], bias=1.0)
            if kind == "O":
                for d in range(2):
                    B.ts("dve", dt[:, t, d * 32:(d + 1) * 32], dt[:, t, d * 32:(d + 1) * 32],
                         omask[:, om * 2 + d:om * 2 + d + 1], None, ALU.mult, None,
                         r=[("dt", t), ("omask", 0)], w=[("dt", t)])
            B.tt("dve", dtA[:, t, :], dt[:, t, :], Abc[:], ALU.mult, r=[("dt", t), ("Abc", 0)], w=[("dtA", t)])
            if kind != "O":
                B.act(lndt[:, t, :], dt[:, t, :], AF.Ln, r=[("dt", t)], w=[("lndt", t)], bias=1e-18)
            ps, pk = bank()
            pv = ps[:, 0:192].rearrange("p (a b) -> p a b", b=64)
            for d in range(2):
                B.mm(pv[:, 0, d * 32:(d + 1) * 32], tris[:, 2 + d, :], dtA[:, t, d * 32:(d + 1) * 32], True, True,
                     r=[("tris", 0), ("dtA", t)], w=[pk])
                B.mm(pv[:, 1, d * 32:(d + 1) * 32], tris[:, d, :], dtA[:, t, d * 32:(d + 1) * 32], True, True,
                     r=[("tris", 0), ("dtA", t)], w=[pk])
            B.mm(pv[:, 2, :], tris[:, 4, :], dtA[:, t, :], True, True, r=[("tris", 0), ("dtA", t)], w=[pk])
            B.act(prep[:, t, :, :], pv, AF.Exp, r=[pk], w=[("prep", t)])
            B.cp("dve", w12[:, t, 0, :], dt[:, t, :], r=[("dt", t)], w=[("w12", t)])
            B.tt("dve", w12[:, t, 1, :], dt[:, t, :], prep[:, t, 0, :], ALU.mult,
                 r=[("dt", t), ("prep", t)], w=[("w12", t)])
        if kind == "O":
            for d, order in ((0, (3, 2, 1, 0)), (1, (0, 1, 2, 3))):
                cs = slice(d * 32, (d + 1) * 32)
                for n_, t in enumerate(order):
                    if n_ == 0:
                        continue
                    tp_ = order[n_ - 1]
                    if n_ == 1:
                        src_ = prep[:, tp_, 2, cs]
                        srck = [("prep", tp_)]
                    else:
                        B.tt("dve", suft[:, cs], (prep[:, order[0], 2, cs] if n_ == 2 else suft[:, cs]),
                             prep[:, tp_, 2, cs], ALU.mult,
                             r=[("prep", order[0]), ("prep", tp_), ("suft", d)], w=[("suft", d)])
                        src_ = suft[:, cs]
                        srck = [("suft", d)]
                    B.tt("dve", w12[:, t, 1, cs], w12[:, t, 1, cs], src_, ALU.mult,
                         r=[("w12", t)] + srck, w=[("w12", t)])
            B.tt("dve", Dst[:, om, :], prep[:, 0, 2, :], prep[:, 1, 2, :], ALU.mult,
                 r=[("prep", 0), ("prep", 1)], w=[("Dst", om)])
            B.tt("dve", Dst[:, om, :], Dst[:, om, :], prep[:, 2, 2, :], ALU.mult,
                 r=[("Dst", om), ("prep", 2)], w=[("Dst", om)])
            B.tt("dve", Dst[:, om, :], Dst[:, om, :], prep[:, 3, 2, :], ALU.mult,
                 r=[("Dst", om), ("prep", 3)], w=[("Dst", om)])

    def block_body(kind, runs, out_rows, state_out, bs=0, om=0, hooks=None):
        hooks = hooks or {}
        nrun = len(runs)
        L = 512 // nrun
        phase()
        xpad = [aB.take(528) for _ in range(2)]
        dg = [aB.take(640, ("p (k c) -> p k c", dict(c=128))) for _ in range(2)]
        dt = aF.take(256, ("p (t c) -> p t c", dict(c=64)))
        dtA = aF.take(256, ("p (t c) -> p t c", dict(c=64)))
        prep = aF.take(768, ("p (t a c) -> p t a c", dict(a=3, c=64)))
        w12 = aF.take(512, ("p (t a c) -> p t a c", dict(a=2, c=64)))
        Rb = Rbr
        ybuf = [[aF.take(512) for _ in range(3)] for _ in range(2)]
        Hf = aF.take(512)
        Hb = aF.take(512)
        Htmp = aF.take(512)
        xgT = aB.take(2048, ("p (c t) -> p c t", dict(t=512)))
        BT2 = [aB.take(512) for _ in range(2)]
        CT2 = [aB.take(512) for _ in range(2)]
        xg2 = [aB.take(2048, ("p (c t) -> p c t", dict(t=512))) for _ in range(2)]
        Bg2 = [aB.take(512, ("p (c t) -> p c t", dict(t=128))) for _ in range(2)]
        sz2 = [aB.take(2048, ("p (c t) -> p c t", dict(t=512))) for _ in range(2)]
        xs2 = [aB.take(1536, ("p (c t) -> p c t", dict(t=512))) for _ in range(2)]
        Lt1 = aB.take(1024)
        Gt = [[aB.take(1024) for _ in range(2)] for _ in range(2)]
        cbm = [aB.take(256, ("p (c t) -> p c t", dict(t=128))) for _ in range(2)]
        yzb2 = [aB.take(512) for _ in range(2)]
        Hfb = aB.take(512)
        Hbe = aB.take(2048, ("p (c t) -> p c t", dict(t=512)))
        if kind == "P":
            for i in range(2):
                B.memset("dve", xpad[i][:, :], 0.0, w=[("xpad", i)])
        dt_prep(kind, om, dt, dtA, prep, w12)
        v3 = lambda ap: ap.rearrange(R3[0], **R3[1])

        def load_group(g_):
            a_ = load_w(w_in, 0, 8, C_GRP + g_ * 1280, 512)
            b_ = load_w(w_in, 0, 8, C_GRP + g_ * 1280 + 512, 256)
            c_ = load_w(w_in, 0, 8, C_GRP + g_ * 1280 + 768, 512)
            return a_, b_, c_
        wq = {0: load_group(0)}
        pre_gates_box = []

        def make_head(g):
            gp = g % 2
            BT, CT, xg, Bg, sz = BT2[gp], CT2[gp], xg2[gp], Bg2[gp], sz2[gp]
            ops_ = []

            def ct_stage1(ct):
                (sx, sxk), (sbc, sbck) = wq[g][0], wq[g][1]
                if ct < 4:
                    sl, slk, c0 = sx, sxk, ct * 128
                else:
                    sl, slk, c0 = sbc, sbck, (ct - 4) * 128
                gct = g * 6 + ct
                ps, pk = bank()
                for k in range(8):
                    B.mm(ps[:, :], sl[:, k, c0:c0 + 128], hT[:, k, :], k == 0, k == 7, r=[slk] + hT_all, w=[pk])
                xp_ = xpad[ct % 2]
                xk = ("xpad", ct % 2)
                xv = xp_[:, 0:nrun * (L + 4)].rearrange("p (r l) -> p r l", l=L + 4)
                B.cp("act", xv[:, :, 2:L + 2], ps[:, :].rearrange("p (r l) -> p r l", l=L), r=[pk], w=[xk])
                if kind != "P":
                    ps2, pk2 = bank()
                    for k in range(8):
                        B.mm(ps2[:, 0:4], sl[:, k, c0:c0 + 128], hTh[:, k, bs * 4:bs * 4 + 4], k == 0, k == 7,
                             r=[slk, ("hTh", 0)], w=[pk2])
                    B.cp("act", xp_[:, 0:2], ps2[:, 0:2], r=[pk2], w=[xk])
                    B.cp("act", xp_[:, 514:516], ps2[:, 2:4], r=[pk2], w=[xk])
                dg_ = dg[ct % 2]
                dgk = ("dg", ct % 2)
                for kk in range(5):
                    B.ts("dve", dg_[:, kk, :], ident[:, :], convw[:, gct, kk:kk + 1], None, ALU.mult, None,
                         r=[("ident", 0), ("convw", 0)], w=[dgk])

            def ct_stage2(ct):
                gct = g * 6 + ct
                xp_ = xpad[ct % 2]
                xk = ("xpad", ct % 2)
                xv = xp_[:, 0:nrun * (L + 4)].rearrange("p (r l) -> p r l", l=L + 4)
                dg_ = dg[ct % 2]
                dgk = ("dg", ct % 2)
                psc, pck = bank()
                for kk in range(5):
                    B.mm(psc[:, :].rearrange("p (r l) -> p r l", l=L), dg_[:, kk, :], xv[:, :, kk:kk + L],
                         kk == 0, kk == 4, r=[dgk, xk], w=[pck])
                if ct < 4:
                    dst_, dstk = xgT[:, ct, :], ("xgT", ct)
                elif ct == 4:
                    dst_, dstk = BT[:, :], ("BT", gp)
                else:
                    dst_, dstk = CT[:, :], ("CT", gp)
                B.act(dst_, psc[:, :], AF.Silu, r=[pck, ("convb", 0)], w=[dstk], bias=convb[:, gct:gct + 1])

            def t_stage(t):
                szw, szk = wq[g][2]
                tb, tk = tbank()
                tbv = tb[:].rearrange("p (k c) -> p k c", c=128)
                for ct in range(4):
                    B.tr(tbv[:, ct, :], xgT[:, ct, t * 128:(t + 1) * 128], ident[:],
                         r=[("xgT", ct), ("ident", 0)], w=[tk])
                B.tr(tbv[:, 4, :], BT[:, t * 128:(t + 1) * 128], ident[:], r=[("BT", gp), ("ident", 0)], w=[tk])
                B.cp("act", xg[:, t, :], tb[:, 0:512], r=[tk], w=[("xg", gp, t)])
                B.cp("act", Bg[:, t, :], tb[:, 512:640], r=[tk], w=[("Bg", gp, t)])
                ps, pk = bank()
                for k in range(8):
                    B.mm(ps[:, :], hT[:, k, t * 128:(t + 1) * 128], szw[:, k, :], k == 0, k == 7,
                         r=[szk, ("hT", t)], w=[pk])
                B.act(sz[:, t, :], ps[:, :], AF.Silu, r=[pk], w=[("sz", gp, t)])

            def first():
                if "group" in hooks:
                    hooks["group"](g)
                ct_stage1(0)
            ops_.append(first)
            for ct in range(6):
                def _f(ct=ct):
                    if ct + 1 < 6:
                        ct_stage1(ct + 1)
                    ct_stage2(ct)
                ops_.append(_f)
            def pre1():
                if g + 1 < NG:
                    g_ = g + 1
                    wq[g_] = [load_w(w_in, 0, 8, C_GRP + g_ * 1280, 512),
                              load_w(w_in, 0, 8, C_GRP + g_ * 1280 + 512, 256), None]
                else:
                    pre_gates_box.extend(load_w(w_in, 0, 8, C_GATE + gi_ * 512, 512) for gi_ in range(2))
            ops_.append(pre1)
            for t in range(4):
                ops_.append(lambda t=t: t_stage(t))

            def pre2():
                if g + 1 < NG:
                    g_ = g + 1
                    wq[g_][2] = load_w(w_in, 0, 8, C_GRP + g_ * 1280 + 768, 512)
                else:
                    pre_gates_box.append(load_w(w_in, 0, 8, C_GATE + 2 * 512, 512))
            ops_.append(pre2)
            return ops_

        def make_sweeps(g):
            gp = g % 2
            BT, CT, xg, Bg, sz = BT2[gp], CT2[gp], xg2[gp], Bg2[gp], sz2[gp]
            ops_ = []

            def xscale(t, j):
                d = j - 1
                xs = xs2[t % 2]
                wv = w12[:, t, 1, d * 32 + g * 8: d * 32 + g * 8 + 8]
                B.tt("pool", v3(xs[:, j, :]), v3(xg[:, t, :]), bc_last(wv, 64), ALU.mult,
                     r=[("xg", gp, t), ("w12", t)], w=[("xs", t % 2, j)])

            def state_update(H, Hk, t, d):
                xscale(t, 1 + d)
                xs = xs2[t % 2]
                ps, pk = bank()
                B.mm(ps[:, :], Bg[:, t, :], xs[:, 1 + d, :], True, True, r=[("Bg", gp, t), ("xs", t % 2, 1 + d)], w=[pk])
                cdv = prep[:, t, 2, d * 32 + g * 8: d * 32 + g * 8 + 8]
                B.tt("dve", v3(Htmp[:, :]), v3(H[:, :]), bc_last(cdv, 64), ALU.mult, r=[Hk, ("prep", t)],
                     w=[("Htmp", 0)])
                B.tt("dve", H[:, :], ps[:, :], Htmp[:, :], ALU.add, r=[pk, ("Htmp", 0)], w=[Hk])

            def chain_init(H, Hk, d):
                B.dma(H[:, :], h0_d[d, :, g * 512:(g + 1) * 512], r=[], w=[Hk])
                for m in (range(3) if d == 0 else reversed(range(3))):
                    y3 = ybuf[0][2]
                    B.dma(y3[:, :], Escr[m, d, g, :, :], r=[("Escr", m, d, g)], w=[("y3", 0)])
                    dv_ = Dst[:, m, d * 32 + g * 8: d * 32 + g * 8 + 8]
                    B.tt("dve", v3(Htmp[:, :]), v3(H[:, :]), bc_last(dv_, 64), ALU.mult, r=[Hk, ("Dst", m)],
                         w=[("Htmp", 0)])
                    B.tt("dve", H[:, :], Htmp[:, :], y3[:, :], ALU.add, r=[("Htmp", 0), ("y3", 0)], w=[Hk])

            for ri, run in enumerate(runs):
                def _init():
                    if kind == "S":
                        chain_init(Hb, ("Hb", 0), 1)
                    else:
                        B.memset("dve", Hb[:, :], 0.0, w=[("Hb", 0)])
                ops_.append(_init)
                for t in reversed(run):
                    def _st(t=t):
                        B.cp("act", Hbe[:, t, :], Hb[:, :], r=[("Hb", 0)], w=[("Hbe", t)])
                        state_update(Hb, ("Hb", 0), t, 1)
                    ops_.append(_st)
                if state_out is not None:
                    def _out(ri=ri):
                        B.dma(hb_o[state_out[ri], :, g * 512:(g + 1) * 512], Hb[:, :], r=[("Hb", 0)],
                              w=[("hb_o", 0)])
                    ops_.append(_out)

            def stage_a(t):
                par = t % 2
                tsl = slice(t * 128, (t + 1) * 128)
                ps, pk = bank()
                B.mm(ps[:, 0:128], BT[:, tsl], CT[:, tsl], True, True, r=[("BT", gp), ("CT", gp)], w=[pk])
                for d in range(2):
                    B.tt("dve", cbm[par][:, d, :], ps[:, 0:128], masks[:, d, :], ALU.mult,
                         r=[pk, ("masks", 0)], w=[("cbm", par, d)])
                for d in range(2):
                    dv = dtA[:, t, d * 32 + g * 8: d * 32 + g * 8 + 8]
                    B.tt("pool", Rb[d][:, :].rearrange("p (h i) -> p h i", i=128), bc_last(dv, 128),
                         bc_mid(tris[:, d, :], 8), ALU.mult, r=[("dtA", t), ("tris", 0)], w=[("Rb", d)])
                    for hh in range(2):
                        B.mm(psS[:, hh * 512:(hh + 1) * 512], trisr[:, d, :], Rb[d][:, hh * 512:(hh + 1) * 512],
                             True, False, r=[("trisr", 0), ("Rb", d)], w=[("psS", hh)])
                        hd0 = d * 32 + g * 8 + hh * 4
                        B.mm(psS[:, hh * 512:(hh + 1) * 512].rearrange("p (h i) -> p h i", i=128), trisr[:, 2, :],
                             bc_last(lndt[:, t, hd0:hd0 + 4], 128),
                             False, True, r=[("trisr", 0), ("lndt", t)], w=[("psS", hh)])
                    B.act(Lt1[:, :], psS[:, :], AF.Exp, r=[("psS", 0), ("psS", 1)], w=[("Lt", 0)])
                    B.tt("dve", Gt[par][d][:, :].rearrange("p (h i) -> p h i", i=128),
                         Lt1[:, :].rearrange("p (h i) -> p h i", i=128), bc_mid(cbm[par][:, d, :], 8), ALU.mult,
                         r=[("Lt", 0), ("cbm", par, d)], w=[("Gt", par, d)])

            def stage_b(t):
                par = t % 2
                tsl = slice(t * 128, (t + 1) * 128)
                psy, pyk = bank()
                xD = xs2[par][:, 0, :]
                B.tt("pool", v3(xD), v3(xg[:, t, :]), bc_last(Dbc[:, g * 8:g * 8 + 8], 64), ALU.mult,
                     r=[("xg", gp, t), ("Dbc", 0)], w=[("xs", par, 0)])
                B.mm(psy[:, :], ident[:, :], xD, True, False, r=[("ident", 0), ("xs", par, 0)], w=[pyk], skip=True)
                n = 0
                for h in range(8):
                    for d in range(2):
                        B.mm(psy[:, h * 64:(h + 1) * 64], Gt[par][d][:, h * 128:(h + 1) * 128],
                             xg[:, t, h * 64:(h + 1) * 64], False, n == 15,
                             r=[("Gt", par, d), ("xg", gp, t)], w=[pyk], skip=True)
                        n += 1
                pof, pofk = bank()
                B.mm(pof[:, :], CT[:, tsl], Hfb[:, :], True, True, r=[("CT", gp), ("Hfb", 0)], w=[pofk])
                pob, pobk = bank()
                B.mm(pob[:, :], CT[:, tsl], Hbe[:, t, :], True, True, r=[("CT", gp), ("Hbe", t)], w=[pobk])
                state_update(Hf, ("Hf", 0), t, 0)
                B.cp("act", Hfb[:, :], Hf[:, :], r=[("Hf", 0)], w=[("Hfb", 0)])
                ef = prep[:, t, 1, g * 8: g * 8 + 8]
                eb = prep[:, t, 1, 32 + g * 8: 32 + g * 8 + 8]
                y1, y2, y3 = ybuf[par]
                yk = lambda nm: (nm, par)
                B.tt("dve", v3(y1[:, :]), v3(pof[:, :]), bc_last(ef, 64), ALU.mult, r=[pofk, ("prep", t)], w=[yk("y1")])
                B.tt("dve", v3(y2[:, :]), v3(pob[:, :]), bc_last(eb, 64), ALU.mult, r=[pobk, ("prep", t)], w=[yk("y2")])
                B.tt("dve", y3[:, :], psy[:, :], y1[:, :], ALU.add, r=[pyk, yk("y1")], w=[yk("y3")])
                B.tt("dve", y3[:, :], y3[:, :], y2[:, :], ALU.add, r=[yk("y3"), yk("y2")], w=[yk("y3")])
                B.tt("dve", y1[:, :], y3[:, :], sz[:, t, :], ALU.mult, r=[yk("y3"), ("sz", gp, t)], w=[yk("y1")])
                B.act(junk[:, 0:512], y1[:, :], AF.Square, r=[yk("y1")], w=[("junk", 0), ("ssq", t, g)],
                      accum=ssq[:, t, g:g + 1])
                B.cp("act", yzb2[par][:, :], y1[:, :], r=[yk("y1")], w=[("yzb", par)])

            def stage_b2(t):
                par = t % 2
                tsl = slice(t * 128, (t + 1) * 128)
                tb, tk = tbank()
                tbv = tb[:].rearrange("p (k c) -> p k c", c=128)
                for ct in range(4):
                    B.tr(tbv[:, ct, :], yzb2[par][:, ct * 128:(ct + 1) * 128], ident[:],
                         r=[("yzb", par), ("ident", 0)], w=[tk])
                B.cp("act", yzT[:, g * 4:g * 4 + 4, tsl], tbv[:, 0:4, :], r=[tk], w=[("yzT", g, t)])

            steps = [(ri, t) for ri, run in enumerate(runs) for t in run]
            ops_.append(lambda: stage_a(steps[0][1]))
            for si, (ri, t) in enumerate(steps):
                run = runs[ri]
                if t == run[0]:
                    def _fi():
                        if kind == "S":
                            chain_init(Hf, ("Hf", 0), 0)
                            B.cp("act", Hfb[:, :], Hf[:, :], r=[("Hf", 0)], w=[("Hfb", 0)])
                        else:
                            B.memset("dve", Hf[:, :], 0.0, w=[("Hf", 0)])
                            B.memset("dve", Hfb[:, :], 0.0, w=[("Hfb", 0)])
                    ops_.append(_fi)
                if si + 1 < len(steps):
                    ops_.append(lambda si=si: stage_a(steps[si + 1][1]))
                ops_.append(lambda t=t: stage_b(t))
                if si > 0:
                    ops_.append(lambda si=si: stage_b2(steps[si - 1][1]))
                if t == run[-1] and state_out is not None:
                    def _fo(ri=ri):
                        B.dma(hf_o[state_out[ri], :, g * 512:(g + 1) * 512], Hf[:, :], r=[("Hf", 0)],
                              w=[("hf_o", 0)])
                    ops_.append(_fo)
            ops_.append(lambda: stage_b2(steps[-1][1]))
            return ops_

        for f_ in make_head(0):
            f_()
        for g in range(NG):
            sw = make_sweeps(g)
            hd = make_head(g + 1) if g + 1 < NG else []
            i_h = 0
            for i_s, f_ in enumerate(sw):
                f_()
                while i_h < len(hd) and i_h * len(sw) <= (i_s + 1) * len(hd) - 1 and i_s >= 1:
                    hd[i_h]()
                    i_h += 1
            while i_h < len(hd):
                hd[i_h]()
                i_h += 1
        pre_gates = pre_gates_box
        for t in range(4):
            B.tt("dve", ssq[:, t, 0:2], ssq[:, t, 0:2], ssq[:, t, 2:4], ALU.add,
                 r=[("ssq", t, g_) for g_ in range(4)], w=[("ssq", t, 0), ("ssq", t, 1)])
            B.tt("dve", ssq[:, t, 0:1], ssq[:, t, 0:1], ssq[:, t, 1:2], ALU.add,
                 r=[("ssq", t, 0), ("ssq", t, 1)], w=[("ssq", t, 0)])
            B.act(ssq[:, t, 0:1], ssq[:, t, 0:1], AF.Sqrt, r=[("ssq", t, 0)], w=[("ssq", t, 0)],
                  bias=EPS, scale=1.0 / DIN)
            B.recip(ry[:, t:t + 1], ssq[:, t, 0:1], r=[("ssq", t, 0)], w=[("ry", t)])
        phase()
        Macc = aF.take(4096, ("p (t c) -> p t c", dict(c=D)))
        y1 = aF.take(512)
        gfs = aB.take(8192, ("p (t a c) -> p t a c", dict(a=2, c=D)))
        Mb = aB.take(1024)
        MT = aB.take(4096, ("p (k t) -> p k t", dict(t=512)))
        mtmp = (aF.take(D), aF.take(D))
        for gi in range(4):
            s, sk = pre_gates[gi] if gi < 3 else load_w(w_in, 0, 8, C_GATE + gi * 512, 512)
            for t in range(4):
                ps, pk = bank()
                for k in range(8):
                    B.mm(ps[:, :], hT[:, k, t * 128:(t + 1) * 128], s[:, k, :], k == 0, k == 7,
                         r=[sk, ("hT", t)], w=[pk])
                B.act(gfs[:, t, gi // 2, (gi % 2) * 512:(gi % 2 + 1) * 512], ps[:, :], AF.Sigmoid,
                      r=[pk], w=[("gfs", t, gi)])
        if "merge_start" in hooks:
            hooks["merge_start"](mtmp)
        for half in range(2):
            hs = slice(half * 512, (half + 1) * 512)
            s, sk = load_w(w_fourier, 0, 8, half * 512, 512)
            for t in range(4):
                ps, pk = bank()
                for k in range(8):
                    B.mm(ps[:, :], YT[:, k, t * 128:(t + 1) * 128], s[:, k, :], k == 0, k == 7,
                         r=[sk] + [("YT", kk) for kk in range(8)], w=[pk])
                B.tt("dve", Macc[:, t, hs], ps[:, :], gfs[:, t, 0, hs], ALU.mult,
                     r=[pk, ("gfs", t, half)], w=[("Macc", t, half)])
        for half in range(2):
            hs = slice(half * 512, (half + 1) * 512)
            sl2 = []
            for kh in range(2):
                s, sk = load_w(w_ssd_out, kh * 1024, 8, half * 512, 512)
                for k in range(8):
                    B.ts("dve", s[:, k, :], s[:, k, :], ssdg[:, kh * 8 + k:kh * 8 + k + 1], None, ALU.mult, None,
                         r=[sk, ("ssdg", 0)], w=[sk])
                sl2.append((s, sk))
            for t in range(4):
                ps, pk = bank()
                for kk in range(16):
                    s, sk = sl2[kk // 8]
                    B.mm(ps[:, :], yzT[:, kk, t * 128:(t + 1) * 128], s[:, kk % 8, :], kk == 0, kk == 15,
                         r=[sk] + [("yzT", g_, t) for g_ in range(4)], w=[pk])
                B.stt(y1[:, :], ps[:, :], ry[:, t:t + 1], gfs[:, t, 1, hs], ALU.mult, ALU.mult,
                      r=[pk, ("ry", t), ("gfs", t, 2 + half)], w=[("y1", 0)])
                B.tt("dve", Macc[:, t, hs], Macc[:, t, hs], y1[:, :], ALU.add,
                     r=[("Macc", t, half), ("y1", 0)], w=[("Macc", t, half)])
        for t in range(4):
            B.cp("act", Mb[:, :], Macc[:, t, :], r=[("Macc", t, 0), ("Macc", t, 1)], w=[("Mb", 0)])
            tile_T(Mb, ("Mb", 0), MT, "MT", t)
        for half in range(2):
            hs = slice(half * 512, (half + 1) * 512)
            s, sk = load_w(w_out, 0, 8, half * 512, 512)
            for t in range(4):
                ps, pk = bank()
                for k in range(8):
                    B.mm(ps[:, :], MT[:, k, t * 128:(t + 1) * 128], s[:, k, :], k == 0, k == 7,
                         r=[sk, ("MT", t)], w=[pk])
                B.tt("dve", y1[:, :], ps[:, :], modg[:, 0, hs], ALU.mult, r=[pk, ("modg", 2)], w=[("y1", 0)])
                B.tt("dve", xres[:, t, hs], xres[:, t, hs], y1[:, :], ALU.add, r=[("xres", t), ("y1", 0)],
                     w=[("xres", t)])
        phase()
        gtmp = aF.take(D)
        y1 = aF.take(512)
        otile = [aF.take(D) for _ in range(2)]
        gFbc = aF.take(D)
        aT = aB.take(16384, ("p (k t) -> p k t", dict(t=512)))
        rl = [aB.take(512) for _ in range(2)]
        B.dma(gFbc[:, :], gFbc_d[:, :], r=[], w=[("gFbc", 0)])
        mtmp = (aF.take(D), aF.take(D))
        if "mlp_start" in hooks:
            hooks["mlp_start"](mtmp)
        for t in range(4):
            norm_mod_T(t, 2, 3, hT, "hT", gtmp)
        if "after_norm2" in hooks:
            hooks["after_norm2"](mtmp)
        for fb in range(8):
            s, sk = load_w(w_ff1, 0, 8, fb * 512, 512)
            for ft in range(4):
                ps, pk = bank()
                for k in range(8):
                    B.mm(ps[:, :], s[:, k, ft * 128:(ft + 1) * 128], hT[:, k, :], k == 0, k == 7,
                         r=[sk] + hT_all, w=[pk])
                r_ = rl[ft % 2]
                rk = ("rl", ft % 2)
                B.act(r_[:, :], ps[:, :], AF.Relu, r=[pk], w=[rk])
                B.tt("dve", aT[:, fb * 4 + ft, :], r_[:, :], r_[:, :], ALU.mult, r=[rk], w=[("aT", fb * 4 + ft)])
        if "after_ff1" in hooks:
            hooks["after_ff1"](mtmp)
        for half in range(2):
            hs = slice(half * 512, (half + 1) * 512)
            for kq in range(4):
                s, sk = load_w(w_ff2, kq * 1024, 8, half * 512, 512)
                for t in range(4):
                    for k in range(8):
                        B.mm(psb[t][:, :], aT[:, kq * 8 + k, t * 128:(t + 1) * 128], s[:, k, :],
                             kq == 0 and k == 0, kq == 3 and k == 7,
                             r=[sk, ("aT", kq * 8 + k)], w=[("psb", t)])
            for t in range(4):
                B.tt("dve", y1[:, :], psb[t][:, :], modg[:, 1, hs], ALU.mult, r=[("psb", t), ("modg", 5)],
                     w=[("y1", 0)])
                B.tt("dve", xres[:, t, hs], xres[:, t, hs], y1[:, :], ALU.add, r=[("xres", t), ("y1", 0)],
                     w=[("xres", t)])
        for t in range(4):
            rms_stats(xres[:, t, :], 4 + t, r=[("xres", t)])
            ot = otile[t % 2]
            ok_ = ("otile", t % 2)
            B.stt(ot[:, :], xres[:, t, :], st_rs[:, 4 + t:5 + t], gFbc[:, :], ALU.mult, ALU.mult,
                  r=[("xres", t), ("st_rs", 4 + t), ("gFbc", 0)], w=[ok_])
            B.dma(out_rows(t), ot[:, :], r=[ok_], w=[("out", t)])

    def other_block(om, hcur, hname, next_norm):
        phase()
        L = 512
        hk_all = [(hname, t) for t in range(4)]
        xpad = [aB.take(528) for _ in range(2)]
        dg = [aB.take(640, ("p (k c) -> p k c", dict(c=128))) for _ in range(2)]
        dt = aF.take(256, ("p (t c) -> p t c", dict(c=64)))
        dtA = aF.take(256, ("p (t c) -> p t c", dict(c=64)))
        prep = aF.take(768, ("p (t a c) -> p t a c", dict(a=3, c=64)))
        w12 = aF.take(512, ("p (t a c) -> p t a c", dict(a=2, c=64)))
        Eb = [aF.take(512) for _ in range(2)]
        xgT = [aB.take(2048, ("p (c t) -> p c t", dict(t=512))) for _ in range(2)]
        BT = [aB.take(512) for _ in range(2)]
        xg = [aB.take(2048, ("p (c t) -> p c t", dict(t=512))) for _ in range(2)]
        Bg = [aB.take(512, ("p (c t) -> p c t", dict(t=128))) for _ in range(2)]
        xsd = [[aB.take(512) for _ in range(4)] for _ in range(2)]
        dt_prep("O", om, dt, dtA, prep, w12, hcur, hname)

        def load_group(g_):
            return (load_w(w_in, 0, 8, C_GRP + g_ * 1280, 512), load_w(w_in, 0, 8, C_GRP + g_ * 1280 + 512, 128))
        pre = [load_group(0)]

        def head(g):
            gp = g % 2
            (sx, sxk), (sbc, sbck) = pre[0]

            def st1(ct):
                if ct < 4:
                    sl, slk, c0 = sx, sxk, ct * 128
                else:
                    sl, slk, c0 = sbc, sbck, 0
                gct = g * 6 + ct
                ps, pk = bank()
                for k in range(8):
                    B.mm(ps[:, :], sl[:, k, c0:c0 + 128], hcur[:, k, :], k == 0, k == 7, r=[slk] + hk_all, w=[pk])
                xp_ = xpad[ct % 2]
                xk = ("xpad", ct % 2)
                B.cp("act", xp_[:, 2:L + 2], ps[:, :], r=[pk], w=[xk])
                ps2, pk2 = bank()
                for k in range(8):
                    B.mm(ps2[:, 0:4], sl[:, k, c0:c0 + 128], hTh[:, k, om * 4:om * 4 + 4], k == 0, k == 7,
                         r=[slk, ("hTh", 0)], w=[pk2])
                B.cp("act", xp_[:, 0:2], ps2[:, 0:2], r=[pk2], w=[xk])
                B.cp("act", xp_[:, 514:516], ps2[:, 2:4], r=[pk2], w=[xk])
                dg_ = dg[ct % 2]
                dgk = ("dg", ct % 2)
                for kk in range(5):
                    B.ts("dve", dg_[:, kk, :], ident[:, :], convw[:, gct, kk:kk + 1], None, ALU.mult, None,
                         r=[("ident", 0), ("convw", 0)], w=[dgk])

            def st2(ct):
                gct = g * 6 + ct
                xp_ = xpad[ct % 2]
                xk = ("xpad", ct % 2)
                dg_ = dg[ct % 2]
                dgk = ("dg", ct % 2)
                psc, pck = bank()
                for kk in range(5):
                    B.mm(psc[:, :], dg_[:, kk, :], xp_[:, kk:kk + L], kk == 0, kk == 4, r=[dgk, xk], w=[pck])
                if ct < 4:
                    dst_, dstk = xgT[gp][:, ct, :], ("xgT", gp, ct)
                else:
                    dst_, dstk = BT[gp][:, :], ("BT", gp)
                B.act(dst_, psc[:, :], AF.Silu, r=[pck, ("convb", 0)], w=[dstk], bias=convb[:, gct:gct + 1])

            st1(0)
            for ct in range(5):
                if ct + 1 < 5:
                    st1(ct + 1)
                st2(ct)
            if g + 1 < NG:
                pre[0] = load_group(g + 1)
            for t in range(4):
                tb, tk = tbank()
                tbv = tb[:].rearrange("p (k c) -> p k c", c=128)
                for ct in range(4):
                    B.tr(tbv[:, ct, :], xgT[gp][:, ct, t * 128:(t + 1) * 128], ident[:],
                         r=[("xgT", gp, ct), ("ident", 0)], w=[tk])
                B.tr(tbv[:, 4, :], BT[gp][:, t * 128:(t + 1) * 128], ident[:], r=[("BT", gp), ("ident", 0)], w=[tk])
                B.cp("act", xg[gp][:, t, :], tb[:, 0:512], r=[tk], w=[("xg", gp, t)])
                B.cp("act", Bg[gp][:, t, :], tb[:, 512:640], r=[tk], w=[("Bg", gp, t)])

        def tail(g):
            gp = g % 2
            for d in range(2):
                ps, pk = bank()
                for t in range(4):
                    wv = w12[:, t, 1, d * 32 + g * 8: d * 32 + g * 8 + 8]
                    eng = "pool" if (t + d) % 2 == 0 else "dve"
                    B.tt(eng, xsd[d][t][:, :].rearrange(R3[0], **R3[1]), xg[gp][:, t, :].rearrange(R3[0], **R3[1]),
                         bc_last(wv, 64), ALU.mult, r=[("xg", gp, t), ("w12", t)], w=[("xs", d, t)])
                    B.mm(ps[:, :], Bg[gp][:, t, :], xsd[d][t][:, :], t == 0, t == 3,
                         r=[("Bg", gp, t), ("xs", d, t)], w=[pk])
                B.cp("act", Eb[d][:, :], ps[:, :], r=[pk], w=[("Hf" if d == 0 else "Hb", 0)])
                B.dma(Escr[om, d, g, :, :], Eb[d][:, :], r=[("Hf" if d == 0 else "Hb", 0)], w=[("Escr", om, d, g)])

        head(0)
        for g in range(NG):
            if g + 1 < NG:
                head(g + 1)
            if g == NG - 2:
                next_norm()
            tail(g)

    def prompt_block(bi):
        seqs = [2 * bi, 2 * bi + 1]
        runs = [(0, 1), (2, 3)]
        phase()
        gtmp = aF.take(D)
        UT = aB.take(4096, ("p (k t) -> p k t", dict(t=512)))
        T12 = [aB.take(1024, ("p (k t) -> p k t", dict(t=512))) for _ in range(2)]
        if bi > 0:
            for t in range(4):
                B.dma(xres[:, t, :], xp[seqs[t // 2], (t % 2) * 128:(t % 2 + 1) * 128, :], r=[], w=[("xres", t)])
        for t in range(4):
            norm_mod_T(t, 0, 1, hT, "hT", gtmp, stats=(bi > 0))
        for half in range(2):
            s, sk = load_w(w_in, 0, 8, C_UF + half * 512, 512)
            for ct in range(4):
                ps, pk = bank()
                for k in range(8):
                    B.mm(ps[:, :], s[:, k, ct * 128:(ct + 1) * 128], hT[:, k, :], k == 0, k == 7,
                         r=[sk] + hT_all, w=[pk])
                B.cp("act", UT[:, half * 4 + ct, :], ps[:, :], r=[pk], w=[("UT", half * 4 + ct)])
        it = 0
        for q in range(2):
            for g in range(NG):
                tb_ = T12[it % 2]
                tk_ = ("T12", it % 2)
                it += 1
                for pt in range(2):
                    ps, pk = bank()
                    for cs in range(2):
                        for ci in range(2):
                            B.mm(ps[:, cs * 256:(cs + 1) * 256],
                                 UT[:, 2 * g + ci, q * 256 + pt * 128: q * 256 + (pt + 1) * 128],
                                 dft[:, cs, ci, :], ci == 0, ci == 1,
                                 r=[("UT", 2 * g + ci)] + DFT_ALL, w=[pk])
                    B.cp("dve", tb_[:, pt, :], ps[:, :], r=[pk], w=[tk_])
                ps, pk = bank()
                for c2 in range(2):
                    n = 0
                    for pt in range(2):
                        for cs in range(2):
                            B.mm(ps[:, c2 * 256:(c2 + 1) * 256],
                                 tb_[:, pt, cs * 256 + c2 * 128: cs * 256 + (c2 + 1) * 128],
                                 dft[:, 0 if cs == 0 else 2, pt, :], n == 0, n == 3,
                                 r=[tk_] + DFT_ALL, w=[pk])
                            n += 1
                B.cp("act", YT[:, 2 * g:2 * g + 2, q * 256:(q + 1) * 256],
                     ps[:, :].rearrange("p (a b) -> p a b", b=256), r=[pk],
                     w=[("YT", 2 * g), ("YT", 2 * g + 1)])
        if bi == 0:
            hooks = {"merge_start": lambda tt_: ([mod_piece(c_) for c_ in range(4, 10)], mod_part(0, (2,), tt_)),
                     "mlp_start": lambda tt_: (mod_piece(10), mod_piece(11), mod_part(0, (3, 4), tt_)),
                     "after_ff1": lambda tt_: mod_part(0, (5,), tt_)}
        elif bi == n_pblocks - 1 and sample:
            hooks = {"merge_start": lambda tt_: mod_part(1, (0, 1), tt_),
                     "mlp_start": lambda tt_: mod_part(1, (2,), tt_),
                     "after_norm2": lambda tt_: mod_part(1, (3, 4), tt_)}
        else:
            hooks = {}
        block_body("P", runs,
                   out_rows=lambda t: yp[seqs[t // 2], (t % 2) * 128:(t % 2 + 1) * 128, :],
                   state_out=seqs, hooks=hooks)

    def norm_tile(src, skey, col, dst, dkey, gtmp, mask=None):
        rms_stats(src, col, r=[skey])
        if mask is not None:
            B.tt("dve", st_rs[:, col:col + 1], st_rs[:, col:col + 1], mask, ALU.mult,
                 r=[("st_rs", col), ("hmask", 0)], w=[("st_rs", col)])
        B.stt(gtmp[:, :], src, st_rs[:, col:col + 1], modb[:, 1, :], ALU.mult, ALU.mult,
              r=[skey, ("st_rs", col), ("modb", 1)], w=[("gtmp", 0)])
        if mask is not None:
            B.stt(dst, modb[:, 0, :], mask, gtmp[:, :], ALU.mult, ALU.add,
                  r=[("gtmp", 0), ("modb", 0), ("hmask", 0)], w=[dkey])
        else:
            B.tt("dve", dst, gtmp[:, :], modb[:, 0, :], ALU.add, r=[("gtmp", 0), ("modb", 0)], w=[dkey])

    def sample_block():
        phase()
        mod_part(1, (5,))
        phase()
        gtmp = aF.take(D)
        HmT = [aF.take(2048).bitcast(BF16).rearrange("p (k t) -> p k t", t=512) for _ in range(2)]
        UcT = aF.take(2048).bitcast(BF16).rearrange("p (k t) -> p k t", t=512)
        UsT = yzT[:, 0:8, :]
        hTM = aB.take(16384, ("p (t c) -> p t c", dict(c=D)))
        B.dma(xres[:, 0, :], xhalo_d[:, :], r=[], w=[("xres", 0)])
        norm_tile(xres[:, 0, :], ("xres", 0), 8, htm[0][:], ("htm", 0), gtmp, mask=hmask[:, 0:1])
        tb, tk = tbank()
        tbv = tb[:].rearrange("p (k c) -> p k c", c=128)
        for k in range(8):
            B.tr(tbv[:, k, :], htm[0][:, k * 128:(k + 1) * 128], ident[:], r=[("htm", 0), ("ident", 0)], w=[tk])
        B.cp("act", hTh[:, :, :], tbv, r=[tk], w=[("hTh", 0)])
        for i in range(16):
            xt = i % 4
            B.dma(xres[:, xt, :], xs_all[i, :, :], r=[], w=[("xres", xt)])
            norm_tile(xres[:, xt, :], ("xres", xt), xt, hTM[:, i, :], ("hTM", i), gtmp)
        hTM_all = [("hTM", i) for i in range(16)]
        for m in range(2):
            sl_ = [load_w(dftp_d[m], kh * 1024, 8, 0, 512) for kh in range(2)]
            for dtl in range(8):
                ps, pk = bank()
                for pt in range(16):
                    sw, swk = sl_[pt // 8]
                    B.mm(ps[:, :], hTM[:, pt, dtl * 128:(dtl + 1) * 128], sw[:, pt % 8, :], pt == 0, pt == 15,
                         r=[swk] + hTM_all, w=[pk])
                B.cp("act", HmT[m][:, dtl, :], ps[:, :], r=[pk], w=[("HmT", m, dtl)])
        for half in range(2):
            s, sk = load_w(w_in, 0, 8, C_UF + half * 512, 512)
            for m in range(2):
                dstU = UcT if m == 0 else UsT
                for ct in range(4):
                    ps, pk = bank()
                    for k in range(8):
                        B.mm(ps[:, :], s[:, k, ct * 128:(ct + 1) * 128], HmT[m][:, k, :], k == 0, k == 7,
                             r=[sk] + [("HmT", m, kk) for kk in range(8)], w=[pk])
                    B.cp("act", dstU[:, half * 4 + ct, :], ps[:, :], r=[pk],
                         w=[("UmT", m, half * 4 + ct), ("yzT", 0, 0)] if m == 1 else [("UmT", m, half * 4 + ct)])
        for g in range(NG):
            for c2 in range(2):
                ps, pk = bank()
                n = 0
                for m in range(2):
                    srcU = UcT if m == 0 else UsT
                    for ci in range(2):
                        B.mm(ps[:, :], dft[:, 0 if m == 0 else 2, ci, c2 * 128:(c2 + 1) * 128],
                             srcU[:, 2 * g + ci, :], n == 0, n == 3,
                             r=DFT_ALL + [("UmT", m, 2 * g + ci)], w=[pk])
                        n += 1
                B.cp("act", YT[:, 2 * g + c2, :], ps[:, :], r=[pk], w=[("YT", 2 * g + c2)])
        hTalt = arenaB[:, NAB - 4096:NAB].rearrange("p (k t) -> p k t", t=512)
        hbufs = [(hTalt, "hTalt"), (hT, "hT")]

        def norm_other(om, dst, dname):
            gtmp2 = aF.take(D)
            xst = [aF.take(D) for _ in range(2)]
            for t in range(4):
                i = om * 4 + t
                B.dma(xst[t % 2][:, :], xs_all[i, :, :], r=[], w=[("xst", t % 2)])
                norm_tile(xst[t % 2][:, :], ("xst", t % 2), t, htm[t % 2][:], ("htm", t % 2), gtmp2)
                tile_T(htm[t % 2], ("htm", t % 2), dst, dname, t)

        def norm_own():
            gtmp3 = aF.take(D)
            for t in range(4):
                norm_mod_T(t, 0, 1, hT, "hT", gtmp3)

        phase()
        norm_other(0, *hbufs[0])
        for om in range(3):
            hc, hn = hbufs[om % 2]
            if om < 2:
                nxt = (lambda om_=om: norm_other(om_ + 1, *hbufs[(om_ + 1) % 2]))
            else:
                nxt = norm_own
            other_block(om, hc, hn, nxt)
        block_body("S", [(0, 1, 2, 3)], out_rows=lambda t: ys[t * 128:(t + 1) * 128, :], state_out=None, bs=3)

    for t in range(4):
        B.dma(xres[:, t, :], xp[t // 2, (t % 2) * 128:(t % 2 + 1) * 128, :], r=[], w=[("xres", t)])
    for t in range(4):
        rms_stats(xres[:, t, :], t, r=[("xres", t)])
    for cb_ in range(4):
        mod_piece(cb_)
    phase()
    mod_part(0, (0, 1))
    for bi in range(n_pblocks):
        prompt_block(bi)
    if sample:
        sample_block()

    B.sbuf_left = nc.sbuf_bytes_remaining
    B.P.emit()
    B.stack.close()
    return nc, B


def _consts():
    k = np.arange(128)
    tri = (k[:, None] <= k[None, :]).astype(np.float32)
    triL = (k[:, None] >= k[None, :]).astype(np.float32)
    SL = (k[:, None] > k[None, :]).astype(np.float32)
    SU = (k[:, None] < k[None, :]).astype(np.float32)
    ones = np.ones((128, 128), np.float32)
    tris = np.stack([tri, triL, SL, SU, ones], axis=1)
    n = np.arange(256)
    ang = 2.0 * np.pi * np.outer(n, n) / 256.0
    C = (np.cos(ang) / 16.0).astype(np.float32)
    S = (np.sin(ang) / 16.0).astype(np.float32)
    dft = np.stack([C, S, -S], axis=0)
    return tris, dft, np.eye(128, dtype=np.float32)


_CACHE = {}


def kernel(x_prompt, x_sample, state_ssm_fwd, state_ssm_bwd, c, c_ctx, w_mod, b_mod, norm1_g,
           w_in, w_fourier, conv_w, conv_b, dt_bias, A_log, D_skip, ssd_norm_g, w_ssd_out,
           w_out, norm2_g, w_ff1, w_ff2, final_norm_g):
    f = lambda a: np.ascontiguousarray(np.asarray(a, dtype=np.float32))
    x_prompt = f(x_prompt); x_sample = f(x_sample)
    w_in0 = f(w_in)[0]
    cols = [np.arange(0, 1024)]
    for g in range(4):
        cols.append(np.arange(3072 + g * 512, 3072 + (g + 1) * 512))
        cols.append(np.arange(5120 + g * 128, 5120 + (g + 1) * 128))
        cols.append(np.arange(5632 + g * 128, 5632 + (g + 1) * 128))
        cols.append(np.arange(1024 + g * 512, 1024 + (g + 1) * 512))
    cols.append(np.arange(6144, 6208))
    cols.append(np.arange(6208, 8256))
    cols = np.concatenate(cols)
    w_in_r = np.ascontiguousarray(w_in0[:, cols])
    cw = f(conv_w)[0]; cbv = f(conv_b)[0]
    ch = []
    for g in range(4):
        ch.append(np.arange(g * 512, (g + 1) * 512))
        ch.append(np.arange(2048 + g * 128, 2048 + (g + 1) * 128))
        ch.append(np.arange(2560 + g * 128, 2560 + (g + 1) * 128))
    ch = np.concatenate(ch)
    convw = np.ascontiguousarray(cw[:, ch].T.reshape(24, 128, 5).transpose(1, 0, 2))
    convb = np.ascontiguousarray(cbv[ch].reshape(24, 128).T)
    rep = lambda v: np.ascontiguousarray(np.broadcast_to(np.asarray(v, np.float32).reshape(1, -1), (128, np.asarray(v).size)))
    tris, dft, ident = _consts()
    common = {
        "w_mod": f(w_mod)[0], "b_mod2": np.ascontiguousarray(np.broadcast_to(f(b_mod)[0][None], (2, 6144))),
        "g1bc": rep(f(norm1_g)[0]), "g2bc": rep(f(norm2_g)[0]), "gFbc": rep(f(final_norm_g)),
        "w_in_r": w_in_r, "w_fourier": f(w_fourier)[0], "w_ssd_out": f(w_ssd_out)[0], "w_out": f(w_out)[0],
        "w_ff1": f(w_ff1)[0], "w_ff2": f(w_ff2)[0], "convw": convw, "convb": convb,
        "dtb": rep(f(dt_bias)[0].reshape(-1)), "alog": rep(f(A_log)[0].reshape(-1)), "dskip": rep(f(D_skip)[0]),
        "ssdg": np.ascontiguousarray(f(ssd_norm_g)[0].reshape(16, 128).T),
        "ident": ident, "tris": tris, "dft256": dft,
    }
    cc = f(c); cctx = f(c_ctx)
    sf = f(state_ssm_fwd); sb_ = f(state_ssm_bwd)
    in_maps = []
    for core in range(8):
        s = core // 4
        cv = np.stack([cctx, cc[s]], axis=-1)
        cv = np.ascontiguousarray(cv.reshape(8, 128, 2).transpose(1, 0, 2))
        m = dict(common)
        m["xp"] = np.ascontiguousarray(x_prompt[4 * core:4 * core + 4])
        m["cvec"] = cv
        j = core % 4
        others = [mm for mm in range(4) if mm != j]
        order = others + [j]
        xs = x_sample[s]
        m["xs_all"] = np.ascontiguousarray(np.concatenate([xs[512 * mm:512 * (mm + 1)] for mm in order], axis=0).reshape(16, 128, 1024))
        xh = np.zeros((128, 1024), np.float32); hm = np.zeros((128, 1), np.float32)
        for bs, mm in enumerate(order):
            if mm > 0:
                xh[4 * bs:4 * bs + 2] = xs[512 * mm - 2:512 * mm]; hm[4 * bs:4 * bs + 2] = 1.0
            if mm < 3:
                xh[4 * bs + 2:4 * bs + 4] = xs[512 * mm + 512:512 * mm + 514]; hm[4 * bs + 2:4 * bs + 4] = 1.0
        m["xhalo"] = xh; m["hmask"] = hm
        om_ = np.zeros((128, 6), np.float32)
        for o_, mm in enumerate(others):
            om_[:, 2 * o_] = 1.0 if mm < j else 0.0
            om_[:, 2 * o_ + 1] = 1.0 if mm > j else 0.0
        m["omask"] = om_
        pos = np.concatenate([np.arange(512 * mm, 512 * (mm + 1)) for mm in order])
        posq = np.arange(512 * j, 512 * (j + 1))
        ang = 2.0 * np.pi * (np.outer(pos // 64, posq // 64) / 32.0 + np.outer(pos % 64, posq % 64) / 64.0)
        nrm = 1.0 / np.sqrt(2048.0)
        m["dftp"] = np.stack([np.cos(ang) * nrm, np.sin(ang) * nrm], axis=0).astype(np.float32)
        m["h0"] = np.ascontiguousarray(np.stack([sf[s, 0].transpose(2, 0, 1).reshape(128, 2048),
                                                 sb_[s, 0].transpose(2, 0, 1).reshape(128, 2048)], axis=0))
        in_maps.append(m)
    if "nc" not in _CACHE:
        _CACHE["nc"] = build_program()[0]
    res = run_bass_kernel_spmd(_CACHE["nc"], in_maps, core_ids=list(range(8)))
    y_prompt = np.concatenate([r["yp"] for r in res.results], axis=0)
    y_sample = np.zeros_like(x_sample)
    for core in range(8):
        s, j = core // 4, core % 4
        y_sample[s, 512 * j:512 * (j + 1)] = res.results[core]["ys"]
    def states(key):
        a = np.concatenate([r[key] for r in res.results], axis=0)
        return np.ascontiguousarray(a.reshape(32, 128, 32, 64).transpose(0, 2, 3, 1))[:, None]
    return (y_prompt.astype(np.float32), y_sample.astype(np.float32), states("hf").astype(np.float32),
            states("hb").astype(np.float32))
```
